# Optimizing a Trainium2 kernel written in Bass

```python
import math
import jax, jax.numpy as jnp
from jax import lax
import numpy as np

D_MODEL = 1024
BATCH = 2
SEQ = 16384
DEPTH = 1

HEAD_DIM = 128
N_Q_HEADS = 8
N_KV_HEADS = 2
Q_PER_KV = N_Q_HEADS // N_KV_HEADS
ATTN_WIDTH = N_Q_HEADS * HEAD_DIM
KV_WIDTH = N_KV_HEADS * HEAD_DIM
FOURIER_GROUPS = 4
FOURIER_GROUP_DIM = 128
FOURIER_WIDTH = FOURIER_GROUPS * FOURIER_GROUP_DIM
IN_WIDTH = ATTN_WIDTH + 2 * KV_WIDTH + ATTN_WIDTH + 2 * FOURIER_WIDTH
GRID_W = 64
AXIS_DIM = HEAD_DIM // 2
ROPE_THETA = 10000.0
Q_BLOCK = 128
NORM_EPS = 1e-6

kernel_name = "hybrid_axial_gqa_fourier_gated_merge"


def rms_norm(x, g):
    xf = x.astype(jnp.float32)
    y = xf * lax.rsqrt(jnp.mean(xf * xf, axis=-1, keepdims=True) + NORM_EPS)
    return (y * g.astype(jnp.float32)).astype(x.dtype)


def axial_rope_tables(seq_len):
    rows = seq_len // GRID_W
    row = jnp.repeat(jnp.arange(rows, dtype=jnp.float32), GRID_W)
    col = jnp.tile(jnp.arange(GRID_W, dtype=jnp.float32), rows)
    inv = ROPE_THETA ** (-jnp.arange(0, AXIS_DIM, 2, dtype=jnp.float32) / AXIS_DIM)
    ang_r = row[:, None] * inv[None, :]
    ang_c = col[:, None] * inv[None, :]
    ang_r = jnp.concatenate([ang_r, ang_r], axis=-1)
    ang_c = jnp.concatenate([ang_c, ang_c], axis=-1)
    return jnp.cos(ang_r), jnp.sin(ang_r), jnp.cos(ang_c), jnp.sin(ang_c)


def rotate_half_apply(x, cos, sin):
    x1, x2 = jnp.split(x, 2, axis=-1)
    rot = jnp.concatenate([-x2, x1], axis=-1)
    return x * cos[:, None, :] + rot * sin[:, None, :]


def axial_rope(x, tables):
    cos_r, sin_r, cos_c, sin_c = tables
    xf = x.astype(jnp.float32)
    xr, xc = jnp.split(xf, 2, axis=-1)
    out = jnp.concatenate([rotate_half_apply(xr, cos_r, sin_r),
                           rotate_half_apply(xc, cos_c, sin_c)], axis=-1)
    return out.astype(x.dtype)


def bidirectional_gqa(q, k, v):
    b, s = q.shape[0], q.shape[1]
    nb = s // Q_BLOCK
    scale = 1.0 / math.sqrt(HEAD_DIM)
    qb = q.reshape(b, nb, Q_BLOCK, N_KV_HEADS, Q_PER_KV, HEAD_DIM).transpose(1, 0, 2, 3, 4, 5)

    def one_block(qi):
        sc = jnp.einsum('bqhgd,bkhd->bhgqk', qi, k).astype(jnp.float32) * scale
        p = jax.nn.softmax(sc, axis=-1)
        return jnp.einsum('bhgqk,bkhd->bqhgd', p.astype(v.dtype), v)

    o = lax.map(one_block, qb)
    return o.transpose(1, 0, 2, 3, 4, 5).reshape(b, s, ATTN_WIDTH)


def fourier_mix(u):
    b, s, _ = u.shape
    ug = u.astype(jnp.float32).reshape(b, s, FOURIER_GROUPS, FOURIER_GROUP_DIM)
    f = jnp.fft.fft2(ug, axes=(1, 3), norm='ortho').real
    return f.reshape(b, s, FOURIER_WIDTH).astype(u.dtype)


def setup_inputs(seed: int = 0) -> dict:
    key = jax.random.key(seed)
    ks = jax.random.split(key, 11)
    f32 = jnp.float32
    x = jax.random.normal(ks[0], (BATCH, SEQ, D_MODEL), f32)
    norm_g = 1.0 + 0.02 * jax.random.normal(ks[1], (DEPTH, D_MODEL), f32)
    w_in = jax.random.normal(ks[2], (DEPTH, D_MODEL, IN_WIDTH), f32) * D_MODEL ** -0.5
    q_norm_g = 1.0 + 0.02 * jax.random.normal(ks[3], (DEPTH, HEAD_DIM), f32)
    k_norm_g = 1.0 + 0.02 * jax.random.normal(ks[4], (DEPTH, HEAD_DIM), f32)
    w_attn_proj = jax.random.normal(ks[5], (DEPTH, ATTN_WIDTH, D_MODEL), f32) * ATTN_WIDTH ** -0.5
    w_fourier_proj = jax.random.normal(ks[6], (DEPTH, FOURIER_WIDTH, D_MODEL), f32) * FOURIER_WIDTH ** -0.5
    w_merge = jax.random.normal(ks[7], (DEPTH, D_MODEL, 2 * D_MODEL), f32) * D_MODEL ** -0.5
    b_merge = 0.01 * jax.random.normal(ks[8], (DEPTH, 2 * D_MODEL), f32)
    w_out = jax.random.normal(ks[9], (DEPTH, D_MODEL, D_MODEL), f32) * D_MODEL ** -0.5
    return {"x": x, "norm_g": norm_g, "w_in": w_in, "q_norm_g": q_norm_g,
            "k_norm_g": k_norm_g, "w_attn_proj": w_attn_proj,
            "w_fourier_proj": w_fourier_proj, "w_merge": w_merge,
            "b_merge": b_merge, "w_out": w_out}


def reference(x, norm_g, w_in, q_norm_g, k_norm_g, w_attn_proj, w_fourier_proj,
              w_merge, b_merge, w_out):
    b, s, _ = x.shape
    tables = axial_rope_tables(s)
    splits = np.cumsum([ATTN_WIDTH, KV_WIDTH, KV_WIDTH, ATTN_WIDTH, FOURIER_WIDTH]).tolist()
    for l in range(DEPTH):
        h = rms_norm(x, norm_g[l])
        proj = jnp.einsum('bsd,de->bse', h, w_in[l])
        q, k, v, z_attn, u_f, z_f = jnp.split(proj, splits, axis=-1)

        q = rms_norm(q.reshape(b, s, N_Q_HEADS, HEAD_DIM), q_norm_g[l])
        k = rms_norm(k.reshape(b, s, N_KV_HEADS, HEAD_DIM), k_norm_g[l])
        v = v.reshape(b, s, N_KV_HEADS, HEAD_DIM)
        q = axial_rope(q, tables).reshape(b, s, N_KV_HEADS, Q_PER_KV, HEAD_DIM)
        k = axial_rope(k, tables)
        a = bidirectional_gqa(q, k, v) * jax.nn.silu(z_attn)
        y_attn = jnp.einsum('bse,ed->bsd', a, w_attn_proj[l])

        f = fourier_mix(u_f) * jax.nn.silu(z_f)
        y_four = jnp.einsum('bse,ed->bsd', f, w_fourier_proj[l])

        gates = jax.nn.sigmoid(jnp.einsum('bsd,de->bse', h, w_merge[l]) + b_merge[l])
        g_attn, g_four = jnp.split(gates, 2, axis=-1)
        merged = g_attn * y_attn + g_four * y_four
        x = x + jnp.einsum('bsd,de->bse', merged, w_out[l])
    return x
```

```python
import contextlib
import numpy as np
import concourse.bass as bass
import concourse.mybir as mybir
from concourse.bass_utils import run_bass_kernel_spmd

F32 = mybir.dt.float32
BF16 = mybir.dt.bfloat16
AF = mybir.ActivationFunctionType
ALU = mybir.AluOpType
AX = mybir.AxisListType

B_, S_, D_ = 2, 16384, 1024
NT = 128
NQ = 32
EPS = 1e-6
PHASES = "ABCD"


class Buf:
    __slots__ = ("w", "r", "multi", "excl")

    def __init__(self, multi=False, excl=False):
        self.w = []
        self.r = []
        self.multi = multi
        self.excl = excl
        _ALL_BUFS.append(self)


_ALL_BUFS = []


class _Rec:
    def __getattr__(self, name):
        def f(*a, **k):
            self.call = (name, a, k)
            return self
        return f


def _free(ap):
    n = 1
    for d in ap.shape[1:]:
        n *= d
    return n


class _Op:
    __slots__ = ("idx", "eng", "calls", "preds", "succs", "dur", "lat", "dma", "pos", "tok", "start", "nready", "indeg")


class Sched:
    ENG = ("pe", "act", "dve", "pool", "sp")
    XLAT = 250.0

    def __init__(self, nc, ctx, n_dma_sems=24):
        self.nc = nc
        self.streams = {e: [] for e in self.ENG}
        self.sems = {}
        self.cnt = {e: 0 for e in self.ENG}
        self.seen = {e: {} for e in self.ENG}
        for e in self.ENG:
            self.sems[e] = ctx.enter_context(nc.semaphore("s_" + e))
        self.dma_pool = {}
        for q in ("sp", "pool"):
            lst = []
            for i in range(n_dma_sems):
                key = "d_%s_%d" % (q, i)
                self.sems[key] = ctx.enter_context(nc.semaphore(key))
                lst.append(key)
            self.dma_pool[q] = lst
        self.dma_rr = {"sp": 0, "pool": 0}
        self.dma_val = {}
        self.ops = []
        self.pend = {e: None for e in self.ENG}
        self.simlog = {e: [] for e in self.ENG}
        del _ALL_BUFS[:]

    def _est(self, eng, call):
        name, a, k = call
        try:
            if eng == "pe":
                if name == "transpose":
                    return 160.0
                return 45.0 + _free(k["rhs"]) / 1.7
            out = k.get("out", a[0] if a else None)
            n = _free(out)
            if eng == "act":
                return 224.0 + 0.833 * n
            if eng == "dve":
                return 70.0 + 1.0 * n
            return 120.0 + 1.8 * n
        except Exception:
            return 300.0

    def _deps(self, op, reads, writes):
        preds = op.preds
        i = op.idx
        for b in reads:
            for p in b.w:
                if p != i:
                    preds[p] = True
            if b.excl:
                for p in b.r:
                    if p != i and p not in preds:
                        preds[p] = False
        for b in writes:
            if not b.multi:
                for p in b.w:
                    if p != i:
                        preds[p] = True
            for p in b.r:
                if p != i and p not in preds:
                    preds[p] = False
        for b in reads:
            if not b.r or b.r[-1] != i:
                b.r.append(i)
        for b in writes:
            if b.multi:
                if not b.w or b.w[-1] != i:
                    b.w.append(i)
            else:
                b.w = [i]
                b.r = []

    def _new(self, eng):
        op = _Op()
        op.idx = len(self.ops)
        op.eng = eng
        op.calls = []
        op.preds = {}
        op.dur = 0.0
        op.lat = 0.0
        op.dma = None
        self.ops.append(op)
        return op

    def op(self, eng, fn, reads=(), writes=(), signal=True):
        rec = _Rec()
        fn(rec)
        call = rec.call
        op = self.pend[eng]
        if op is None:
            op = self._new(eng)
            self.pend[eng] = op
        op.calls.append(call)
        op.dur += self._est(eng, call)
        self._deps(op, reads, writes)
        if signal:
            self.pend[eng] = None

    def dma(self, q, out, in_, reads=(), writes=()):
        assert self.pend[q] is None
        op = self._new(q)
        op.dma = (out, in_)
        op.dur = 60.0
        try:
            nbytes = out.shape[0] * _free(out) * (2 if out.dtype == BF16 else 4)
        except Exception:
            nbytes = 1 << 19
        op.lat = 2000.0 + nbytes / 120.0
        self._deps(op, reads, writes)

    def _flush(self):
        import heapq
        ops = self.ops
        if not ops:
            return
        for e in self.ENG:
            assert self.pend[e] is None, "unterminated instruction group on " + e
        n = len(ops)
        for o in ops:
            o.succs = []
            o.nready = 0.0
        for o in ops:
            for p in o.preds:
                ops[p].succs.append(o.idx)
            o.indeg = len(o.preds)
        free = {e: 0.0 for e in self.ENG}
        heap = [(0.0, o.idx) for o in ops if o.indeg == 0]
        heapq.heapify(heap)
        order = {e: [] for e in self.ENG}
        done = 0
        while heap:
            key, i = heapq.heappop(heap)
            o = ops[i]
            st = max(o.nready, free[o.eng])
            if st > key + 1e-9:
                heapq.heappush(heap, (st, i))
                continue
            o.start = st
            fin_eng = st + o.dur
            free[o.eng] = fin_eng
            fin = fin_eng + o.lat
            order[o.eng].append(o)
            done += 1
            for sidx in o.succs:
                s2 = ops[sidx]
                lat = 0.0 if (s2.eng == o.eng and o.dma is None) else self.XLAT
                if fin + lat > s2.nready:
                    s2.nready = fin + lat
                s2.indeg -= 1
                if s2.indeg == 0:
                    heapq.heappush(heap, (s2.nready, sidx))
        assert done == n, "dependency cycle"
        for e in self.ENG:
            for o in order[e]:
                if o.dma is None:
                    self.cnt[e] += 1
                    o.pos = self.cnt[e]
                    o.tok = (e, o.pos)
                else:
                    pool = self.dma_pool[e]
                    key = pool[self.dma_rr[e] % len(pool)]
                    self.dma_rr[e] += 1
                    prev = self.dma_val.get(key, 0)
                    self.dma_val[key] = prev + 16
                    o.tok = (key, prev + 16)
                    o.pos = prev
        sems = self.sems
        for e in self.ENG:
            seen = self.seen[e]
            for o in order[e]:
                waits = []
                for p, hard in o.preds.items():
                    po = ops[p]
                    if po.eng == e and po.dma is None and (e == "pe" or not hard):
                        continue
                    k, v = po.tok
                    if seen.get(k, 0) < v:
                        seen[k] = v
                        waits.append((k, v))
                if o.dma is not None:
                    k, v = o.tok
                    if o.pos > 0 and seen.get(k, 0) < o.pos:
                        seen[k] = o.pos
                        waits.append((k, o.pos))
                self.streams[e].append(self._mk(e, o, waits))
                self.simlog[e].append((waits, o.tok[0], 16 if o.dma is not None else 1))
        self.ops = []
        for b in _ALL_BUFS:
            b.w = []
            b.r = []

    def _mk(self, eng, o, waits):
        sems = self.sems
        semh = sems[eng]
        if o.dma is not None:
            out, in_ = o.dma
            key = o.tok[0]

            def emit(e):
                for (k, v) in waits:
                    e.wait_ge(sems[k], v)
                e.dma_start(out=out, in_=in_).then_inc(sems[key], 16)
            return emit
        calls = o.calls

        def emit(e):
            for (k, v) in waits:
                e.wait_ge(sems[k], v)
            ins = None
            for (cname, cargs, ckw) in calls:
                ins = getattr(e, cname)(*cargs, **ckw)
            ins.then_inc(semh, 1)
        return emit

    def barrier(self):
        self._flush()
        targets = [(e, self.cnt[e]) for e in self.ENG if self.cnt[e] > 0]
        targets += list(self.dma_val.items())
        sems = self.sems
        for eng in self.ENG:
            waits = []
            seen = self.seen[eng]
            for key, val in targets:
                if key == eng and eng == "pe":
                    continue
                if seen.get(key, 0) < val:
                    seen[key] = val
                    waits.append((key, val))

            def emit(e, waits=waits):
                for (k, v) in waits:
                    e.wait_ge(sems[k], v)

            self.streams[eng].append(emit)
            self.simlog[eng].append((waits, None, 0))

    def check(self):
        val = {}
        pc = {e: 0 for e in self.ENG}
        prog = True
        while prog:
            prog = False
            for e in self.ENG:
                lg = self.simlog[e]
                while pc[e] < len(lg):
                    waits, k, inc = lg[pc[e]]
                    if any(val.get(wk, 0) < wv for wk, wv in waits):
                        break
                    if k is not None:
                        val[k] = val.get(k, 0) + inc
                    pc[e] += 1
                    prog = True
        stuck = {e: (pc[e], len(self.simlog[e])) for e in self.ENG if pc[e] < len(self.simlog[e])}
        for e, (p, n) in stuck.items():
            waits, k, inc = self.simlog[e][p]
            print("STUCK", e, p, n, [(wk, wv, val.get(wk, 0)) for wk, wv in waits if val.get(wk, 0) < wv])
        return not stuck

    def wait_all(self, eng, bufs):
        pass

    def emit(self):
        self.barrier()
        assert self.check(), "semaphore deadlock in generated program"
        nc = self.nc
        st = self.streams
        with nc.Block() as block:
            @block.tensor
            def _(e):
                for f in st["pe"]:
                    f(e)

            @block.scalar
            def _(e):
                for f in st["act"]:
                    f(e)

            @block.vector
            def _(e):
                for f in st["dve"]:
                    f(e)

            @block.gpsimd
            def _(e):
                for f in st["pool"]:
                    f(e)

            @block.sync
            def _(e):
                for f in st["sp"]:
                    f(e)


def build_nc(phases=PHASES, debug=False):
    nc = bass.Bass("TRN2", target_bir_lowering=False)
    din = lambda name, shape: nc.dram_tensor(name, shape, F32, kind="ExternalInput").ap()
    xs = din("xs", [NT, 128, 8, 128])
    tab = din("tab", [NT, 128, 256])
    ecd = din("ec", [128, NT, 64])
    wcd = din("wc", [128, 512])
    f2d = din("f2", [128, 256])
    idnd = din("idn", [128, 128])
    gld = din("gl", [128, 8])
    gqkd = din("gqk", [128, 256])
    bmd = din("bm", [128, 2048])
    w_in = din("w_in", [1024, 3584])
    w_ap = din("w_ap", [1024, 1024])
    w_fp = din("w_fp", [512, 1024])
    w_mg = din("w_mg", [1024, 2048])
    w_out = din("w_out", [1024, 1024])
    outT = nc.dram_tensor("outT", [NQ, 128, 8, 128], F32, kind="ExternalOutput").ap()
    QTs = nc.dram_tensor("QTs", [NQ, 128, 1024], BF16, kind="Internal").ap()
    SAs = nc.dram_tensor("SAs", [NQ, 128, 1024], BF16, kind="Internal").ap()
    GAs = nc.dram_tensor("GAs", [NQ, 128, 1024], BF16, kind="Internal").ap()
    MFs = nc.dram_tensor("MFs", [NQ, 128, 1024], F32, kind="Internal").ap()
    dbg = {}
    if debug:
        dbg["fraw"] = nc.dram_tensor("dbg_fraw", [128, NQ, 512], F32, kind="ExternalOutput").ap()

    with contextlib.ExitStack() as ctx:
        S = Sched(nc, ctx)

        def T(c, name, shape, dt):
            return c.enter_context(nc.sbuf_tensor("sb_" + name, shape, dt))

        pk = [ctx.enter_context(nc.psum_tensor("pk%d" % i, [128, 1024], F32)) for i in range(4)]
        bank = [pk[i // 2][:, (i % 2) * 512:(i % 2) * 512 + 512] for i in range(8)]
        bankb = [bank[i].bitcast(BF16) for i in range(8)]
        PB = [Buf(excl=True) for _ in range(8)]

        idb = T(ctx, "idb", [128, 128], BF16)
        ones = T(ctx, "ones", [128, 1], BF16)
        gl = T(ctx, "gl", [128, 8], F32)
        gqk = T(ctx, "gqk", [128, 256], F32)
        negB = T(ctx, "negB", [128, 1], F32)
        gqs = T(ctx, "gqs", [128, 128], F32)
        Bgqs = Buf(multi=True)
        sm = T(ctx, "sm", [128, 8], F32)
        Bc = Buf()
        out_buf = Buf(multi=True)

        with contextlib.ExitStack() as c0:
            idf = T(c0, "idf", [128, 128], F32)
            Bi = Buf()
            S.dma("sp", idf[:], idnd, writes=[Bi])
            S.dma("sp", gl[:], gld, writes=[Bc])
            S.dma("sp", gqk[:], gqkd, writes=[Bc])
            S.op("dve", lambda e: e.tensor_copy(out=idb[:], in_=idf[:]), reads=[Bi], writes=[Bc])
            S.op("pool", lambda e: e.memset(ones[:], 1.0), writes=[Bc])
            for blk in range(2):
                for f in range(2):
                    S.op("dve", lambda e, blk=blk, f=f: e.tensor_copy(out=gqs[:, blk * 64 + f * 32:blk * 64 + f * 32 + 32], in_=gqk[:, blk * 64 + (1 - f) * 32:blk * 64 + (1 - f) * 32 + 32]),
                         reads=[Bc], writes=[Bgqs])
            S.op("dve", lambda e: e.tensor_tensor(out=idf[:, 0:128], in0=gqk[:, 0:128], in1=gqk[:, 0:128], op=ALU.mult),
                 reads=[Bc], writes=[Bi])
            S.op("dve", lambda e: e.tensor_reduce(out=sm[:, 0:1], in_=idf[:, 0:128], axis=AX.X, op=ALU.max),
                 reads=[Bi], writes=[Bc])
            S.op("dve", lambda e: e.tensor_tensor(out=idf[:, 0:128], in0=gqk[:, 128:256], in1=gqk[:, 128:256], op=ALU.mult),
                 reads=[Bc], writes=[Bi])
            S.op("dve", lambda e: e.tensor_reduce(out=sm[:, 1:2], in_=idf[:, 0:128], axis=AX.X, op=ALU.max),
                 reads=[Bi], writes=[Bc])
            S.op("dve", lambda e: e.tensor_tensor(out=sm[:, 2:3], in0=sm[:, 0:1], in1=sm[:, 1:2], op=ALU.mult),
                 reads=[Bc], writes=[Bc])
            S.op("act", lambda e: e.activation(out=sm[:, 3:4], in_=sm[:, 2:3], func=AF.Sqrt, scale=128.0, bias=0.0),
                 reads=[Bc], writes=[Bc])
            S.op("dve", lambda e: e.tensor_scalar(out=negB[:], in0=sm[:, 3:4], scalar1=-1.0, scalar2=None, op0=ALU.mult),
                 reads=[Bc], writes=[Bc])
            S.barrier()

        def pipeline(stages, n):
            ns = len(stages)
            for step in range(n + ns - 1):
                for si, f in enumerate(stages):
                    t = step - si
                    if 0 <= t < n:
                        f(t)

        junk = T(ctx, "junk", [128, 128], BF16)
        Bjunk = Buf(multi=True)

        class XPipe:
            def __init__(self, c, tag, ssb=(7,), rb=3):
                self.ssb = ssb
                self.rb = rb
                self.xt = [T(c, "xt%s%d" % (tag, i), [128, 8, 128], F32) for i in range(2)]
                self.xq = [T(c, "xq%s%d" % (tag, i), [128, 8, 128], BF16) for i in range(2)]
                self.xb = [T(c, "xb%s%d" % (tag, i), [128, 8, 128], BF16) for i in range(rb)]
                self.rs = [T(c, "rs%s%d" % (tag, i), [128, 2], F32) for i in range(rb)]
                self.Bxt = [Buf() for _ in range(2)]
                self.Bxq = [Buf() for _ in range(2)]
                self.Bxb = [Buf() for _ in range(rb)]
                self.Brs = [Buf() for _ in range(rb)]

            def load(self, i, t):
                S.dma("sp", self.xt[i % 2][:], xs[t], writes=[self.Bxt[i % 2]])

            def XB(self, i):
                return self.xb[i % self.rb], self.Bxb[i % self.rb]

            def RS(self, i):
                return self.rs[i % self.rb], self.Brs[i % self.rb]

            def prep(self, i):
                xt, xq, Bxt, Bxq = self.xt[i % 2], self.xq[i % 2], self.Bxt[i % 2], self.Bxq[i % 2]
                xb, Bxb = self.XB(i)
                rs, Brs = self.RS(i)
                sb_ = self.ssb[i % len(self.ssb)]
                S.op("dve", lambda e: e.tensor_copy(out=xb[:], in_=xt[:]), reads=[Bxt], writes=[Bxb])
                S.op("act", lambda e: e.activation(out=xq[:], in_=xt[:], func=AF.Square, scale=1.0, bias=0.0), reads=[Bxt], writes=[Bxq])
                for dh in range(8):
                    S.op("pe", lambda e, dh=dh: e.matmul(bank[sb_][:, 0:1], lhsT=xq[:, dh, :], rhs=ones[:], start=(dh == 0), stop=(dh == 7)),
                         reads=[Bxq, Bc], writes=[PB[sb_]], signal=(dh == 7))
                S.op("act", lambda e: e.activation(out=rs[:, 1:2], in_=bank[sb_][:, 0:1], func=AF.Sqrt, scale=1.0 / D_, bias=EPS),
                     reads=[PB[sb_]], writes=[Brs])
                S.op("dve", lambda e: e.reciprocal(out=rs[:, 0:1], in_=rs[:, 1:2]), reads=[Brs], writes=[Brs])

        def head_norm(heads, rs, Brs, gcol, dst, Bdst, ssq, Bss):
            H = len(heads)
            S.op("dve", lambda e: e.memset(ssq[:, 0:8], 0.0), writes=[Bss])
            for h, (pa, pbuf) in enumerate(heads):
                S.op("act", lambda e, h=h, pa=pa: e.activation(out=junk[:], in_=pa, func=AF.Square, scale=rs[:, 0:1], bias=0.0, accum_out=ssq[:, h:h + 1]),
                     reads=[pbuf, Brs], writes=[Bjunk, Bss])
            S.op("act", lambda e: e.activation(out=ssq[:, 8:8 + H], in_=ssq[:, 0:H], func=AF.Sqrt, scale=1.0 / 128, bias=EPS),
                 reads=[Bss], writes=[Bss])
            S.op("dve", lambda e: e.reciprocal(out=ssq[:, 16:16 + H], in_=ssq[:, 8:8 + H]), reads=[Bss], writes=[Bss])
            S.op("dve", lambda e: e.tensor_scalar(out=ssq[:, 24:24 + H], in0=ssq[:, 16:16 + H], scalar1=rs[:, 0:1], scalar2=None, op0=ALU.mult),
                 reads=[Bss, Brs], writes=[Bss])
            for h, (pa, pbuf) in enumerate(heads):
                S.op("dve", lambda e, h=h, pa=pa: e.scalar_tensor_tensor(out=dst[:, h * 128:(h + 1) * 128], in0=pa, scalar=ssq[:, 24 + h:25 + h],
                                                                        in1=gqk[:, gcol:gcol + 128], op0=ALU.mult, op1=ALU.mult),
                     reads=[pbuf, Bss, Bc], writes=[Bdst])

        def rope(src, Bsrc, H, tabt, Btab, t1, t2, Bt1, Bt2, dst, Bdst, rq=None, Brq=None):
            W = H * 128
            s3 = src[:, 0:W].rearrange("p (h d) -> p h d", h=H)
            cos_b = tabt[:, 0:128].unsqueeze(1).broadcast_to([128, H, 128])
            S.op("dve", lambda e: e.tensor_tensor(out=t1[:, 0:W].rearrange("p (h d) -> p h d", h=H), in0=s3, in1=cos_b, op=ALU.mult),
                 reads=[Bsrc, Btab], writes=[Bt1])
            s5 = src[:, 0:W].rearrange("p (h b f w) -> p h b f w", h=H, b=2, f=2)
            t5 = t2[:, 0:W].rearrange("p (h b f w) -> p h b f w", h=H, b=2, f=2)
            sn4 = tabt[:, 128:256].rearrange("p (b f w) -> p b f w", b=2, f=2)
            for f in range(2):
                sin_b = sn4[:, :, f, :].unsqueeze(1).broadcast_to([128, H, 2, 32])
                S.op("dve", lambda e, f=f, sin_b=sin_b: e.tensor_tensor(out=t5[:, :, :, f, :], in0=s5[:, :, :, 1 - f, :], in1=sin_b, op=ALU.mult),
                     reads=[Bsrc, Btab], writes=[Bt2])
            if rq is None:
                S.op("dve", lambda e: e.tensor_tensor(out=dst[:].rearrange("p h d -> p (h d)"), in0=t1[:, 0:W], in1=t2[:, 0:W], op=ALU.add),
                     reads=[Bt1, Bt2], writes=[Bdst])
            else:
                S.op("dve", lambda e: e.tensor_tensor(out=t1[:, 0:W], in0=t1[:, 0:W], in1=t2[:, 0:W], op=ALU.add),
                     reads=[Bt1, Bt2], writes=[Bt1])
                S.op("dve", lambda e: e.tensor_tensor(out=dst[:], in0=t1[:, 0:W].rearrange("p (h d) -> p h d", h=H), in1=rq.unsqueeze(2).broadcast_to([128, H, 128]), op=ALU.mult),
                     reads=[Bt1, Brq], writes=[Bdst])

        def load_w(stg, Bstg, dst, Bdst, wd, col0, ncols, fold, kh=8, state=[0]):
            cw = stg[0].shape[2]
            for c0_ in range(0, ncols, cw):
                w = min(cw, ncols - c0_)
                i = state[0] % 2
                state[0] += 1
                src = wd[:, col0 + c0_:col0 + c0_ + w].rearrange("(dh dl) e -> dl dh e", dl=128)
                S.dma("sp", stg[i][:, 0:kh, 0:w], src, writes=[Bstg[i]])
                if fold:
                    for dh in range(kh):
                        S.op("dve", lambda e, dh=dh, i=i, c0_=c0_, w=w: e.tensor_scalar(
                            out=dst[:, dh, c0_:c0_ + w], in0=stg[i][:, dh, 0:w], scalar1=gl[:, dh:dh + 1], scalar2=None, op0=ALU.mult),
                            reads=[Bstg[i], Bc], writes=[Bdst])
                else:
                    S.op("dve", lambda e, i=i, c0_=c0_, w=w: e.tensor_copy(out=dst[:, :, c0_:c0_ + w], in_=stg[i][:, 0:kh, 0:w]),
                         reads=[Bstg[i]], writes=[Bdst])

        cab = contextlib.ExitStack()
        fraw = T(cab, "fraw", [128, NQ, 512], BF16)
        Bfraw = Buf(multi=True)

        if "A" in phases:
            with contextlib.ExitStack() as ca:
                Wu = T(ca, "Wu", [128, 8, 512], BF16)
                Ec = T(ca, "Ec", [128, NT, 64], BF16)
                Wc = T(ca, "Wc", [128, 512], BF16)
                F2 = T(ca, "F2", [128, 256], BF16)
                BWu, BEc, BWc, BF2, BY = Buf(multi=True), Buf(multi=True), Buf(), Buf(), Buf(multi=True)
                stg = [T(ca, "stgA%d" % i, [128, 8, 256], F32) for i in range(2)]
                Bstg = [Buf(), Buf()]
                load_w(stg, Bstg, Wu, BWu, w_in, 2560, 512, True)
                for i in range(4):
                    sf = stg[i % 2][:].rearrange("p a b -> p (a b)")
                    S.dma("sp", sf, ecd[:, 32 * i:32 * i + 32, :].rearrange("p a b -> p (a b)"), writes=[Bstg[i % 2]])
                    S.op("dve", lambda e, i=i, sf=sf: e.tensor_copy(out=Ec[:, 32 * i:32 * i + 32, :].rearrange("p a b -> p (a b)"), in_=sf),
                         reads=[Bstg[i % 2]], writes=[BEc])
                S.dma("sp", stg[0][:, 0:2, :].rearrange("p a b -> p (a b)"), wcd, writes=[Bstg[0]])
                S.op("dve", lambda e: e.tensor_copy(out=Wc[:], in_=stg[0][:, 0:2, :].rearrange("p a b -> p (a b)")), reads=[Bstg[0]], writes=[BWc])
                S.dma("sp", stg[1][:, 0, 0:256], f2d, writes=[Bstg[1]])
                S.op("dve", lambda e: e.tensor_copy(out=F2[:], in_=stg[1][:, 0, 0:256]), reads=[Bstg[1]], writes=[BF2])
                Ysb = T(ca, "Ysb", [128, NT, 4, 2, 32], BF16)
                Zg = [T(ca, "Zg%d" % i, [128, 32, 2, 128], BF16) for i in range(2)]
                BZ = [Buf(multi=True), Buf(multi=True)]
                usb = [T(ca, "usb%d" % i, [128, 512], BF16) for i in range(3)]
                Bus = [Buf() for _ in range(3)]
                xp = XPipe(ca, "A", (6, 7), rb=3)

                def a_s0(t):
                    xp.load(t, t)

                def a_s1(t):
                    xp.prep(t)

                def a_s2(t):
                    xb, Bxb = xp.XB(t)
                    rs, Brs = xp.RS(t)
                    ub = t % 2
                    for dh in range(8):
                        S.op("pe", lambda e, dh=dh: e.matmul(bank[ub], lhsT=xb[:, dh, :], rhs=Wu[:, dh, :], start=(dh == 0), stop=(dh == 7)),
                             reads=[Bxb, BWu], writes=[PB[ub]], signal=(dh == 7))
                    S.op("dve", lambda e: e.tensor_scalar(out=usb[t % 3][:], in0=bank[ub], scalar1=rs[:, 0:1], scalar2=None, op0=ALU.mult),
                         reads=[PB[ub], Brs], writes=[Bus[t % 3]])

                def a_s3(t):
                    yb = 2 + t % 2
                    for g in range(4):
                        S.op("pe", lambda e, g=g: e.matmul(bank[yb][:, g * 64:(g + 1) * 64], lhsT=usb[t % 3][:, g * 128:(g + 1) * 128], rhs=Ec[:, t, :], start=True, stop=True),
                             reads=[Bus[t % 3], BEc], writes=[PB[yb]], signal=(g == 3))
                    S.op("act", lambda e: e.copy(out=Ysb[:, t, :, :, :].rearrange("p g r d -> p (g r d)"), in_=bank[yb][:, 0:256]),
                         reads=[PB[yb]], writes=[BY])

                pipeline([a_s0, a_s1, a_s2, a_s3], NT)
                n = 0
                for g in range(4):
                    zg, bz = Zg[g % 2], BZ[g % 2]
                    for dp in range(32):
                        zb = 4 + n % 2
                        S.op("pe", lambda e, g=g, dp=dp, zb=zb: e.matmul(bank[zb][:, 0:256], lhsT=Ysb[:, :, g, 0, dp], rhs=Wc[:, 0:256], start=True, stop=False),
                             reads=[BY, BWc], writes=[PB[zb]], signal=False)
                        S.op("pe", lambda e, g=g, dp=dp, zb=zb: e.matmul(bank[zb][:, 0:256], lhsT=Ysb[:, :, g, 1, dp], rhs=Wc[:, 256:512], start=False, stop=True),
                             reads=[BY, BWc], writes=[PB[zb]])
                        if n % 2 == 0:
                            S.op("dve", lambda e, dp=dp, zb=zb, zg=zg: e.tensor_copy(out=zg[:, dp, :, :].rearrange("p r c -> p (r c)"), in_=bank[zb][:, 0:256]),
                                 reads=[PB[zb]], writes=[bz])
                        else:
                            S.op("act", lambda e, dp=dp, zb=zb, zg=zg: e.copy(out=zg[:, dp, :, :].rearrange("p r c -> p (r c)"), in_=bank[zb][:, 0:256]),
                                 reads=[PB[zb]], writes=[bz])
                        n += 1
                    for q4 in range(8):
                        fb = q4 % 2
                        for i in range(4):
                            dp = 4 * q4 + i
                            S.op("pe", lambda e, dp=dp, i=i, fb=fb, zg=zg: e.matmul(bank[fb][:, i * 128:(i + 1) * 128], lhsT=F2[:, 0:128], rhs=zg[:, dp, 0, :], start=True, stop=False),
                                 reads=[bz, BF2], writes=[PB[fb]], signal=False)
                            S.op("pe", lambda e, dp=dp, i=i, fb=fb, zg=zg: e.matmul(bank[fb][:, i * 128:(i + 1) * 128], lhsT=F2[:, 128:256], rhs=zg[:, dp, 1, :], start=False, stop=True),
                                 reads=[bz, BF2], writes=[PB[fb]], signal=(i == 3))
                        dst = fraw[:, 4 * q4:4 * q4 + 4, g * 128:(g + 1) * 128]
                        src = bank[fb].rearrange("p (i c) -> p i c", i=4)
                        if q4 % 2 == 0:
                            S.op("dve", lambda e, dst=dst, src=src: e.tensor_copy(out=dst, in_=src), reads=[PB[fb]], writes=[Bfraw])
                        else:
                            S.op("act", lambda e, dst=dst, src=src: e.copy(out=dst, in_=src), reads=[PB[fb]], writes=[Bfraw])
                if debug:
                    dstg = [T(ca, "dstg%d" % i, [128, 4, 512], F32) for i in range(2)]
                    Bstg = [Buf(), Buf()]
                    for i in range(8):
                        k = i % 2
                        S.op("dve", lambda e, i=i, k=k: e.tensor_copy(out=dstg[k][:], in_=fraw[:, 4 * i:4 * i + 4, :]),
                             reads=[Bfraw], writes=[Bstg[k]])
                        S.dma("pool", dbg["fraw"][:, 4 * i:4 * i + 4, :], dstg[k][:], reads=[Bstg[k]], writes=[out_buf])
                S.barrier()

        BQTs = [Buf() for _ in range(NQ)]
        BSAs = [Buf() for _ in range(NQ)]
        BGAs = [Buf() for _ in range(NQ)]
        BMFs = [Buf() for _ in range(NQ)]
        if "B" in phases:
            with contextlib.ExitStack() as cb:
                Wq = T(cb, "Wq", [128, 8, 1024], BF16)
                Wza = T(cb, "Wza", [128, 8, 1024], BF16)
                Wzf = T(cb, "Wzf", [128, 8, 512], BF16)
                Wmg = T(cb, "Wmg", [128, 8, 2048], BF16)
                Wfp = T(cb, "Wfp", [128, 4, 1024], BF16)
                bm = T(cb, "bm", [128, 2048], F32)
                BWq, BWza, BWzf, BWmg, BWfp = [Buf(multi=True) for _ in range(5)]
                Bbm = Buf()
                S.dma("sp", bm[:], bmd, writes=[Bbm])
                stg = [T(cb, "stgB%d" % i, [128, 8, 256], F32) for i in range(2)]
                Bstg = [Buf(), Buf()]
                load_w(stg, Bstg, Wq, BWq, w_in, 0, 1024, True)
                load_w(stg, Bstg, Wza, BWza, w_in, 1536, 1024, True)
                load_w(stg, Bstg, Wzf, BWzf, w_in, 3072, 512, True)
                load_w(stg, Bstg, Wfp, BWfp, w_fp, 0, 1024, False, kh=4)
                load_w(stg, Bstg, Wmg, BWmg, w_mg, 0, 2048, True)
                xp = XPipe(cb, "B", (7,), rb=4)

                def ring(name, shape, dt, n=2):
                    return [T(cb, "%s%d" % (name, i), shape, dt) for i in range(n)], [Buf() for _ in range(n)]
                tabt, Btab = ring("tabB", [128, 256], F32, 4)
                tabg, Btabg = ring("tabg", [128, 256], F32, 1)
                qn, Bqn = ring("qn", [128, 1024], F32, 2)
                ssq, Bssq = ring("ssq", [128, 32], F32, 2)
                t1, Bt1 = ring("t1", [128, 1024], F32, 1)
                t2, Bt2 = ring("t2", [128, 1024], F32, 1)
                qr, Bqr = ring("qr", [128, 8, 128], BF16, 1)
                QTt, BQT = ring("QTt", [128, 1024], BF16, 1)
                sg, Bsg = ring("sg", [128, 512], F32, 2)
                sa, Bsa = ring("sa", [128, 1024], BF16, 2)
                sf, Bsf = ring("sf", [128, 512], F32, 1)
                fg, Bfg = ring("fg", [128, 512], BF16, 2)
                fT, BfT = ring("fT", [128, 4, 128], BF16, 2)
                gt, Bgt = ring("gt", [128, 512], F32, 1)
                ga, Bga = ring("ga", [128, 1024], BF16, 1)
                gf, Bgf = ring("gf", [128, 512], F32, 1)
                mf, Bmf = ring("mf", [128, 1024], F32, 1)

                def proj(i, pb, Wt, BW, c0_):
                    xb, Bxb = xp.XB(i)
                    for dh in range(8):
                        S.op("pe", lambda e, dh=dh: e.matmul(bank[pb], lhsT=xb[:, dh, :], rhs=Wt[:, dh, c0_:c0_ + 512], start=(dh == 0), stop=(dh == 7)),
                             reads=[Bxb, BW], writes=[PB[pb]], signal=(dh == 7))

                def b_s0(i):
                    xp.load(i, 4 * i)
                    S.dma("sp", tabt[i % 4][:], tab[4 * i], writes=[Btab[i % 4]])

                def b_s1(i):
                    xp.prep(i)

                def b_s2(i):
                    rs, Brs = xp.RS(i)
                    k = i % 2
                    proj(i, 0, Wq, BWq, 0)
                    proj(i, 1, Wq, BWq, 512)
                    for c in range(2):
                        S.op("act", lambda e, c=c: e.activation(out=qn[k][:, c * 512:(c + 1) * 512], in_=bank[c], func=AF.Identity, scale=rs[:, 0:1], bias=0.0),
                             reads=[PB[c], Brs], writes=[Bqn[k]])
                    sq_, Bsq_ = ssq[k], Bssq[k]
                    S.op("dve", lambda e: e.memset(sq_[:, 0:8], 0.0), writes=[Bsq_])
                    for h in range(8):
                        S.op("act", lambda e, h=h: e.activation(out=junk[:], in_=qn[k][:, h * 128:(h + 1) * 128], func=AF.Square, scale=1.0, bias=0.0, accum_out=sq_[:, h:h + 1]),
                             reads=[Bqn[k]], writes=[Bjunk, Bsq_])
                    S.op("act", lambda e: e.activation(out=sq_[:, 8:16], in_=sq_[:, 0:8], func=AF.Sqrt, scale=1.0 / 128, bias=EPS), reads=[Bsq_], writes=[Bsq_])
                    S.op("dve", lambda e: e.reciprocal(out=sq_[:, 16:24], in_=sq_[:, 8:16]), reads=[Bsq_], writes=[Bsq_])
                    tg, Btg = tabg[0], Btabg[0]
                    S.op("dve", lambda e: e.tensor_tensor(out=tg[:, 0:128], in0=tabt[i % 4][:, 0:128], in1=gqk[:, 0:128], op=ALU.mult),
                         reads=[Btab[i % 4], Bc], writes=[Btg])
                    S.op("dve", lambda e: e.tensor_tensor(out=tg[:, 128:256], in0=tabt[i % 4][:, 128:256], in1=gqs[:], op=ALU.mult),
                         reads=[Btab[i % 4], Bc], writes=[Btg])
                    rope(qn[k], Bqn[k], 8, tg, Btg, t1[0], t2[0], Bt1[0], Bt2[0], qr[0], Bqr[0], rq=sq_[:, 16:24], Brq=Bsq_)
                    for h in range(8):
                        S.op("pe", lambda e, h=h: e.transpose(bankb[2][:, h * 128:(h + 1) * 128], qr[0][:, h, :], idb[:]),
                             reads=[Bqr[0], Bc], writes=[PB[2]], signal=(h == 7))
                    S.op("dve", lambda e: e.tensor_copy(out=QTt[0][:], in_=bankb[2][:, 0:1024]), reads=[PB[2]], writes=[BQT[0]])
                    S.dma("pool", QTs[i], QTt[0][:], reads=[BQT[0]], writes=[BQTs[i]])

                def b_s3(i):
                    rs, Brs = xp.RS(i)
                    k = i % 2
                    for c in range(2):
                        pb = 3 + c
                        proj(i, pb, Wza, BWza, c * 512)
                        S.op("act", lambda e, pb=pb, c=c: e.activation(out=sg[c][:], in_=bank[pb], func=AF.Sigmoid, scale=rs[:, 0:1], bias=0.0),
                             reads=[PB[pb], Brs], writes=[Bsg[c]])
                        S.op("dve", lambda e, pb=pb, c=c: e.scalar_tensor_tensor(out=sa[k][:, c * 512:(c + 1) * 512], in0=bank[pb], scalar=rs[:, 0:1], in1=sg[c][:], op0=ALU.mult, op1=ALU.mult),
                             reads=[PB[pb], Brs, Bsg[c]], writes=[Bsa[k]])
                    S.dma("pool", SAs[i], sa[k][:], reads=[Bsa[k]], writes=[BSAs[i]])
                    proj(i, 5, Wzf, BWzf, 0)
                    S.op("act", lambda e: e.activation(out=sg[0][:], in_=bank[5], func=AF.Sigmoid, scale=rs[:, 0:1], bias=0.0),
                         reads=[PB[5], Brs], writes=[Bsg[0]])
                    S.op("dve", lambda e: e.scalar_tensor_tensor(out=sf[0][:], in0=bank[5], scalar=rs[:, 0:1], in1=sg[0][:], op0=ALU.mult, op1=ALU.mult),
                         reads=[PB[5], Brs, Bsg[0]], writes=[Bsf[0]])
                    S.op("dve", lambda e: e.tensor_tensor(out=fg[k][:], in0=sf[0][:], in1=fraw[:, i, :], op=ALU.mult),
                         reads=[Bsf[0], Bfraw], writes=[Bfg[k]])
                    for g in range(4):
                        S.op("pe", lambda e, g=g: e.transpose(bankb[6][:, g * 128:(g + 1) * 128], fg[k][:, g * 128:(g + 1) * 128], idb[:]),
                             reads=[Bfg[k], Bc], writes=[PB[6]], signal=(g == 3))
                    S.op("dve", lambda e: e.tensor_copy(out=fT[k][:].rearrange("p g t -> p (g t)"), in_=bankb[6][:, 0:512]), reads=[PB[6]], writes=[BfT[k]])
                    for c in range(2):
                        for g in range(4):
                            S.op("pe", lambda e, g=g, c=c: e.matmul(bank[c], lhsT=fT[k][:, g, :], rhs=Wfp[:, g, c * 512:(c + 1) * 512], start=(g == 0), stop=(g == 3)),
                                 reads=[BfT[k], BWfp], writes=[PB[c]], signal=(g == 3))
                    for c in range(4):
                        pb = 3 + c % 2
                        kk = c % 2
                        proj(i, pb, Wmg, BWmg, c * 512)
                        S.op("dve", lambda e, pb=pb, c=c, kk=kk: e.scalar_tensor_tensor(out=gt[0][:], in0=bank[pb], scalar=rs[:, 0:1], in1=bm[:, c * 512:(c + 1) * 512], op0=ALU.mult, op1=ALU.add),
                             reads=[PB[pb], Brs, Bbm], writes=[Bgt[0]])
                        if c < 2:
                            S.op("act", lambda e, c=c, kk=kk: e.activation(out=ga[0][:, c * 512:(c + 1) * 512], in_=gt[0][:], func=AF.Sigmoid, scale=1.0, bias=0.0),
                                 reads=[Bgt[0]], writes=[Bga[0]])
                        else:
                            S.op("act", lambda e, kk=kk: e.activation(out=gf[0][:], in_=gt[0][:], func=AF.Sigmoid, scale=1.0, bias=0.0),
                                 reads=[Bgt[0]], writes=[Bgf[0]])
                            S.op("dve", lambda e, c=c, kk=kk: e.tensor_tensor(out=mf[0][:, (c - 2) * 512:(c - 1) * 512], in0=bank[c - 2], in1=gf[0][:], op=ALU.mult),
                                 reads=[PB[c - 2], Bgf[0]], writes=[Bmf[0]])
                    S.dma("pool", GAs[i], ga[0][:], reads=[Bga[0]], writes=[BGAs[i]])
                    S.dma("pool", MFs[i], mf[0][:], reads=[Bmf[0]], writes=[BMFs[i]])

                pipeline([b_s0, b_s1, b_s2, b_s3], NQ)
                S.barrier()
        cab.close()
        if "C" in phases:
            with contextlib.ExitStack() as cd:
                KT = T(cd, "KT", [128, 2, NT, 128], BF16)
                Vs = T(cd, "Vs", [128, NT, 2, 129], BF16)
                Wap = T(cd, "Wap", [128, 8, 1024], BF16)
                Wout = T(cd, "Wout", [128, 8, 1024], BF16)
                BKT, BV, BWap, BWout = Buf(multi=True), Buf(multi=True), Buf(multi=True), Buf(multi=True)
                with contextlib.ExitStack() as cc:
                    Wkv = T(cc, "Wkv", [128, 8, 512], BF16)
                    BWkv = Buf(multi=True)
                    stg = [T(cc, "stgC%d" % i, [128, 8, 128], F32) for i in range(2)]
                    Bstg = [Buf(), Buf()]
                    load_w(stg, Bstg, Wkv, BWkv, w_in, 1024, 512, True)
                    load_w(stg, Bstg, Wap, BWap, w_ap, 0, 1024, False)
                    load_w(stg, Bstg, Wout, BWout, w_out, 0, 1024, False)
                    xp = XPipe(cc, "C", (6, 7), rb=3)

                    def ringc(name, shape, dt, n=2):
                        return [T(cc, "%s%d" % (name, i), shape, dt) for i in range(n)], [Buf() for _ in range(n)]
                    tabt, Btab = ringc("tabC", [128, 256], F32, 5)
                    kn, Bkn = ringc("kn", [128, 256], F32, 3)
                    ssq, Bssq = ringc("ssqc", [128, 32], F32, 2)
                    t1, Bt1 = ringc("t1c", [128, 256], F32, 1)
                    t2, Bt2 = ringc("t2c", [128, 256], F32, 1)
                    kr, Bkr = ringc("kr", [128, 2, 128], BF16, 1)
                    S.op("pool", lambda e: e.memset(Vs[:, :, :, 128:129], 1.0), writes=[BV])

                    def c_s0(t):
                        xp.load(t, t)
                        S.dma("sp", tabt[t % 5][:], tab[t], writes=[Btab[t % 5]])

                    def c_s1(t):
                        xp.prep(t)

                    def c_s2(t):
                        xb, Bxb = xp.XB(t)
                        rs, Brs = xp.RS(t)
                        kb_ = t % 2
                        for dh in range(8):
                            S.op("pe", lambda e, dh=dh: e.matmul(bank[kb_], lhsT=xb[:, dh, :], rhs=Wkv[:, dh, :], start=(dh == 0), stop=(dh == 7)),
                                 reads=[Bxb, BWkv], writes=[PB[kb_]], signal=(dh == 7))
                        S.op("act", lambda e: e.activation(out=Vs[:, t, :, 0:128], in_=bank[kb_][:, 256:512].rearrange("p (h d) -> p h d", h=2), func=AF.Identity, scale=rs[:, 0:1], bias=0.0),
                             reads=[PB[kb_], Brs], writes=[BV])
                        heads = [(bank[kb_][:, h * 128:(h + 1) * 128], PB[kb_]) for h in range(2)]
                        head_norm(heads, rs, Brs, 128, kn[t % 3], Bkn[t % 3], ssq[t % 2], Bssq[t % 2])

                    def c_s3(t):
                        k = t % 2
                        rope(kn[t % 3], Bkn[t % 3], 2, tabt[t % 5], Btab[t % 5], t1[0], t2[0], Bt1[0], Bt2[0], kr[0], Bkr[0])
                        tb = 2 + t % 2
                        for h in range(2):
                            S.op("pe", lambda e, h=h: e.transpose(bankb[tb][:, h * 128:(h + 1) * 128], kr[0][:, h, :], idb[:]),
                                 reads=[Bkr[0], Bc], writes=[PB[tb]], signal=(h == 1))
                        S.op("act", lambda e: e.copy(out=KT[:, :, t, :], in_=bankb[tb][:, 0:256].rearrange("p (h k) -> p h k", h=2)),
                             reads=[PB[tb]], writes=[BKT])

                    pipeline([c_s0, c_s1, c_s2, c_s3], NT)
                    S.barrier()

                if "D" in phases:
                    with contextlib.ExitStack() as c4:
                        QTt = [T(c4, "QTd%d" % i, [128, 8, 128], BF16) for i in range(2)]
                        sat = T(c4, "sat", [128, 8, 128], BF16)
                        gat = T(c4, "gat", [128, 1024], BF16)
                        mft = T(c4, "mft", [128, 1024], F32)
                        xTt = T(c4, "xTt", [128, 8, 128], F32)
                        pT = [T(c4, "pT%d" % i, [128, 1024], BF16) for i in range(4)]
                        asb = T(c4, "asb", [128, 8, 128], BF16)
                        aT = T(c4, "aT", [128, 8, 128], BF16)
                        tmpf = T(c4, "tmpf", [128, 1024], F32)
                        mg = T(c4, "mg", [128, 1024], BF16)
                        mT = T(c4, "mT", [128, 8, 128], BF16)
                        osb = T(c4, "osb", [128, 8, 128], F32)
                        rinv = T(c4, "rinv", [128, 4], F32)
                        BQd = [Buf(), Buf()]
                        Bsat, Bgat, Bmft, BxT, Basb, BaT, Btmp, Bmg, BmT, Bosb, Brinv = [Buf() for _ in range(11)]
                        BpT = [Buf() for _ in range(4)]
                        obank = [bank[6][:, 0:129], bank[6][:, 256:385], bank[7][:, 0:129], bank[7][:, 256:385]]
                        POB = [PB[6], PB[6], PB[7], PB[7]]
                        sc = float(1.0 / np.sqrt(128.0))
                        S.dma("sp", QTt[0][:].rearrange("p h q -> p (h q)"), QTs[0], reads=[BQTs[0]], writes=[BQd[0]])
                        for dp in range(NQ):
                            sl = dp % 2
                            if dp + 1 < NQ:
                                S.dma("sp", QTt[1 - sl][:].rearrange("p h q -> p (h q)"), QTs[dp + 1], reads=[BQTs[dp + 1]], writes=[BQd[1 - sl]])
                            S.dma("sp", sat[:].rearrange("p h q -> p (h q)"), SAs[dp], reads=[BSAs[dp]], writes=[Bsat])
                            S.dma("sp", gat[:], GAs[dp], reads=[BGAs[dp]], writes=[Bgat])
                            S.dma("sp", mft[:], MFs[dp], reads=[BMFs[dp]], writes=[Bmft])
                            S.dma("sp", xTt[:], xs[4 * dp], writes=[BxT])
                            for kvh in range(2):
                                def qk(j, kvh=kvh, sl=sl):
                                    p2 = j % 2
                                    for i in range(2):
                                        kb = 2 * j + i
                                        S.op("pe", lambda e, i=i, kb=kb: e.matmul(pk[p2][:, i * 512:(i + 1) * 512], lhsT=KT[:, kvh, kb, :],
                                                                                rhs=QTt[sl][:, 4 * kvh:4 * kvh + 4, :].rearrange("p h q -> p (h q)"), start=True, stop=True),
                                             reads=[BKT, BQd[sl]], writes=[PB[2 * p2 + i]], signal=(i == 1))
                                    S.op("act", lambda e: e.activation(out=pT[j % 4][:], in_=pk[p2][:], func=AF.Exp, scale=sc, bias=negB[:, 0:1]),
                                         reads=[PB[2 * p2], PB[2 * p2 + 1], Bc], writes=[BpT[j % 4]])

                                def pv(j, kvh=kvh):
                                    for i in range(2):
                                        kb = 2 * j + i
                                        for hh in range(4):
                                            S.op("pe", lambda e, i=i, kb=kb, hh=hh: e.matmul(obank[hh], lhsT=pT[j % 4][:, i * 512 + hh * 128:i * 512 + hh * 128 + 128],
                                                                                           rhs=Vs[:, kb, kvh, :], start=(kb == 0 and hh % 2 == 0), stop=(kb == NT - 1), skip_group_check=True),
                                                 reads=[BpT[j % 4], BV], writes=[POB[hh]], signal=(i == 1 and hh == 3))
                                qk(0)
                                qk(1)
                                for j in range(NT // 2):
                                    if j + 2 < NT // 2:
                                        qk(j + 2)
                                    pv(j)
                                for hh in range(4):
                                    h = 4 * kvh + hh
                                    S.op("dve", lambda e, hh=hh: e.reciprocal(out=rinv[:, hh:hh + 1], in_=obank[hh][:, 128:129]),
                                         reads=[POB[hh]], writes=[Brinv])
                                    S.op("dve", lambda e, hh=hh, h=h: e.scalar_tensor_tensor(out=asb[:, h, :], in0=obank[hh][:, 0:128], scalar=rinv[:, hh:hh + 1], in1=sat[:, h, :], op0=ALU.mult, op1=ALU.mult),
                                         reads=[POB[hh], Brinv, Bsat], writes=[Basb])
                            for h in range(8):
                                S.op("pe", lambda e, h=h: e.transpose(bankb[4][:, h * 128:(h + 1) * 128], asb[:, h, :], idb[:]),
                                     reads=[Basb, Bc], writes=[PB[4]], signal=True)
                            S.op("dve", lambda e: e.tensor_copy(out=aT[:].rearrange("p h q -> p (h q)"), in_=bankb[4][:, 0:1024]), reads=[PB[4]], writes=[BaT])
                            for c in range(2):
                                for eh in range(8):
                                    S.op("pe", lambda e, c=c, eh=eh: e.matmul(bank[5], lhsT=aT[:, eh, :], rhs=Wap[:, eh, c * 512:(c + 1) * 512], start=(eh == 0), stop=(eh == 7)),
                                         reads=[BaT, BWap], writes=[PB[5]], signal=True)
                                S.op("dve", lambda e, c=c: e.tensor_tensor(out=tmpf[:, c * 512:(c + 1) * 512], in0=bank[5], in1=gat[:, c * 512:(c + 1) * 512], op=ALU.mult),
                                     reads=[PB[5], Bgat], writes=[Btmp])
                            S.op("dve", lambda e: e.tensor_tensor(out=mg[:], in0=tmpf[:], in1=mft[:], op=ALU.add),
                                 reads=[Btmp, Bmft], writes=[Bmg])
                            for h in range(8):
                                S.op("pe", lambda e, h=h: e.transpose(bankb[4][:, h * 128:(h + 1) * 128], mg[:, h * 128:(h + 1) * 128], idb[:]),
                                     reads=[Bmg, Bc], writes=[PB[4]], signal=True)
                            S.op("dve", lambda e: e.tensor_copy(out=mT[:].rearrange("p h q -> p (h q)"), in_=bankb[4][:, 0:1024]), reads=[PB[4]], writes=[BmT])
                            for half in range(2):
                                for e4 in range(4):
                                    eo = 4 * half + e4
                                    for dh in range(8):
                                        S.op("pe", lambda e, eo=eo, e4=e4, dh=dh: e.matmul(bank[5][:, e4 * 128:(e4 + 1) * 128], lhsT=Wout[:, dh, eo * 128:(eo + 1) * 128], rhs=mT[:, dh, :], start=(dh == 0), stop=(dh == 7)),
                                             reads=[BmT, BWout], writes=[PB[5]], signal=True)
                                S.op("dve", lambda e, half=half: e.tensor_tensor(out=osb[:, 4 * half:4 * half + 4, :].rearrange("p h q -> p (h q)"), in0=bank[5],
                                                                                   in1=xTt[:, 4 * half:4 * half + 4, :].rearrange("p h q -> p (h q)"), op=ALU.add),
                                     reads=[PB[5], BxT], writes=[Bosb])
                            S.dma("pool", outT[dp], osb[:], reads=[Bosb], writes=[out_buf])

        S.wait_all("sp", [out_buf])
        S.wait_all("pool", [out_buf])
        S.emit()
    return nc


def _alpha(j):
    return np.array([4 * (t // 4) + ((j + t % 4) % 4) for t in range(NT)], dtype=np.int64)


def _consts(j):
    al = _alpha(j)
    d = 4 * np.arange(32, dtype=np.int64) + j
    bt = np.arange(128, dtype=np.int64)
    num = (128 * bt[:, None, None] * d[None, None, :] + al[None, :, None] * d[None, None, :]) % 16384
    ang = 2.0 * np.pi * num.astype(np.float64) / 16384.0
    sc = 1.0 / np.sqrt(128.0)
    ec = np.concatenate([np.cos(ang) * sc, -np.sin(ang) * sc], axis=2).astype(np.float32)
    a2 = 2.0 * np.pi * ((bt[:, None] * bt[None, :]) % 128).astype(np.float64) / 128.0
    Cc, Sc = np.cos(a2) * sc, np.sin(a2) * sc
    wc = np.concatenate([Cc, -Sc, Sc, Cc], axis=1).astype(np.float32)
    a3 = 2.0 * np.pi * ((al[:, None] * bt[None, :]) % 128).astype(np.float64) / 128.0
    f2 = np.concatenate([np.cos(a3) * sc, np.sin(a3) * sc], axis=1).astype(np.float32)
    inv = (np.float32(10000.0) ** (-np.arange(0, 64, 2, dtype=np.float32) / np.float32(64))).astype(np.float32)
    s = al[:, None] + 128 * bt[None, :]
    row = (s // 64).astype(np.float32)
    col = (s % 64).astype(np.float32)
    ar = (row[:, :, None] * inv[None, None, :]).astype(np.float32)
    ac = (col[:, :, None] * inv[None, None, :]).astype(np.float32)
    cr, sr, cc, scn = np.cos(ar), np.sin(ar), np.cos(ac), np.sin(ac)
    tab = np.concatenate([cr, cr, cc, cc, -sr, sr, -scn, scn], axis=2).astype(np.float32)
    return al, ec, wc, f2, np.ascontiguousarray(tab)


def kernel(x, norm_g, w_in, q_norm_g, k_norm_g, w_attn_proj, w_fourier_proj, w_merge, b_merge, w_out, _phases=PHASES, _debug=False):
    x = np.asarray(x, dtype=np.float32)
    f = lambda a: np.ascontiguousarray(np.asarray(a, dtype=np.float32))
    gl = f(np.asarray(norm_g)[0].reshape(8, 128).T)
    gqk = f(np.concatenate([np.broadcast_to(np.asarray(q_norm_g)[0][None, :], (128, 128)),
                            np.broadcast_to(np.asarray(k_norm_g)[0][None, :], (128, 128))], axis=1))
    bm = f(np.broadcast_to(np.asarray(b_merge)[0][None, :], (128, 2048)))
    common = {"idn": np.eye(128, dtype=np.float32), "gl": gl, "gqk": gqk, "bm": bm,
              "w_in": f(np.asarray(w_in)[0]), "w_ap": f(np.asarray(w_attn_proj)[0]), "w_fp": f(np.asarray(w_fourier_proj)[0]),
              "w_mg": f(np.asarray(w_merge)[0]), "w_out": f(np.asarray(w_out)[0])}
    in_maps = []
    for core in range(8):
        b, j = core // 4, core % 4
        al, ec, wc, f2, tab = _consts(j)
        xv = x[b].reshape(128, 128, 8, 128).transpose(1, 3, 2, 0)
        xsv = np.ascontiguousarray(xv[al])
        m = dict(common)
        m.update({"xs": xsv, "tab": tab, "ec": ec, "wc": wc, "f2": f2})
        in_maps.append(m)
    nc = build_nc(_phases, _debug)
    res = run_bass_kernel_spmd(nc, in_maps, core_ids=list(range(8)))
    out = np.empty((B_, S_, D_), dtype=np.float32)
    for core in range(8):
        b, j = core // 4, core % 4
        o = res.results[core]["outT"].transpose(3, 0, 2, 1).reshape(128, NQ, 1024)
        out[b].reshape(128, NQ, 4, 1024)[:, :, j, :] = o
    if _debug:
        return out, res
    return out
```

```python
import contextlib
import numpy as np
import concourse.bass as bass
import concourse.mybir as mybir
from concourse.bass_utils import run_bass_kernel_spmd

F32 = mybir.dt.float32
BF16 = mybir.dt.bfloat16
AF = mybir.ActivationFunctionType
ALU = mybir.AluOpType
AX = mybir.AxisListType

B_, S_, D_ = 2, 16384, 1024
NT = 128
NQ = 32
EPS = 1e-6
PHASES = "ABCD"


class Buf:
    __slots__ = ("w", "r", "multi", "excl")

    def __init__(self, multi=False, excl=False):
        self.w = []
        self.r = []
        self.multi = multi
        self.excl = excl
        _ALL_BUFS.append(self)


_ALL_BUFS = []


class _Rec:
    def __getattr__(self, name):
        def f(*a, **k):
            self.call = (name, a, k)
            return self
        return f


def _free(ap):
    n = 1
    for d in ap.shape[1:]:
        n *= d
    return n


class _Op:
    __slots__ = ("idx", "eng", "calls", "preds", "succs", "dur", "lat", "dma", "pos", "tok", "start", "nready", "indeg", "prio")


class Sched:
    ENG = ("pe", "act", "dve", "pool", "sp")
    XLAT = 250.0

    def __init__(self, nc, ctx, n_dma_sems=24):
        self.nc = nc
        self.streams = {e: [] for e in self.ENG}
        self.sems = {}
        self.cnt = {e: 0 for e in self.ENG}
        self.seen = {e: {} for e in self.ENG}
        for e in self.ENG:
            self.sems[e] = ctx.enter_context(nc.semaphore("s_" + e))
        self.dma_pool = {}
        for q in ("sp", "pool"):
            lst = []
            for i in range(n_dma_sems):
                key = "d_%s_%d" % (q, i)
                self.sems[key] = ctx.enter_context(nc.semaphore(key))
                lst.append(key)
            self.dma_pool[q] = lst
        self.dma_rr = {"sp": 0, "pool": 0}
        self.dma_val = {}
        self.ops = []
        self.pend = {e: None for e in self.ENG}
        self.simlog = {e: [] for e in self.ENG}
        self.cur_prio = 0
        del _ALL_BUFS[:]

    def _est(self, eng, call):
        name, a, k = call
        try:
            if eng == "pe":
                if name == "transpose":
                    return 70.0
                return 12.0 + _free(k["rhs"]) / 2.3
            out = k.get("out", a[0] if a else None)
            n = _free(out)
            if eng == "act":
                return 224.0 + 0.833 * n
            if eng == "dve":
                return 70.0 + 1.0 * n
            return 120.0 + 1.8 * n
        except Exception:
            return 300.0

    def _deps(self, op, reads, writes):
        preds = op.preds
        i = op.idx
        for b in reads:
            for p in b.w:
                if p != i:
                    preds[p] = True
            if b.excl:
                for p in b.r:
                    if p != i and p not in preds:
                        preds[p] = False
        for b in writes:
            if not b.multi:
                for p in b.w:
                    if p != i:
                        preds[p] = True
            for p in b.r:
                if p != i and p not in preds:
                    preds[p] = False
        for b in reads:
            if not b.r or b.r[-1] != i:
                b.r.append(i)
        for b in writes:
            if b.multi:
                if not b.w or b.w[-1] != i:
                    b.w.append(i)
            else:
                b.w = [i]
                b.r = []

    def _new(self, eng):
        op = _Op()
        op.idx = len(self.ops)
        op.eng = eng
        op.calls = []
        op.preds = {}
        op.dur = 0.0
        op.lat = 0.0
        op.dma = None
        op.prio = self.cur_prio
        self.ops.append(op)
        return op

    def op(self, eng, fn, reads=(), writes=(), signal=True):
        rec = _Rec()
        fn(rec)
        call = rec.call
        op = self.pend[eng]
        if op is None:
            op = self._new(eng)
            self.pend[eng] = op
        op.calls.append(call)
        op.dur += self._est(eng, call)
        self._deps(op, reads, writes)
        if signal:
            self.pend[eng] = None

    def dma(self, q, out, in_, reads=(), writes=()):
        assert self.pend[q] is None
        op = self._new(q)
        op.dma = (out, in_)
        op.dur = 60.0
        try:
            nbytes = out.shape[0] * _free(out) * (2 if out.dtype == BF16 else 4)
        except Exception:
            nbytes = 1 << 19
        op.lat = 2000.0 + nbytes / 120.0
        self._deps(op, reads, writes)

    def _flush(self):
        import heapq
        ops = self.ops
        if not ops:
            return
        for e in self.ENG:
            assert self.pend[e] is None, "unterminated instruction group on " + e
        n = len(ops)
        for o in ops:
            o.succs = []
            o.nready = 0.0
        for o in ops:
            for p in o.preds:
                ops[p].succs.append(o.idx)
            o.indeg = len(o.preds)
        free = {e: 0.0 for e in self.ENG}
        heap = [(0.0, o.prio, o.idx) for o in ops if o.indeg == 0]
        heapq.heapify(heap)
        order = {e: [] for e in self.ENG}
        done = 0
        while heap:
            key, _pr, i = heapq.heappop(heap)
            o = ops[i]
            st = max(o.nready, free[o.eng])
            if st > key + 1e-9:
                heapq.heappush(heap, (st, o.prio, i))
                continue
            o.start = st
            fin_eng = st + o.dur
            free[o.eng] = fin_eng
            fin = fin_eng + o.lat
            order[o.eng].append(o)
            done += 1
            for sidx in o.succs:
                s2 = ops[sidx]
                lat = 0.0 if (s2.eng == o.eng and o.dma is None) else self.XLAT
                if fin + lat > s2.nready:
                    s2.nready = fin + lat
                s2.indeg -= 1
                if s2.indeg == 0:
                    heapq.heappush(heap, (s2.nready, s2.prio, sidx))
        assert done == n, "dependency cycle"
        for e in self.ENG:
            for o in order[e]:
                if o.dma is None:
                    self.cnt[e] += 1
                    o.pos = self.cnt[e]
                    o.tok = (e, o.pos)
                else:
                    pool = self.dma_pool[e]
                    key = pool[self.dma_rr[e] % len(pool)]
                    self.dma_rr[e] += 1
                    prev = self.dma_val.get(key, 0)
                    self.dma_val[key] = prev + 16
                    o.tok = (key, prev + 16)
                    o.pos = prev
        sems = self.sems
        for e in self.ENG:
            seen = self.seen[e]
            for o in order[e]:
                waits = []
                for p, hard in o.preds.items():
                    po = ops[p]
                    if po.eng == e and po.dma is None and (e == "pe" or not hard):
                        continue
                    k, v = po.tok
                    if seen.get(k, 0) < v:
                        seen[k] = v
                        waits.append((k, v))
                if o.dma is not None:
                    k, v = o.tok
                    if o.pos > 0 and seen.get(k, 0) < o.pos:
                        seen[k] = o.pos
                        waits.append((k, o.pos))
                self.streams[e].append(self._mk(e, o, waits))
                self.simlog[e].append((waits, o.tok[0], 16 if o.dma is not None else 1))
        self.ops = []
        for b in _ALL_BUFS:
            b.w = []
            b.r = []

    def _mk(self, eng, o, waits):
        sems = self.sems
        semh = sems[eng]
        if o.dma is not None:
            out, in_ = o.dma
            key = o.tok[0]

            def emit(e):
                for (k, v) in waits:
                    e.wait_ge(sems[k], v)
                e.dma_start(out=out, in_=in_).then_inc(sems[key], 16)
            return emit
        calls = o.calls

        def emit(e):
            for (k, v) in waits:
                e.wait_ge(sems[k], v)
            ins = None
            for (cname, cargs, ckw) in calls:
                ins = getattr(e, cname)(*cargs, **ckw)
            ins.then_inc(semh, 1)
        return emit

    def barrier(self):
        self._flush()
        targets = [(e, self.cnt[e]) for e in self.ENG if self.cnt[e] > 0]
        targets += list(self.dma_val.items())
        sems = self.sems
        for eng in self.ENG:
            waits = []
            seen = self.seen[eng]
            for key, val in targets:
                if key == eng and eng == "pe":
                    continue
                if seen.get(key, 0) < val:
                    seen[key] = val
                    waits.append((key, val))

            def emit(e, waits=waits):
                for (k, v) in waits:
                    e.wait_ge(sems[k], v)

            self.streams[eng].append(emit)
            self.simlog[eng].append((waits, None, 0))

    def check(self):
        val = {}
        pc = {e: 0 for e in self.ENG}
        prog = True
        while prog:
            prog = False
            for e in self.ENG:
                lg = self.simlog[e]
                while pc[e] < len(lg):
                    waits, k, inc = lg[pc[e]]
                    if any(val.get(wk, 0) < wv for wk, wv in waits):
                        break
                    if k is not None:
                        val[k] = val.get(k, 0) + inc
                    pc[e] += 1
                    prog = True
        stuck = {e: (pc[e], len(self.simlog[e])) for e in self.ENG if pc[e] < len(self.simlog[e])}
        for e, (p, n) in stuck.items():
            waits, k, inc = self.simlog[e][p]
            print("STUCK", e, p, n, [(wk, wv, val.get(wk, 0)) for wk, wv in waits if val.get(wk, 0) < wv])
        return not stuck

    def wait_all(self, eng, bufs):
        pass

    def emit(self):
        self.barrier()
        assert self.check(), "semaphore deadlock in generated program"
        nc = self.nc
        st = self.streams
        with nc.Block() as block:
            @block.tensor
            def _(e):
                for f in st["pe"]:
                    f(e)

            @block.scalar
            def _(e):
                for f in st["act"]:
                    f(e)

            @block.vector
            def _(e):
                for f in st["dve"]:
                    f(e)

            @block.gpsimd
            def _(e):
                for f in st["pool"]:
                    f(e)

            @block.sync
            def _(e):
                for f in st["sp"]:
                    f(e)


def build_nc(phases=PHASES, debug=False):
    nc = bass.Bass("TRN2", target_bir_lowering=False)
    din = lambda name, shape: nc.dram_tensor(name, shape, F32, kind="ExternalInput").ap()
    xs = din("xs", [NT, 128, 8, 128])
    tab = din("tab", [NT, 128, 256])
    ecd = din("ec", [128, NT, 64])
    wcd = din("wc", [128, 512])
    f2d = din("f2", [128, 256])
    idnd = din("idn", [128, 128])
    gld = din("gl", [128, 8])
    gqkd = din("gqk", [128, 256])
    bmd = din("bm", [128, 2048])
    w_in = din("w_in", [1024, 3584])
    w_ap = din("w_ap", [1024, 1024])
    w_fp = din("w_fp", [512, 1024])
    w_mg = din("w_mg", [1024, 2048])
    w_out = din("w_out", [1024, 1024])
    outT = nc.dram_tensor("outT", [NQ, 128, 8, 128], F32, kind="ExternalOutput").ap()
    QTs = nc.dram_tensor("QTs", [NQ, 128, 1024], BF16, kind="Internal").ap()
    SAs = nc.dram_tensor("SAs", [NQ, 128, 1024], BF16, kind="Internal").ap()
    GAs = nc.dram_tensor("GAs", [NQ, 128, 1024], BF16, kind="Internal").ap()
    MFs = nc.dram_tensor("MFs", [NQ, 128, 1024], F32, kind="Internal").ap()
    dbg = {}
    if debug:
        dbg["fraw"] = nc.dram_tensor("dbg_fraw", [128, NQ, 512], F32, kind="ExternalOutput").ap()

    with contextlib.ExitStack() as ctx:
        S = Sched(nc, ctx)

        def T(c, name, shape, dt):
            return c.enter_context(nc.sbuf_tensor("sb_" + name, shape, dt))

        pk = [ctx.enter_context(nc.psum_tensor("pk%d" % i, [128, 1024], F32)) for i in range(4)]
        bank = [pk[i // 2][:, (i % 2) * 512:(i % 2) * 512 + 512] for i in range(8)]
        bankb = [bank[i].bitcast(BF16) for i in range(8)]
        PB = [Buf(excl=True) for _ in range(8)]

        idb = T(ctx, "idb", [128, 128], BF16)
        ones = T(ctx, "ones", [128, 1], BF16)
        gl = T(ctx, "gl", [128, 8], F32)
        gqk = T(ctx, "gqk", [128, 256], F32)
        negB = T(ctx, "negB", [128, 1], F32)
        gqs = T(ctx, "gqs", [128, 128], F32)
        Bgqs = Buf(multi=True)
        sm = T(ctx, "sm", [128, 8], F32)
        Bc = Buf()
        out_buf = Buf(multi=True)

        with contextlib.ExitStack() as c0:
            idf = T(c0, "idf", [128, 128], F32)
            Bi = Buf()
            S.dma("sp", idf[:], idnd, writes=[Bi])
            S.dma("sp", gl[:], gld, writes=[Bc])
            S.dma("sp", gqk[:], gqkd, writes=[Bc])
            S.op("dve", lambda e: e.tensor_copy(out=idb[:], in_=idf[:]), reads=[Bi], writes=[Bc])
            S.op("pool", lambda e: e.memset(ones[:], 1.0), writes=[Bc])
            for blk in range(2):
                for f in range(2):
                    S.op("dve", lambda e, blk=blk, f=f: e.tensor_copy(out=gqs[:, blk * 64 + f * 32:blk * 64 + f * 32 + 32], in_=gqk[:, blk * 64 + (1 - f) * 32:blk * 64 + (1 - f) * 32 + 32]),
                         reads=[Bc], writes=[Bgqs])
            S.op("dve", lambda e: e.tensor_tensor(out=idf[:, 0:128], in0=gqk[:, 0:128], in1=gqk[:, 0:128], op=ALU.mult),
                 reads=[Bc], writes=[Bi])
            S.op("dve", lambda e: e.tensor_reduce(out=sm[:, 0:1], in_=idf[:, 0:128], axis=AX.X, op=ALU.max),
                 reads=[Bi], writes=[Bc])
            S.op("dve", lambda e: e.tensor_tensor(out=idf[:, 0:128], in0=gqk[:, 128:256], in1=gqk[:, 128:256], op=ALU.mult),
                 reads=[Bc], writes=[Bi])
            S.op("dve", lambda e: e.tensor_reduce(out=sm[:, 1:2], in_=idf[:, 0:128], axis=AX.X, op=ALU.max),
                 reads=[Bi], writes=[Bc])
            S.op("dve", lambda e: e.tensor_tensor(out=sm[:, 2:3], in0=sm[:, 0:1], in1=sm[:, 1:2], op=ALU.mult),
                 reads=[Bc], writes=[Bc])
            S.op("act", lambda e: e.activation(out=sm[:, 3:4], in_=sm[:, 2:3], func=AF.Sqrt, scale=128.0, bias=0.0),
                 reads=[Bc], writes=[Bc])
            S.op("dve", lambda e: e.tensor_scalar(out=negB[:], in0=sm[:, 3:4], scalar1=-1.0, scalar2=None, op0=ALU.mult),
                 reads=[Bc], writes=[Bc])
            S.barrier()

        def pipeline(stages, n):
            ns = len(stages)
            for step in range(n + ns - 1):
                for si, f in enumerate(stages):
                    t = step - si
                    if 0 <= t < n:
                        f(t)

        junk = T(ctx, "junk", [128, 128], BF16)
        Bjunk = Buf(multi=True)

        class XPipe:
            def __init__(self, c, tag, ssb=(7,), rb=3):
                self.ssb = ssb
                self.rb = rb
                self.xt = [T(c, "xt%s%d" % (tag, i), [128, 8, 128], F32) for i in range(2)]
                self.xq = [T(c, "xq%s%d" % (tag, i), [128, 8, 128], BF16) for i in range(2)]
                self.xb = [T(c, "xb%s%d" % (tag, i), [128, 8, 128], BF16) for i in range(rb)]
                self.rs = [T(c, "rs%s%d" % (tag, i), [128, 2], F32) for i in range(rb)]
                self.Bxt = [Buf() for _ in range(2)]
                self.Bxq = [Buf() for _ in range(2)]
                self.Bxb = [Buf() for _ in range(rb)]
                self.Brs = [Buf() for _ in range(rb)]

            def load(self, i, t):
                S.dma("sp", self.xt[i % 2][:], xs[t], writes=[self.Bxt[i % 2]])

            def XB(self, i):
                return self.xb[i % self.rb], self.Bxb[i % self.rb]

            def RS(self, i):
                return self.rs[i % self.rb], self.Brs[i % self.rb]

            def prep(self, i):
                xt, xq, Bxt, Bxq = self.xt[i % 2], self.xq[i % 2], self.Bxt[i % 2], self.Bxq[i % 2]
                xb, Bxb = self.XB(i)
                rs, Brs = self.RS(i)
                sb_ = self.ssb[i % len(self.ssb)]
                S.op("dve", lambda e: e.tensor_copy(out=xb[:], in_=xt[:]), reads=[Bxt], writes=[Bxb])
                S.op("act", lambda e: e.activation(out=xq[:], in_=xt[:], func=AF.Square, scale=1.0, bias=0.0), reads=[Bxt], writes=[Bxq])
                for dh in range(8):
                    S.op("pe", lambda e, dh=dh: e.matmul(bank[sb_][:, 0:1], lhsT=xq[:, dh, :], rhs=ones[:], start=(dh == 0), stop=(dh == 7)),
                         reads=[Bxq, Bc], writes=[PB[sb_]], signal=(dh == 7))
                S.op("act", lambda e: e.activation(out=rs[:, 1:2], in_=bank[sb_][:, 0:1], func=AF.Sqrt, scale=1.0 / D_, bias=EPS),
                     reads=[PB[sb_]], writes=[Brs])
                S.op("dve", lambda e: e.reciprocal(out=rs[:, 0:1], in_=rs[:, 1:2]), reads=[Brs], writes=[Brs])

        def head_norm(heads, rs, Brs, gcol, dst, Bdst, ssq, Bss):
            H = len(heads)
            S.op("dve", lambda e: e.memset(ssq[:, 0:8], 0.0), writes=[Bss])
            for h, (pa, pbuf) in enumerate(heads):
                S.op("act", lambda e, h=h, pa=pa: e.activation(out=junk[:], in_=pa, func=AF.Square, scale=rs[:, 0:1], bias=0.0, accum_out=ssq[:, h:h + 1]),
                     reads=[pbuf, Brs], writes=[Bjunk, Bss])
            S.op("act", lambda e: e.activation(out=ssq[:, 8:8 + H], in_=ssq[:, 0:H], func=AF.Sqrt, scale=1.0 / 128, bias=EPS),
                 reads=[Bss], writes=[Bss])
            S.op("dve", lambda e: e.reciprocal(out=ssq[:, 16:16 + H], in_=ssq[:, 8:8 + H]), reads=[Bss], writes=[Bss])
            S.op("dve", lambda e: e.tensor_scalar(out=ssq[:, 24:24 + H], in0=ssq[:, 16:16 + H], scalar1=rs[:, 0:1], scalar2=None, op0=ALU.mult),
                 reads=[Bss, Brs], writes=[Bss])
            for h, (pa, pbuf) in enumerate(heads):
                S.op("dve", lambda e, h=h, pa=pa: e.scalar_tensor_tensor(out=dst[:, h * 128:(h + 1) * 128], in0=pa, scalar=ssq[:, 24 + h:25 + h],
                                                                        in1=gqk[:, gcol:gcol + 128], op0=ALU.mult, op1=ALU.mult),
                     reads=[pbuf, Bss, Bc], writes=[Bdst])

        def rope(src, Bsrc, H, tabt, Btab, t1, t2, Bt1, Bt2, dst, Bdst, rq=None, Brq=None):
            W = H * 128
            s3 = src[:, 0:W].rearrange("p (h d) -> p h d", h=H)
            cos_b = tabt[:, 0:128].unsqueeze(1).broadcast_to([128, H, 128])
            S.op("dve", lambda e: e.tensor_tensor(out=t1[:, 0:W].rearrange("p (h d) -> p h d", h=H), in0=s3, in1=cos_b, op=ALU.mult),
                 reads=[Bsrc, Btab], writes=[Bt1])
            s5 = src[:, 0:W].rearrange("p (h b f w) -> p h b f w", h=H, b=2, f=2)
            t5 = t2[:, 0:W].rearrange("p (h b f w) -> p h b f w", h=H, b=2, f=2)
            sn4 = tabt[:, 128:256].rearrange("p (b f w) -> p b f w", b=2, f=2)
            for f in range(2):
                sin_b = sn4[:, :, f, :].unsqueeze(1).broadcast_to([128, H, 2, 32])
                S.op("dve", lambda e, f=f, sin_b=sin_b: e.tensor_tensor(out=t5[:, :, :, f, :], in0=s5[:, :, :, 1 - f, :], in1=sin_b, op=ALU.mult),
                     reads=[Bsrc, Btab], writes=[Bt2])
            if rq is None:
                S.op("dve", lambda e: e.tensor_tensor(out=dst[:].rearrange("p h d -> p (h d)"), in0=t1[:, 0:W], in1=t2[:, 0:W], op=ALU.add),
                     reads=[Bt1, Bt2], writes=[Bdst])
            else:
                S.op("dve", lambda e: e.tensor_tensor(out=t1[:, 0:W], in0=t1[:, 0:W], in1=t2[:, 0:W], op=ALU.add),
                     reads=[Bt1, Bt2], writes=[Bt1])
                S.op("dve", lambda e: e.tensor_tensor(out=dst[:], in0=t1[:, 0:W].rearrange("p (h d) -> p h d", h=H), in1=rq.unsqueeze(2).broadcast_to([128, H, 128]), op=ALU.mult),
                     reads=[Bt1, Brq], writes=[Bdst])

        def load_w(stg, Bstg, dst, Bdst, wd, col0, ncols, fold, kh=8, state=[0]):
            cw = stg[0].shape[2]
            for c0_ in range(0, ncols, cw):
                w = min(cw, ncols - c0_)
                i = state[0] % 2
                state[0] += 1
                src = wd[:, col0 + c0_:col0 + c0_ + w].rearrange("(dh dl) e -> dl dh e", dl=128)
                S.dma("sp", stg[i][:, 0:kh, 0:w], src, writes=[Bstg[i]])
                if fold:
                    for dh in range(kh):
                        S.op("dve", lambda e, dh=dh, i=i, c0_=c0_, w=w: e.tensor_scalar(
                            out=dst[:, dh, c0_:c0_ + w], in0=stg[i][:, dh, 0:w], scalar1=gl[:, dh:dh + 1], scalar2=None, op0=ALU.mult),
                            reads=[Bstg[i], Bc], writes=[Bdst])
                else:
                    S.op("dve", lambda e, i=i, c0_=c0_, w=w: e.tensor_copy(out=dst[:, :, c0_:c0_ + w], in_=stg[i][:, 0:kh, 0:w]),
                         reads=[Bstg[i]], writes=[Bdst])

        cab = contextlib.ExitStack()
        fraw = T(cab, "fraw", [128, NQ, 512], BF16)
        Bfraw = Buf(multi=True)

        if "A" in phases:
            with contextlib.ExitStack() as ca:
                Wu = T(ca, "Wu", [128, 8, 512], BF16)
                Ec = T(ca, "Ec", [128, NT, 64], BF16)
                Wc = T(ca, "Wc", [128, 512], BF16)
                F2 = T(ca, "F2", [128, 256], BF16)
                BWu, BEc, BWc, BF2, BY = Buf(multi=True), Buf(multi=True), Buf(), Buf(), Buf(multi=True)
                stg = [T(ca, "stgA%d" % i, [128, 8, 256], F32) for i in range(2)]
                Bstg = [Buf(), Buf()]
                load_w(stg, Bstg, Wu, BWu, w_in, 2560, 512, True)
                for i in range(4):
                    sf = stg[i % 2][:].rearrange("p a b -> p (a b)")
                    S.dma("sp", sf, ecd[:, 32 * i:32 * i + 32, :].rearrange("p a b -> p (a b)"), writes=[Bstg[i % 2]])
                    S.op("dve", lambda e, i=i, sf=sf: e.tensor_copy(out=Ec[:, 32 * i:32 * i + 32, :].rearrange("p a b -> p (a b)"), in_=sf),
                         reads=[Bstg[i % 2]], writes=[BEc])
                S.dma("sp", stg[0][:, 0:2, :].rearrange("p a b -> p (a b)"), wcd, writes=[Bstg[0]])
                S.op("dve", lambda e: e.tensor_copy(out=Wc[:], in_=stg[0][:, 0:2, :].rearrange("p a b -> p (a b)")), reads=[Bstg[0]], writes=[BWc])
                S.dma("sp", stg[1][:, 0, 0:256], f2d, writes=[Bstg[1]])
                S.op("dve", lambda e: e.tensor_copy(out=F2[:], in_=stg[1][:, 0, 0:256]), reads=[Bstg[1]], writes=[BF2])
                Ysb = T(ca, "Ysb", [128, NT, 4, 2, 32], BF16)
                Zg = [T(ca, "Zg%d" % i, [128, 32, 2, 128], BF16) for i in range(2)]
                BZ = [Buf(multi=True), Buf(multi=True)]
                usb = [T(ca, "usb%d" % i, [128, 512], BF16) for i in range(3)]
                Bus = [Buf() for _ in range(3)]
                xp = XPipe(ca, "A", (6, 7), rb=3)

                def a_s0(t):
                    xp.load(t, t)

                def a_s1(t):
                    xp.prep(t)

                def a_s2(t):
                    xb, Bxb = xp.XB(t)
                    rs, Brs = xp.RS(t)
                    ub = t % 2
                    for dh in range(8):
                        S.op("pe", lambda e, dh=dh: e.matmul(bank[ub], lhsT=xb[:, dh, :], rhs=Wu[:, dh, :], start=(dh == 0), stop=(dh == 7)),
                             reads=[Bxb, BWu], writes=[PB[ub]], signal=(dh == 7))
                    S.op("dve", lambda e: e.tensor_scalar(out=usb[t % 3][:], in0=bank[ub], scalar1=rs[:, 0:1], scalar2=None, op0=ALU.mult),
                         reads=[PB[ub], Brs], writes=[Bus[t % 3]])

                def a_s3(t):
                    yb = 2 + t % 2
                    for g in range(4):
                        S.op("pe", lambda e, g=g: e.matmul(bank[yb][:, g * 64:(g + 1) * 64], lhsT=usb[t % 3][:, g * 128:(g + 1) * 128], rhs=Ec[:, t, :], start=True, stop=True),
                             reads=[Bus[t % 3], BEc], writes=[PB[yb]], signal=(g == 3))
                    S.op("act", lambda e: e.copy(out=Ysb[:, t, :, :, :].rearrange("p g r d -> p (g r d)"), in_=bank[yb][:, 0:256]),
                         reads=[PB[yb]], writes=[BY])

                pipeline([a_s0, a_s1, a_s2, a_s3], NT)
                n = 0
                for g in range(4):
                    zg, bz = Zg[g % 2], BZ[g % 2]
                    for dp in range(32):
                        zb = 4 + n % 2
                        S.op("pe", lambda e, g=g, dp=dp, zb=zb: e.matmul(bank[zb][:, 0:256], lhsT=Ysb[:, :, g, 0, dp], rhs=Wc[:, 0:256], start=True, stop=False),
                             reads=[BY, BWc], writes=[PB[zb]], signal=False)
                        S.op("pe", lambda e, g=g, dp=dp, zb=zb: e.matmul(bank[zb][:, 0:256], lhsT=Ysb[:, :, g, 1, dp], rhs=Wc[:, 256:512], start=False, stop=True),
                             reads=[BY, BWc], writes=[PB[zb]])
                        if n % 2 == 0:
                            S.op("dve", lambda e, dp=dp, zb=zb, zg=zg: e.tensor_copy(out=zg[:, dp, :, :].rearrange("p r c -> p (r c)"), in_=bank[zb][:, 0:256]),
                                 reads=[PB[zb]], writes=[bz])
                        else:
                            S.op("act", lambda e, dp=dp, zb=zb, zg=zg: e.copy(out=zg[:, dp, :, :].rearrange("p r c -> p (r c)"), in_=bank[zb][:, 0:256]),
                                 reads=[PB[zb]], writes=[bz])
                        n += 1
                    for q4 in range(8):
                        fb = q4 % 2
                        for i in range(4):
                            dp = 4 * q4 + i
                            S.op("pe", lambda e, dp=dp, i=i, fb=fb, zg=zg: e.matmul(bank[fb][:, i * 128:(i + 1) * 128], lhsT=F2[:, 0:128], rhs=zg[:, dp, 0, :], start=True, stop=False),
                                 reads=[bz, BF2], writes=[PB[fb]], signal=False)
                            S.op("pe", lambda e, dp=dp, i=i, fb=fb, zg=zg: e.matmul(bank[fb][:, i * 128:(i + 1) * 128], lhsT=F2[:, 128:256], rhs=zg[:, dp, 1, :], start=False, stop=True),
                                 reads=[bz, BF2], writes=[PB[fb]], signal=(i == 3))
                        dst = fraw[:, 4 * q4:4 * q4 + 4, g * 128:(g + 1) * 128]
                        src = bank[fb].rearrange("p (i c) -> p i c", i=4)
                        if q4 % 2 == 0:
                            S.op("dve", lambda e, dst=dst, src=src: e.tensor_copy(out=dst, in_=src), reads=[PB[fb]], writes=[Bfraw])
                        else:
                            S.op("act", lambda e, dst=dst, src=src: e.copy(out=dst, in_=src), reads=[PB[fb]], writes=[Bfraw])
                if debug:
                    dstg = [T(ca, "dstg%d" % i, [128, 4, 512], F32) for i in range(2)]
                    Bstg = [Buf(), Buf()]
                    for i in range(8):
                        k = i % 2
                        S.op("dve", lambda e, i=i, k=k: e.tensor_copy(out=dstg[k][:], in_=fraw[:, 4 * i:4 * i + 4, :]),
                             reads=[Bfraw], writes=[Bstg[k]])
                        S.dma("pool", dbg["fraw"][:, 4 * i:4 * i + 4, :], dstg[k][:], reads=[Bstg[k]], writes=[out_buf])
                S.barrier()

        BQTs = [Buf() for _ in range(NQ)]
        BSAs = [Buf() for _ in range(NQ)]
        BGAs = [Buf() for _ in range(NQ)]
        BMFs = [Buf() for _ in range(NQ)]
        if "B" in phases:
            with contextlib.ExitStack() as cb:
                Wq = T(cb, "Wq", [128, 8, 1024], BF16)
                Wza = T(cb, "Wza", [128, 8, 1024], BF16)
                Wzf = T(cb, "Wzf", [128, 8, 512], BF16)
                Wmg = T(cb, "Wmg", [128, 8, 2048], BF16)
                Wfp = T(cb, "Wfp", [128, 4, 1024], BF16)
                bm = T(cb, "bm", [128, 2048], F32)
                BWq, BWza, BWzf, BWmg, BWfp = [Buf(multi=True) for _ in range(5)]
                Bbm = Buf()
                S.dma("sp", bm[:], bmd, writes=[Bbm])
                stg = [T(cb, "stgB%d" % i, [128, 8, 256], F32) for i in range(2)]
                Bstg = [Buf(), Buf()]
                load_w(stg, Bstg, Wq, BWq, w_in, 0, 1024, True)
                load_w(stg, Bstg, Wza, BWza, w_in, 1536, 1024, True)
                load_w(stg, Bstg, Wzf, BWzf, w_in, 3072, 512, True)
                load_w(stg, Bstg, Wfp, BWfp, w_fp, 0, 1024, False, kh=4)
                load_w(stg, Bstg, Wmg, BWmg, w_mg, 0, 2048, True)
                xp = XPipe(cb, "B", (7,), rb=4)

                def ring(name, shape, dt, n=2):
                    return [T(cb, "%s%d" % (name, i), shape, dt) for i in range(n)], [Buf() for _ in range(n)]
                tabt, Btab = ring("tabB", [128, 256], F32, 4)
                tabg, Btabg = ring("tabg", [128, 256], F32, 1)
                qn, Bqn = ring("qn", [128, 1024], F32, 2)
                ssq, Bssq = ring("ssq", [128, 32], F32, 2)
                t1, Bt1 = ring("t1", [128, 1024], F32, 1)
                t2, Bt2 = ring("t2", [128, 1024], F32, 1)
                qr, Bqr = ring("qr", [128, 8, 128], BF16, 1)
                QTt, BQT = ring("QTt", [128, 1024], BF16, 1)
                sg, Bsg = ring("sg", [128, 512], F32, 2)
                sa, Bsa = ring("sa", [128, 1024], BF16, 2)
                sf, Bsf = ring("sf", [128, 512], F32, 1)
                fg, Bfg = ring("fg", [128, 512], BF16, 2)
                fT, BfT = ring("fT", [128, 4, 128], BF16, 2)
                gt, Bgt = ring("gt", [128, 512], F32, 1)
                ga, Bga = ring("ga", [128, 1024], BF16, 1)
                gf, Bgf = ring("gf", [128, 512], F32, 1)
                mf, Bmf = ring("mf", [128, 1024], F32, 1)

                def proj(i, pb, Wt, BW, c0_):
                    xb, Bxb = xp.XB(i)
                    for dh in range(8):
                        S.op("pe", lambda e, dh=dh: e.matmul(bank[pb], lhsT=xb[:, dh, :], rhs=Wt[:, dh, c0_:c0_ + 512], start=(dh == 0), stop=(dh == 7)),
                             reads=[Bxb, BW], writes=[PB[pb]], signal=(dh == 7))

                def b_s0(i):
                    xp.load(i, 4 * i)
                    S.dma("sp", tabt[i % 4][:], tab[4 * i], writes=[Btab[i % 4]])

                def b_s1(i):
                    xp.prep(i)

                def b_s2(i):
                    rs, Brs = xp.RS(i)
                    k = i % 2
                    proj(i, 0, Wq, BWq, 0)
                    proj(i, 1, Wq, BWq, 512)
                    for c in range(2):
                        S.op("act", lambda e, c=c: e.activation(out=qn[k][:, c * 512:(c + 1) * 512], in_=bank[c], func=AF.Identity, scale=rs[:, 0:1], bias=0.0),
                             reads=[PB[c], Brs], writes=[Bqn[k]])
                    sq_, Bsq_ = ssq[k], Bssq[k]
                    S.op("dve", lambda e: e.memset(sq_[:, 0:8], 0.0), writes=[Bsq_])
                    for h in range(8):
                        S.op("act", lambda e, h=h: e.activation(out=junk[:], in_=qn[k][:, h * 128:(h + 1) * 128], func=AF.Square, scale=1.0, bias=0.0, accum_out=sq_[:, h:h + 1]),
                             reads=[Bqn[k]], writes=[Bjunk, Bsq_])
                    S.op("act", lambda e: e.activation(out=sq_[:, 8:16], in_=sq_[:, 0:8], func=AF.Sqrt, scale=1.0 / 128, bias=EPS), reads=[Bsq_], writes=[Bsq_])
                    S.op("dve", lambda e: e.reciprocal(out=sq_[:, 16:24], in_=sq_[:, 8:16]), reads=[Bsq_], writes=[Bsq_])
                    tg, Btg = tabg[0], Btabg[0]
                    S.op("dve", lambda e: e.tensor_tensor(out=tg[:, 0:128], in0=tabt[i % 4][:, 0:128], in1=gqk[:, 0:128], op=ALU.mult),
                         reads=[Btab[i % 4], Bc], writes=[Btg])
                    S.op("dve", lambda e: e.tensor_tensor(out=tg[:, 128:256], in0=tabt[i % 4][:, 128:256], in1=gqs[:], op=ALU.mult),
                         reads=[Btab[i % 4], Bc], writes=[Btg])
                    rope(qn[k], Bqn[k], 8, tg, Btg, t1[0], t2[0], Bt1[0], Bt2[0], qr[0], Bqr[0], rq=sq_[:, 16:24], Brq=Bsq_)
                    for h in range(8):
                        S.op("pe", lambda e, h=h: e.transpose(bankb[2][:, h * 128:(h + 1) * 128], qr[0][:, h, :], idb[:]),
                             reads=[Bqr[0], Bc], writes=[PB[2]], signal=(h == 7))
                    S.op("dve", lambda e: e.tensor_copy(out=QTt[0][:], in_=bankb[2][:, 0:1024]), reads=[PB[2]], writes=[BQT[0]])
                    S.dma("pool", QTs[i], QTt[0][:], reads=[BQT[0]], writes=[BQTs[i]])

                def b_s3(i):
                    rs, Brs = xp.RS(i)
                    k = i % 2
                    for c in range(2):
                        pb = 3 + c
                        proj(i, pb, Wza, BWza, c * 512)
                        S.op("act", lambda e, pb=pb, c=c: e.activation(out=sg[c][:], in_=bank[pb], func=AF.Sigmoid, scale=rs[:, 0:1], bias=0.0),
                             reads=[PB[pb], Brs], writes=[Bsg[c]])
                        S.op("dve", lambda e, pb=pb, c=c: e.scalar_tensor_tensor(out=sa[k][:, c * 512:(c + 1) * 512], in0=bank[pb], scalar=rs[:, 0:1], in1=sg[c][:], op0=ALU.mult, op1=ALU.mult),
                             reads=[PB[pb], Brs, Bsg[c]], writes=[Bsa[k]])
                    S.dma("pool", SAs[i], sa[k][:], reads=[Bsa[k]], writes=[BSAs[i]])
                    proj(i, 5, Wzf, BWzf, 0)
                    S.op("act", lambda e: e.activation(out=sg[0][:], in_=bank[5], func=AF.Sigmoid, scale=rs[:, 0:1], bias=0.0),
                         reads=[PB[5], Brs], writes=[Bsg[0]])
                    S.op("dve", lambda e: e.scalar_tensor_tensor(out=sf[0][:], in0=bank[5], scalar=rs[:, 0:1], in1=sg[0][:], op0=ALU.mult, op1=ALU.mult),
                         reads=[PB[5], Brs, Bsg[0]], writes=[Bsf[0]])
                    S.op("dve", lambda e: e.tensor_tensor(out=fg[k][:], in0=sf[0][:], in1=fraw[:, i, :], op=ALU.mult),
                         reads=[Bsf[0], Bfraw], writes=[Bfg[k]])
                    for g in range(4):
                        S.op("pe", lambda e, g=g: e.transpose(bankb[6][:, g * 128:(g + 1) * 128], fg[k][:, g * 128:(g + 1) * 128], idb[:]),
                             reads=[Bfg[k], Bc], writes=[PB[6]], signal=(g == 3))
                    S.op("dve", lambda e: e.tensor_copy(out=fT[k][:].rearrange("p g t -> p (g t)"), in_=bankb[6][:, 0:512]), reads=[PB[6]], writes=[BfT[k]])
                    for c in range(2):
                        for g in range(4):
                            S.op("pe", lambda e, g=g, c=c: e.matmul(bank[c], lhsT=fT[k][:, g, :], rhs=Wfp[:, g, c * 512:(c + 1) * 512], start=(g == 0), stop=(g == 3)),
                                 reads=[BfT[k], BWfp], writes=[PB[c]], signal=(g == 3))
                    for c in range(4):
                        pb = 3 + c % 2
                        kk = c % 2
                        proj(i, pb, Wmg, BWmg, c * 512)
                        S.op("dve", lambda e, pb=pb, c=c, kk=kk: e.scalar_tensor_tensor(out=gt[0][:], in0=bank[pb], scalar=rs[:, 0:1], in1=bm[:, c * 512:(c + 1) * 512], op0=ALU.mult, op1=ALU.add),
                             reads=[PB[pb], Brs, Bbm], writes=[Bgt[0]])
                        if c < 2:
                            S.op("act", lambda e, c=c, kk=kk: e.activation(out=ga[0][:, c * 512:(c + 1) * 512], in_=gt[0][:], func=AF.Sigmoid, scale=1.0, bias=0.0),
                                 reads=[Bgt[0]], writes=[Bga[0]])
                        else:
                            S.op("act", lambda e, kk=kk: e.activation(out=gf[0][:], in_=gt[0][:], func=AF.Sigmoid, scale=1.0, bias=0.0),
                                 reads=[Bgt[0]], writes=[Bgf[0]])
                            S.op("dve", lambda e, c=c, kk=kk: e.tensor_tensor(out=mf[0][:, (c - 2) * 512:(c - 1) * 512], in0=bank[c - 2], in1=gf[0][:], op=ALU.mult),
                                 reads=[PB[c - 2], Bgf[0]], writes=[Bmf[0]])
                    S.dma("pool", GAs[i], ga[0][:], reads=[Bga[0]], writes=[BGAs[i]])
                    S.dma("pool", MFs[i], mf[0][:], reads=[Bmf[0]], writes=[BMFs[i]])

                pipeline([b_s0, b_s1, b_s2, b_s3], NQ)
                S.barrier()
        cab.close()
        if "C" in phases:
            with contextlib.ExitStack() as cd:
                KT = T(cd, "KT", [128, 2, NT, 128], BF16)
                Vs = T(cd, "Vs", [128, NT, 2, 129], BF16)
                Wap = T(cd, "Wap", [128, 8, 1024], BF16)
                Wout = T(cd, "Wout", [128, 8, 1024], BF16)
                BKT, BV, BWap, BWout = Buf(multi=True), Buf(multi=True), Buf(multi=True), Buf(multi=True)
                with contextlib.ExitStack() as cc:
                    Wkv = T(cc, "Wkv", [128, 8, 512], BF16)
                    BWkv = Buf(multi=True)
                    stg = [T(cc, "stgC%d" % i, [128, 8, 128], F32) for i in range(2)]
                    Bstg = [Buf(), Buf()]
                    load_w(stg, Bstg, Wkv, BWkv, w_in, 1024, 512, True)
                    load_w(stg, Bstg, Wap, BWap, w_ap, 0, 1024, False)
                    load_w(stg, Bstg, Wout, BWout, w_out, 0, 1024, False)
                    xp = XPipe(cc, "C", (6, 7), rb=3)

                    def ringc(name, shape, dt, n=2):
                        return [T(cc, "%s%d" % (name, i), shape, dt) for i in range(n)], [Buf() for _ in range(n)]
                    tabt, Btab = ringc("tabC", [128, 256], F32, 5)
                    kn, Bkn = ringc("kn", [128, 256], F32, 3)
                    ssq, Bssq = ringc("ssqc", [128, 32], F32, 2)
                    t1, Bt1 = ringc("t1c", [128, 256], F32, 1)
                    t2, Bt2 = ringc("t2c", [128, 256], F32, 1)
                    kr, Bkr = ringc("kr", [128, 2, 128], BF16, 1)
                    S.op("pool", lambda e: e.memset(Vs[:, :, :, 128:129], 1.0), writes=[BV])

                    def c_s0(t):
                        xp.load(t, t)
                        S.dma("sp", tabt[t % 5][:], tab[t], writes=[Btab[t % 5]])

                    def c_s1(t):
                        xp.prep(t)

                    def c_s2(t):
                        xb, Bxb = xp.XB(t)
                        rs, Brs = xp.RS(t)
                        kb_ = t % 2
                        for dh in range(8):
                            S.op("pe", lambda e, dh=dh: e.matmul(bank[kb_], lhsT=xb[:, dh, :], rhs=Wkv[:, dh, :], start=(dh == 0), stop=(dh == 7)),
                                 reads=[Bxb, BWkv], writes=[PB[kb_]], signal=(dh == 7))
                        S.op("act", lambda e: e.activation(out=Vs[:, t, :, 0:128], in_=bank[kb_][:, 256:512].rearrange("p (h d) -> p h d", h=2), func=AF.Identity, scale=rs[:, 0:1], bias=0.0),
                             reads=[PB[kb_], Brs], writes=[BV])
                        heads = [(bank[kb_][:, h * 128:(h + 1) * 128], PB[kb_]) for h in range(2)]
                        head_norm(heads, rs, Brs, 128, kn[t % 3], Bkn[t % 3], ssq[t % 2], Bssq[t % 2])

                    def c_s3(t):
                        k = t % 2
                        rope(kn[t % 3], Bkn[t % 3], 2, tabt[t % 5], Btab[t % 5], t1[0], t2[0], Bt1[0], Bt2[0], kr[0], Bkr[0])
                        tb = 2 + t % 2
                        for h in range(2):
                            S.op("pe", lambda e, h=h: e.transpose(bankb[tb][:, h * 128:(h + 1) * 128], kr[0][:, h, :], idb[:]),
                                 reads=[Bkr[0], Bc], writes=[PB[tb]], signal=(h == 1))
                        S.op("act", lambda e: e.copy(out=KT[:, :, t, :], in_=bankb[tb][:, 0:256].rearrange("p (h k) -> p h k", h=2)),
                             reads=[PB[tb]], writes=[BKT])

                    pipeline([c_s0, c_s1, c_s2, c_s3], NT)
                    S.barrier()

                if "D" in phases:
                    with contextlib.ExitStack() as c4:
                        QTt = [T(c4, "QTd%d" % i, [128, 8, 128], BF16) for i in range(2)]
                        sat = T(c4, "sat", [128, 8, 128], BF16)
                        gat = T(c4, "gat", [128, 1024], BF16)
                        mft = T(c4, "mft", [128, 1024], F32)
                        xTt = T(c4, "xTt", [128, 8, 128], F32)
                        pT = [T(c4, "pT%d" % i, [128, 1024], BF16) for i in range(4)]
                        asb = T(c4, "asb", [128, 8, 128], BF16)
                        aT = T(c4, "aT", [128, 8, 128], BF16)
                        tmpf = T(c4, "tmpf", [128, 1024], F32)
                        mg = T(c4, "mg", [128, 1024], BF16)
                        mT = T(c4, "mT", [128, 8, 128], BF16)
                        osb = T(c4, "osb", [128, 8, 128], F32)
                        rinv = T(c4, "rinv", [128, 4], F32)
                        BQd = [Buf(), Buf()]
                        Bsat, Bgat, Bmft, BxT, Basb, BaT, Btmp, Bmg, BmT, Bosb, Brinv = [Buf() for _ in range(11)]
                        BpT = [Buf() for _ in range(4)]
                        obank = [bank[6][:, 0:129], bank[6][:, 256:385], bank[7][:, 0:129], bank[7][:, 256:385]]
                        POB = [PB[6], PB[6], PB[7], PB[7]]
                        sc = float(1.0 / np.sqrt(128.0))
                        S.dma("sp", QTt[0][:].rearrange("p h q -> p (h q)"), QTs[0], reads=[BQTs[0]], writes=[BQd[0]])
                        for dp in range(NQ):
                            sl = dp % 2
                            if dp + 1 < NQ:
                                S.dma("sp", QTt[1 - sl][:].rearrange("p h q -> p (h q)"), QTs[dp + 1], reads=[BQTs[dp + 1]], writes=[BQd[1 - sl]])
                            S.dma("sp", sat[:].rearrange("p h q -> p (h q)"), SAs[dp], reads=[BSAs[dp]], writes=[Bsat])
                            S.dma("sp", gat[:], GAs[dp], reads=[BGAs[dp]], writes=[Bgat])
                            S.dma("sp", mft[:], MFs[dp], reads=[BMFs[dp]], writes=[Bmft])
                            S.dma("sp", xTt[:], xs[4 * dp], writes=[BxT])
                            for kvh in range(2):
                                def qk(j, kvh=kvh, sl=sl):
                                    p2 = j % 2
                                    for i in range(2):
                                        kb = 2 * j + i
                                        S.op("pe", lambda e, i=i, kb=kb: e.matmul(pk[p2][:, i * 512:(i + 1) * 512], lhsT=KT[:, kvh, kb, :],
                                                                                rhs=QTt[sl][:, 4 * kvh:4 * kvh + 4, :].rearrange("p h q -> p (h q)"), start=True, stop=True),
                                             reads=[BKT, BQd[sl]], writes=[PB[2 * p2 + i]], signal=(i == 1))
                                    S.op("act", lambda e: e.activation(out=pT[j % 4][:], in_=pk[p2][:], func=AF.Exp, scale=sc, bias=negB[:, 0:1]),
                                         reads=[PB[2 * p2], PB[2 * p2 + 1], Bc], writes=[BpT[j % 4]])

                                def pv(j, kvh=kvh):
                                    for i in range(2):
                                        kb = 2 * j + i
                                        for hh in range(4):
                                            S.op("pe", lambda e, i=i, kb=kb, hh=hh: e.matmul(obank[hh], lhsT=pT[j % 4][:, i * 512 + hh * 128:i * 512 + hh * 128 + 128],
                                                                                           rhs=Vs[:, kb, kvh, :], start=(kb == 0 and hh % 2 == 0), stop=(kb == NT - 1), skip_group_check=True),
                                                 reads=[BpT[j % 4], BV], writes=[POB[hh]], signal=(i == 1 and hh == 3))
                                qk(0)
                                qk(1)
                                for j in range(NT // 2):
                                    if j + 2 < NT // 2:
                                        qk(j + 2)
                                    pv(j)
                                for hh in range(4):
                                    h = 4 * kvh + hh
                                    S.op("dve", lambda e, hh=hh: e.reciprocal(out=rinv[:, hh:hh + 1], in_=obank[hh][:, 128:129]),
                                         reads=[POB[hh]], writes=[Brinv])
                                    S.op("dve", lambda e, hh=hh, h=h: e.scalar_tensor_tensor(out=asb[:, h, :], in0=obank[hh][:, 0:128], scalar=rinv[:, hh:hh + 1], in1=sat[:, h, :], op0=ALU.mult, op1=ALU.mult),
                                         reads=[POB[hh], Brinv, Bsat], writes=[Basb])
                            S.cur_prio = 1
                            for h in range(8):
                                S.op("pe", lambda e, h=h: e.transpose(bankb[4][:, h * 128:(h + 1) * 128], asb[:, h, :], idb[:]),
                                     reads=[Basb, Bc], writes=[PB[4]], signal=True)
                            S.op("dve", lambda e: e.tensor_copy(out=aT[:].rearrange("p h q -> p (h q)"), in_=bankb[4][:, 0:1024]), reads=[PB[4]], writes=[BaT])
                            for c in range(2):
                                for eh in range(8):
                                    S.op("pe", lambda e, c=c, eh=eh: e.matmul(bank[5], lhsT=aT[:, eh, :], rhs=Wap[:, eh, c * 512:(c + 1) * 512], start=(eh == 0), stop=(eh == 7)),
                                         reads=[BaT, BWap], writes=[PB[5]], signal=True)
                                S.op("dve", lambda e, c=c: e.tensor_tensor(out=tmpf[:, c * 512:(c + 1) * 512], in0=bank[5], in1=gat[:, c * 512:(c + 1) * 512], op=ALU.mult),
                                     reads=[PB[5], Bgat], writes=[Btmp])
                            S.op("dve", lambda e: e.tensor_tensor(out=mg[:], in0=tmpf[:], in1=mft[:], op=ALU.add),
                                 reads=[Btmp, Bmft], writes=[Bmg])
                            for h in range(8):
                                S.op("pe", lambda e, h=h: e.transpose(bankb[4][:, h * 128:(h + 1) * 128], mg[:, h * 128:(h + 1) * 128], idb[:]),
                                     reads=[Bmg, Bc], writes=[PB[4]], signal=True)
                            S.op("dve", lambda e: e.tensor_copy(out=mT[:].rearrange("p h q -> p (h q)"), in_=bankb[4][:, 0:1024]), reads=[PB[4]], writes=[BmT])
                            for half in range(2):
                                for e4 in range(4):
                                    eo = 4 * half + e4
                                    for dh in range(8):
                                        S.op("pe", lambda e, eo=eo, e4=e4, dh=dh: e.matmul(bank[5][:, e4 * 128:(e4 + 1) * 128], lhsT=Wout[:, dh, eo * 128:(eo + 1) * 128], rhs=mT[:, dh, :], start=(dh == 0), stop=(dh == 7)),
                                             reads=[BmT, BWout], writes=[PB[5]], signal=True)
                                S.op("dve", lambda e, half=half: e.tensor_tensor(out=osb[:, 4 * half:4 * half + 4, :].rearrange("p h q -> p (h q)"), in0=bank[5],
                                                                                   in1=xTt[:, 4 * half:4 * half + 4, :].rearrange("p h q -> p (h q)"), op=ALU.add),
                                     reads=[PB[5], BxT], writes=[Bosb])
                            S.dma("pool", outT[dp], osb[:], reads=[Bosb], writes=[out_buf])
                            S.cur_prio = 0

        S.wait_all("sp", [out_buf])
        S.wait_all("pool", [out_buf])
        S.emit()
    return nc


def _alpha(j):
    return np.array([4 * (t // 4) + ((j + t % 4) % 4) for t in range(NT)], dtype=np.int64)


def _consts(j):
    al = _alpha(j)
    d = 4 * np.arange(32, dtype=np.int64) + j
    bt = np.arange(128, dtype=np.int64)
    num = (128 * bt[:, None, None] * d[None, None, :] + al[None, :, None] * d[None, None, :]) % 16384
    ang = 2.0 * np.pi * num.astype(np.float64) / 16384.0
    sc = 1.0 / np.sqrt(128.0)
    ec = np.concatenate([np.cos(ang) * sc, -np.sin(ang) * sc], axis=2).astype(np.float32)
    a2 = 2.0 * np.pi * ((bt[:, None] * bt[None, :]) % 128).astype(np.float64) / 128.0
    Cc, Sc = np.cos(a2) * sc, np.sin(a2) * sc
    wc = np.concatenate([Cc, -Sc, Sc, Cc], axis=1).astype(np.float32)
    a3 = 2.0 * np.pi * ((al[:, None] * bt[None, :]) % 128).astype(np.float64) / 128.0
    f2 = np.concatenate([np.cos(a3) * sc, np.sin(a3) * sc], axis=1).astype(np.float32)
    inv = (np.float32(10000.0) ** (-np.arange(0, 64, 2, dtype=np.float32) / np.float32(64))).astype(np.float32)
    s = al[:, None] + 128 * bt[None, :]
    row = (s // 64).astype(np.float32)
    col = (s % 64).astype(np.float32)
    ar = (row[:, :, None] * inv[None, None, :]).astype(np.float32)
    ac = (col[:, :, None] * inv[None, None, :]).astype(np.float32)
    cr, sr, cc, scn = np.cos(ar), np.sin(ar), np.cos(ac), np.sin(ac)
    tab = np.concatenate([cr, cr, cc, cc, -sr, sr, -scn, scn], axis=2).astype(np.float32)
    return al, ec, wc, f2, np.ascontiguousarray(tab)


def kernel(x, norm_g, w_in, q_norm_g, k_norm_g, w_attn_proj, w_fourier_proj, w_merge, b_merge, w_out, _phases=PHASES, _debug=False):
    x = np.asarray(x, dtype=np.float32)
    f = lambda a: np.ascontiguousarray(np.asarray(a, dtype=np.float32))
    gl = f(np.asarray(norm_g)[0].reshape(8, 128).T)
    gqk = f(np.concatenate([np.broadcast_to(np.asarray(q_norm_g)[0][None, :], (128, 128)),
                            np.broadcast_to(np.asarray(k_norm_g)[0][None, :], (128, 128))], axis=1))
    bm = f(np.broadcast_to(np.asarray(b_merge)[0][None, :], (128, 2048)))
    common = {"idn": np.eye(128, dtype=np.float32), "gl": gl, "gqk": gqk, "bm": bm,
              "w_in": f(np.asarray(w_in)[0]), "w_ap": f(np.asarray(w_attn_proj)[0]), "w_fp": f(np.asarray(w_fourier_proj)[0]),
              "w_mg": f(np.asarray(w_merge)[0]), "w_out": f(np.asarray(w_out)[0])}
    in_maps = []
    for core in range(8):
        b, j = core // 4, core % 4
        al, ec, wc, f2, tab = _consts(j)
        xv = x[b].reshape(128, 128, 8, 128).transpose(1, 3, 2, 0)
        xsv = np.ascontiguousarray(xv[al])
        m = dict(common)
        m.update({"xs": xsv, "tab": tab, "ec": ec, "wc": wc, "f2": f2})
        in_maps.append(m)
    nc = build_nc(_phases, _debug)
    res = run_bass_kernel_spmd(nc, in_maps, core_ids=list(range(8)))
    out = np.empty((B_, S_, D_), dtype=np.float32)
    for core in range(8):
        b, j = core // 4, core % 4
        o = res.results[core]["outT"].transpose(3, 0, 2, 1).reshape(128, NQ, 1024)
        out[b].reshape(128, NQ, 4, 1024)[:, :, j, :] = o
    if _debug:
        return out, res
    return out
```

```python
import contextlib
import numpy as np
import concourse.bass as bass
import concourse.mybir as mybir
from concourse.bass_utils import run_bass_kernel_spmd

F32 = mybir.dt.float32
BF16 = mybir.dt.bfloat16
AF = mybir.ActivationFunctionType
ALU = mybir.AluOpType
AX = mybir.AxisListType

B_, S_, D_ = 2, 16384, 1024
NT = 128
NQ = 32
EPS = 1e-6
PHASES = "ABCD"


class Buf:
    __slots__ = ("w", "r", "multi", "excl")

    def __init__(self, multi=False, excl=False):
        self.w = []
        self.r = []
        self.multi = multi
        self.excl = excl
        _ALL_BUFS.append(self)


_ALL_BUFS = []


class _Rec:
    def __getattr__(self, name):
        def f(*a, **k):
            self.call = (name, a, k)
            return self
        return f


def _free(ap):
    n = 1
    for d in ap.shape[1:]:
        n *= d
    return n


class _Op:
    __slots__ = ("idx", "eng", "calls", "preds", "succs", "dur", "lat", "dma", "pos", "tok", "start", "nready", "indeg", "prio")


class Sched:
    ENG = ("pe", "act", "dve", "pool", "sp")
    XLAT = 250.0

    def __init__(self, nc, ctx, n_dma_sems=24):
        self.nc = nc
        self.streams = {e: [] for e in self.ENG}
        self.sems = {}
        self.cnt = {e: 0 for e in self.ENG}
        self.seen = {e: {} for e in self.ENG}
        for e in self.ENG:
            self.sems[e] = ctx.enter_context(nc.semaphore("s_" + e))
        self.dma_pool = {}
        for q in ("sp", "pool"):
            lst = []
            for i in range(n_dma_sems):
                key = "d_%s_%d" % (q, i)
                self.sems[key] = ctx.enter_context(nc.semaphore(key))
                lst.append(key)
            self.dma_pool[q] = lst
        self.dma_rr = {"sp": 0, "pool": 0}
        self.dma_val = {}
        self.ops = []
        self.pend = {e: None for e in self.ENG}
        self.simlog = {e: [] for e in self.ENG}
        self.cur_prio = 0
        self.pe_scale = 1.0
        del _ALL_BUFS[:]

    def _est(self, eng, call):
        name, a, k = call
        try:
            if eng == "pe":
                if name == "transpose":
                    return 70.0
                return (12.0 + _free(k["rhs"]) / 2.3) * self.pe_scale
            out = k.get("out", a[0] if a else None)
            n = _free(out)
            if eng == "act":
                return 224.0 + 0.833 * n
            if eng == "dve":
                return 70.0 + 1.0 * n
            return 120.0 + 1.8 * n
        except Exception:
            return 300.0

    def _deps(self, op, reads, writes):
        preds = op.preds
        i = op.idx
        for b in reads:
            for p in b.w:
                if p != i:
                    preds[p] = True
            if b.excl:
                for p in b.r:
                    if p != i and p not in preds:
                        preds[p] = False
        for b in writes:
            if not b.multi:
                for p in b.w:
                    if p != i:
                        preds[p] = True
            for p in b.r:
                if p != i and p not in preds:
                    preds[p] = False
        for b in reads:
            if not b.r or b.r[-1] != i:
                b.r.append(i)
        for b in writes:
            if b.multi:
                if not b.w or b.w[-1] != i:
                    b.w.append(i)
            else:
                b.w = [i]
                b.r = []

    def _new(self, eng):
        op = _Op()
        op.idx = len(self.ops)
        op.eng = eng
        op.calls = []
        op.preds = {}
        op.dur = 0.0
        op.lat = 0.0
        op.dma = None
        op.prio = self.cur_prio
        self.ops.append(op)
        return op

    def op(self, eng, fn, reads=(), writes=(), signal=True):
        rec = _Rec()
        fn(rec)
        call = rec.call
        op = self.pend[eng]
        if op is None:
            op = self._new(eng)
            self.pend[eng] = op
        op.calls.append(call)
        op.dur += self._est(eng, call)
        self._deps(op, reads, writes)
        if signal:
            self.pend[eng] = None

    def dma(self, q, out, in_, reads=(), writes=()):
        assert self.pend[q] is None
        op = self._new(q)
        op.dma = (out, in_)
        op.dur = 60.0
        try:
            nbytes = out.shape[0] * _free(out) * (2 if out.dtype == BF16 else 4)
        except Exception:
            nbytes = 1 << 19
        op.lat = 2000.0 + nbytes / 120.0
        self._deps(op, reads, writes)

    def _flush(self):
        import heapq
        ops = self.ops
        if not ops:
            return
        for e in self.ENG:
            assert self.pend[e] is None, "unterminated instruction group on " + e
        n = len(ops)
        for o in ops:
            o.succs = []
            o.nready = 0.0
        for o in ops:
            for p in o.preds:
                ops[p].succs.append(o.idx)
            o.indeg = len(o.preds)
        free = {e: 0.0 for e in self.ENG}
        heap = [(0.0, o.prio, o.idx) for o in ops if o.indeg == 0]
        heapq.heapify(heap)
        order = {e: [] for e in self.ENG}
        done = 0
        while heap:
            key, _pr, i = heapq.heappop(heap)
            o = ops[i]
            st = max(o.nready, free[o.eng])
            if st > key + 1e-9:
                heapq.heappush(heap, (st, o.prio, i))
                continue
            o.start = st
            fin_eng = st + o.dur
            free[o.eng] = fin_eng
            fin = fin_eng + o.lat
            order[o.eng].append(o)
            done += 1
            for sidx in o.succs:
                s2 = ops[sidx]
                lat = 0.0 if (s2.eng == o.eng and o.dma is None) else self.XLAT
                if fin + lat > s2.nready:
                    s2.nready = fin + lat
                s2.indeg -= 1
                if s2.indeg == 0:
                    heapq.heappush(heap, (s2.nready, s2.prio, sidx))
        assert done == n, "dependency cycle"
        for e in self.ENG:
            for o in order[e]:
                if o.dma is None:
                    self.cnt[e] += 1
                    o.pos = self.cnt[e]
                    o.tok = (e, o.pos)
                else:
                    pool = self.dma_pool[e]
                    key = pool[self.dma_rr[e] % len(pool)]
                    self.dma_rr[e] += 1
                    prev = self.dma_val.get(key, 0)
                    self.dma_val[key] = prev + 16
                    o.tok = (key, prev + 16)
                    o.pos = prev
        sems = self.sems
        for e in self.ENG:
            seen = self.seen[e]
            for o in order[e]:
                waits = []
                for p, hard in o.preds.items():
                    po = ops[p]
                    if po.eng == e and po.dma is None and (e == "pe" or not hard):
                        continue
                    k, v = po.tok
                    if seen.get(k, 0) < v:
                        seen[k] = v
                        waits.append((k, v))
                if o.dma is not None:
                    k, v = o.tok
                    if o.pos > 0 and seen.get(k, 0) < o.pos:
                        seen[k] = o.pos
                        waits.append((k, o.pos))
                self.streams[e].append(self._mk(e, o, waits))
                self.simlog[e].append((waits, o.tok[0], 16 if o.dma is not None else 1))
        self.ops = []
        for b in _ALL_BUFS:
            b.w = []
            b.r = []

    def _mk(self, eng, o, waits):
        sems = self.sems
        semh = sems[eng]
        if o.dma is not None:
            out, in_ = o.dma
            key = o.tok[0]

            def emit(e):
                for (k, v) in waits:
                    e.wait_ge(sems[k], v)
                e.dma_start(out=out, in_=in_).then_inc(sems[key], 16)
            return emit
        calls = o.calls

        def emit(e):
            for (k, v) in waits:
                e.wait_ge(sems[k], v)
            ins = None
            for (cname, cargs, ckw) in calls:
                ins = getattr(e, cname)(*cargs, **ckw)
            ins.then_inc(semh, 1)
        return emit

    def barrier(self):
        self._flush()
        targets = [(e, self.cnt[e]) for e in self.ENG if self.cnt[e] > 0]
        targets += list(self.dma_val.items())
        sems = self.sems
        for eng in self.ENG:
            waits = []
            seen = self.seen[eng]
            for key, val in targets:
                if key == eng and eng == "pe":
                    continue
                if seen.get(key, 0) < val:
                    seen[key] = val
                    waits.append((key, val))

            def emit(e, waits=waits):
                for (k, v) in waits:
                    e.wait_ge(sems[k], v)

            self.streams[eng].append(emit)
            self.simlog[eng].append((waits, None, 0))

    def check(self):
        val = {}
        pc = {e: 0 for e in self.ENG}
        prog = True
        while prog:
            prog = False
            for e in self.ENG:
                lg = self.simlog[e]
                while pc[e] < len(lg):
                    waits, k, inc = lg[pc[e]]
                    if any(val.get(wk, 0) < wv for wk, wv in waits):
                        break
                    if k is not None:
                        val[k] = val.get(k, 0) + inc
                    pc[e] += 1
                    prog = True
        stuck = {e: (pc[e], len(self.simlog[e])) for e in self.ENG if pc[e] < len(self.simlog[e])}
        for e, (p, n) in stuck.items():
            waits, k, inc = self.simlog[e][p]
            print("STUCK", e, p, n, [(wk, wv, val.get(wk, 0)) for wk, wv in waits if val.get(wk, 0) < wv])
        return not stuck

    def wait_all(self, eng, bufs):
        pass

    def emit(self):
        self.barrier()
        assert self.check(), "semaphore deadlock in generated program"
        nc = self.nc
        st = self.streams
        with nc.Block() as block:
            @block.tensor
            def _(e):
                for f in st["pe"]:
                    f(e)

            @block.scalar
            def _(e):
                for f in st["act"]:
                    f(e)

            @block.vector
            def _(e):
                for f in st["dve"]:
                    f(e)

            @block.gpsimd
            def _(e):
                for f in st["pool"]:
                    f(e)

            @block.sync
            def _(e):
                for f in st["sp"]:
                    f(e)


def build_nc(phases=PHASES, debug=False):
    nc = bass.Bass("TRN2", target_bir_lowering=False)
    din = lambda name, shape: nc.dram_tensor(name, shape, F32, kind="ExternalInput").ap()
    xs = din("xs", [NT, 128, 8, 128])
    tab = din("tab", [NT, 128, 256])
    ecd = din("ec", [128, NT, 64])
    wcd = din("wc", [128, 512])
    f2d = din("f2", [128, 256])
    idnd = din("idn", [128, 128])
    gld = din("gl", [128, 8])
    gqkd = din("gqk", [128, 256])
    bmd = din("bm", [128, 2048])
    w_in = din("w_in", [1024, 3584])
    w_ap = din("w_ap", [1024, 1024])
    w_fp = din("w_fp", [512, 1024])
    w_mg = din("w_mg", [1024, 2048])
    w_out = din("w_out", [1024, 1024])
    outT = nc.dram_tensor("outT", [NQ, 128, 8, 128], F32, kind="ExternalOutput").ap()
    QTs = nc.dram_tensor("QTs", [NQ, 128, 1024], BF16, kind="Internal").ap()
    SAs = nc.dram_tensor("SAs", [NQ, 128, 1024], BF16, kind="Internal").ap()
    GAs = nc.dram_tensor("GAs", [NQ, 128, 1024], BF16, kind="Internal").ap()
    MFs = nc.dram_tensor("MFs", [NQ, 128, 1024], F32, kind="Internal").ap()
    dbg = {}
    if debug:
        dbg["fraw"] = nc.dram_tensor("dbg_fraw", [128, NQ, 512], F32, kind="ExternalOutput").ap()

    with contextlib.ExitStack() as ctx:
        S = Sched(nc, ctx)

        def T(c, name, shape, dt):
            return c.enter_context(nc.sbuf_tensor("sb_" + name, shape, dt))

        pk = [ctx.enter_context(nc.psum_tensor("pk%d" % i, [128, 1024], F32)) for i in range(4)]
        bank = [pk[i // 2][:, (i % 2) * 512:(i % 2) * 512 + 512] for i in range(8)]
        bankb = [bank[i].bitcast(BF16) for i in range(8)]
        PB = [Buf(excl=True) for _ in range(8)]

        idb = T(ctx, "idb", [128, 128], BF16)
        ones = T(ctx, "ones", [128, 1], BF16)
        gl = T(ctx, "gl", [128, 8], F32)
        gqk = T(ctx, "gqk", [128, 256], F32)
        negB = T(ctx, "negB", [128, 1], F32)
        gqs = T(ctx, "gqs", [128, 128], F32)
        Bgqs = Buf(multi=True)
        sm = T(ctx, "sm", [128, 8], F32)
        Bc = Buf()
        out_buf = Buf(multi=True)

        with contextlib.ExitStack() as c0:
            idf = T(c0, "idf", [128, 128], F32)
            Bi = Buf()
            S.dma("sp", idf[:], idnd, writes=[Bi])
            S.dma("sp", gl[:], gld, writes=[Bc])
            S.dma("sp", gqk[:], gqkd, writes=[Bc])
            S.op("dve", lambda e: e.tensor_copy(out=idb[:], in_=idf[:]), reads=[Bi], writes=[Bc])
            S.op("pool", lambda e: e.memset(ones[:], 1.0), writes=[Bc])
            for blk in range(2):
                for f in range(2):
                    S.op("dve", lambda e, blk=blk, f=f: e.tensor_copy(out=gqs[:, blk * 64 + f * 32:blk * 64 + f * 32 + 32], in_=gqk[:, blk * 64 + (1 - f) * 32:blk * 64 + (1 - f) * 32 + 32]),
                         reads=[Bc], writes=[Bgqs])
            S.op("dve", lambda e: e.tensor_tensor(out=idf[:, 0:128], in0=gqk[:, 0:128], in1=gqk[:, 0:128], op=ALU.mult),
                 reads=[Bc], writes=[Bi])
            S.op("dve", lambda e: e.tensor_reduce(out=sm[:, 0:1], in_=idf[:, 0:128], axis=AX.X, op=ALU.max),
                 reads=[Bi], writes=[Bc])
            S.op("dve", lambda e: e.tensor_tensor(out=idf[:, 0:128], in0=gqk[:, 128:256], in1=gqk[:, 128:256], op=ALU.mult),
                 reads=[Bc], writes=[Bi])
            S.op("dve", lambda e: e.tensor_reduce(out=sm[:, 1:2], in_=idf[:, 0:128], axis=AX.X, op=ALU.max),
                 reads=[Bi], writes=[Bc])
            S.op("dve", lambda e: e.tensor_tensor(out=sm[:, 2:3], in0=sm[:, 0:1], in1=sm[:, 1:2], op=ALU.mult),
                 reads=[Bc], writes=[Bc])
            S.op("act", lambda e: e.activation(out=sm[:, 3:4], in_=sm[:, 2:3], func=AF.Sqrt, scale=128.0, bias=0.0),
                 reads=[Bc], writes=[Bc])
            S.op("dve", lambda e: e.tensor_scalar(out=negB[:], in0=sm[:, 3:4], scalar1=-1.0, scalar2=None, op0=ALU.mult),
                 reads=[Bc], writes=[Bc])
            S.barrier()

        def pipeline(stages, n):
            ns = len(stages)
            for step in range(n + ns - 1):
                for si, f in enumerate(stages):
                    t = step - si
                    if 0 <= t < n:
                        f(t)

        junk = T(ctx, "junk", [128, 128], BF16)
        Bjunk = Buf(multi=True)

        class XPipe:
            def __init__(self, c, tag, ssb=(7,), rb=3):
                self.ssb = ssb
                self.rb = rb
                self.xt = [T(c, "xt%s%d" % (tag, i), [128, 8, 128], F32) for i in range(2)]
                self.xq = [T(c, "xq%s%d" % (tag, i), [128, 8, 128], BF16) for i in range(2)]
                self.xb = [T(c, "xb%s%d" % (tag, i), [128, 8, 128], BF16) for i in range(rb)]
                self.rs = [T(c, "rs%s%d" % (tag, i), [128, 2], F32) for i in range(rb)]
                self.Bxt = [Buf() for _ in range(2)]
                self.Bxq = [Buf() for _ in range(2)]
                self.Bxb = [Buf() for _ in range(rb)]
                self.Brs = [Buf() for _ in range(rb)]

            def load(self, i, t):
                S.dma("sp", self.xt[i % 2][:], xs[t], writes=[self.Bxt[i % 2]])

            def XB(self, i):
                return self.xb[i % self.rb], self.Bxb[i % self.rb]

            def RS(self, i):
                return self.rs[i % self.rb], self.Brs[i % self.rb]

            def prep(self, i):
                xt, xq, Bxt, Bxq = self.xt[i % 2], self.xq[i % 2], self.Bxt[i % 2], self.Bxq[i % 2]
                xb, Bxb = self.XB(i)
                rs, Brs = self.RS(i)
                sb_ = self.ssb[i % len(self.ssb)]
                S.op("dve", lambda e: e.tensor_copy(out=xb[:], in_=xt[:]), reads=[Bxt], writes=[Bxb])
                S.op("act", lambda e: e.activation(out=xq[:], in_=xt[:], func=AF.Square, scale=1.0, bias=0.0), reads=[Bxt], writes=[Bxq])
                for dh in range(8):
                    S.op("pe", lambda e, dh=dh: e.matmul(bank[sb_][:, 0:1], lhsT=xq[:, dh, :], rhs=ones[:], start=(dh == 0), stop=(dh == 7)),
                         reads=[Bxq, Bc], writes=[PB[sb_]], signal=(dh == 7))
                S.op("act", lambda e: e.activation(out=rs[:, 1:2], in_=bank[sb_][:, 0:1], func=AF.Sqrt, scale=1.0 / D_, bias=EPS),
                     reads=[PB[sb_]], writes=[Brs])
                S.op("dve", lambda e: e.reciprocal(out=rs[:, 0:1], in_=rs[:, 1:2]), reads=[Brs], writes=[Brs])

        def head_norm(heads, rs, Brs, gcol, dst, Bdst, ssq, Bss):
            H = len(heads)
            S.op("dve", lambda e: e.memset(ssq[:, 0:8], 0.0), writes=[Bss])
            for h, (pa, pbuf) in enumerate(heads):
                S.op("act", lambda e, h=h, pa=pa: e.activation(out=junk[:], in_=pa, func=AF.Square, scale=rs[:, 0:1], bias=0.0, accum_out=ssq[:, h:h + 1]),
                     reads=[pbuf, Brs], writes=[Bjunk, Bss])
            S.op("act", lambda e: e.activation(out=ssq[:, 8:8 + H], in_=ssq[:, 0:H], func=AF.Sqrt, scale=1.0 / 128, bias=EPS),
                 reads=[Bss], writes=[Bss])
            S.op("dve", lambda e: e.reciprocal(out=ssq[:, 16:16 + H], in_=ssq[:, 8:8 + H]), reads=[Bss], writes=[Bss])
            S.op("dve", lambda e: e.tensor_scalar(out=ssq[:, 24:24 + H], in0=ssq[:, 16:16 + H], scalar1=rs[:, 0:1], scalar2=None, op0=ALU.mult),
                 reads=[Bss, Brs], writes=[Bss])
            for h, (pa, pbuf) in enumerate(heads):
                S.op("dve", lambda e, h=h, pa=pa: e.scalar_tensor_tensor(out=dst[:, h * 128:(h + 1) * 128], in0=pa, scalar=ssq[:, 24 + h:25 + h],
                                                                        in1=gqk[:, gcol:gcol + 128], op0=ALU.mult, op1=ALU.mult),
                     reads=[pbuf, Bss, Bc], writes=[Bdst])

        def rope(src, Bsrc, H, tabt, Btab, t1, t2, Bt1, Bt2, dst, Bdst, rq=None, Brq=None):
            W = H * 128
            s3 = src[:, 0:W].rearrange("p (h d) -> p h d", h=H)
            cos_b = tabt[:, 0:128].unsqueeze(1).broadcast_to([128, H, 128])
            S.op("dve", lambda e: e.tensor_tensor(out=t1[:, 0:W].rearrange("p (h d) -> p h d", h=H), in0=s3, in1=cos_b, op=ALU.mult),
                 reads=[Bsrc, Btab], writes=[Bt1])
            s5 = src[:, 0:W].rearrange("p (h b f w) -> p h b f w", h=H, b=2, f=2)
            t5 = t2[:, 0:W].rearrange("p (h b f w) -> p h b f w", h=H, b=2, f=2)
            sn4 = tabt[:, 128:256].rearrange("p (b f w) -> p b f w", b=2, f=2)
            for f in range(2):
                sin_b = sn4[:, :, f, :].unsqueeze(1).broadcast_to([128, H, 2, 32])
                S.op("dve", lambda e, f=f, sin_b=sin_b: e.tensor_tensor(out=t5[:, :, :, f, :], in0=s5[:, :, :, 1 - f, :], in1=sin_b, op=ALU.mult),
                     reads=[Bsrc, Btab], writes=[Bt2])
            if rq is None:
                S.op("dve", lambda e: e.tensor_tensor(out=dst[:].rearrange("p h d -> p (h d)"), in0=t1[:, 0:W], in1=t2[:, 0:W], op=ALU.add),
                     reads=[Bt1, Bt2], writes=[Bdst])
            else:
                S.op("dve", lambda e: e.tensor_tensor(out=t1[:, 0:W], in0=t1[:, 0:W], in1=t2[:, 0:W], op=ALU.add),
                     reads=[Bt1, Bt2], writes=[Bt1])
                S.op("dve", lambda e: e.tensor_tensor(out=dst[:], in0=t1[:, 0:W].rearrange("p (h d) -> p h d", h=H), in1=rq.unsqueeze(2).broadcast_to([128, H, 128]), op=ALU.mult),
                     reads=[Bt1, Brq], writes=[Bdst])

        def load_w(stg, Bstg, dst, Bdst, wd, col0, ncols, fold, kh=8, state=[0]):
            cw = stg[0].shape[2]
            for c0_ in range(0, ncols, cw):
                w = min(cw, ncols - c0_)
                i = state[0] % 2
                state[0] += 1
                src = wd[:, col0 + c0_:col0 + c0_ + w].rearrange("(dh dl) e -> dl dh e", dl=128)
                S.dma("sp", stg[i][:, 0:kh, 0:w], src, writes=[Bstg[i]])
                if fold:
                    for dh in range(kh):
                        S.op("dve", lambda e, dh=dh, i=i, c0_=c0_, w=w: e.tensor_scalar(
                            out=dst[:, dh, c0_:c0_ + w], in0=stg[i][:, dh, 0:w], scalar1=gl[:, dh:dh + 1], scalar2=None, op0=ALU.mult),
                            reads=[Bstg[i], Bc], writes=[Bdst])
                else:
                    S.op("dve", lambda e, i=i, c0_=c0_, w=w: e.tensor_copy(out=dst[:, :, c0_:c0_ + w], in_=stg[i][:, 0:kh, 0:w]),
                         reads=[Bstg[i]], writes=[Bdst])

        cab = contextlib.ExitStack()
        fraw = T(cab, "fraw", [128, NQ, 512], BF16)
        Bfraw = Buf(multi=True)

        if "A" in phases:
            S.pe_scale = 1.5
            with contextlib.ExitStack() as ca:
                Wu = T(ca, "Wu", [128, 8, 512], BF16)
                Ec = T(ca, "Ec", [128, NT, 64], BF16)
                Wc = T(ca, "Wc", [128, 512], BF16)
                F2 = T(ca, "F2", [128, 256], BF16)
                BWu, BEc, BWc, BF2, BY = Buf(multi=True), Buf(multi=True), Buf(), Buf(), Buf(multi=True)
                stg = [T(ca, "stgA%d" % i, [128, 8, 256], F32) for i in range(2)]
                Bstg = [Buf(), Buf()]
                load_w(stg, Bstg, Wu, BWu, w_in, 2560, 512, True)
                for i in range(4):
                    sf = stg[i % 2][:].rearrange("p a b -> p (a b)")
                    S.dma("sp", sf, ecd[:, 32 * i:32 * i + 32, :].rearrange("p a b -> p (a b)"), writes=[Bstg[i % 2]])
                    S.op("dve", lambda e, i=i, sf=sf: e.tensor_copy(out=Ec[:, 32 * i:32 * i + 32, :].rearrange("p a b -> p (a b)"), in_=sf),
                         reads=[Bstg[i % 2]], writes=[BEc])
                S.dma("sp", stg[0][:, 0:2, :].rearrange("p a b -> p (a b)"), wcd, writes=[Bstg[0]])
                S.op("dve", lambda e: e.tensor_copy(out=Wc[:], in_=stg[0][:, 0:2, :].rearrange("p a b -> p (a b)")), reads=[Bstg[0]], writes=[BWc])
                S.dma("sp", stg[1][:, 0, 0:256], f2d, writes=[Bstg[1]])
                S.op("dve", lambda e: e.tensor_copy(out=F2[:], in_=stg[1][:, 0, 0:256]), reads=[Bstg[1]], writes=[BF2])
                Ysb = T(ca, "Ysb", [128, NT, 4, 2, 32], BF16)
                Zg = [T(ca, "Zg%d" % i, [128, 32, 2, 128], BF16) for i in range(2)]
                BZ = [Buf(multi=True), Buf(multi=True)]
                usb = [T(ca, "usb%d" % i, [128, 512], BF16) for i in range(3)]
                Bus = [Buf() for _ in range(3)]
                xp = XPipe(ca, "A", (6, 7), rb=3)

                def a_s0(t):
                    xp.load(t, t)

                def a_s1(t):
                    xp.prep(t)

                def a_s2(t):
                    xb, Bxb = xp.XB(t)
                    rs, Brs = xp.RS(t)
                    ub = t % 2
                    for dh in range(8):
                        S.op("pe", lambda e, dh=dh: e.matmul(bank[ub], lhsT=xb[:, dh, :], rhs=Wu[:, dh, :], start=(dh == 0), stop=(dh == 7)),
                             reads=[Bxb, BWu], writes=[PB[ub]], signal=(dh == 7))
                    S.op("dve", lambda e: e.tensor_scalar(out=usb[t % 3][:], in0=bank[ub], scalar1=rs[:, 0:1], scalar2=None, op0=ALU.mult),
                         reads=[PB[ub], Brs], writes=[Bus[t % 3]])

                def a_s3(t):
                    yb = 2 + t % 2
                    for g in range(4):
                        S.op("pe", lambda e, g=g: e.matmul(bank[yb][:, g * 64:(g + 1) * 64], lhsT=usb[t % 3][:, g * 128:(g + 1) * 128], rhs=Ec[:, t, :], start=True, stop=True),
                             reads=[Bus[t % 3], BEc], writes=[PB[yb]], signal=(g == 3))
                    S.op("act", lambda e: e.copy(out=Ysb[:, t, :, :, :].rearrange("p g r d -> p (g r d)"), in_=bank[yb][:, 0:256]),
                         reads=[PB[yb]], writes=[BY])

                pipeline([a_s0, a_s1, a_s2, a_s3], NT)
                n = 0
                for g in range(4):
                    zg, bz = Zg[g % 2], BZ[g % 2]
                    for dp in range(32):
                        zb = 4 + n % 2
                        S.op("pe", lambda e, g=g, dp=dp, zb=zb: e.matmul(bank[zb][:, 0:256], lhsT=Ysb[:, :, g, 0, dp], rhs=Wc[:, 0:256], start=True, stop=False),
                             reads=[BY, BWc], writes=[PB[zb]], signal=False)
                        S.op("pe", lambda e, g=g, dp=dp, zb=zb: e.matmul(bank[zb][:, 0:256], lhsT=Ysb[:, :, g, 1, dp], rhs=Wc[:, 256:512], start=False, stop=True),
                             reads=[BY, BWc], writes=[PB[zb]])
                        if n % 2 == 0:
                            S.op("dve", lambda e, dp=dp, zb=zb, zg=zg: e.tensor_copy(out=zg[:, dp, :, :].rearrange("p r c -> p (r c)"), in_=bank[zb][:, 0:256]),
                                 reads=[PB[zb]], writes=[bz])
                        else:
                            S.op("act", lambda e, dp=dp, zb=zb, zg=zg: e.copy(out=zg[:, dp, :, :].rearrange("p r c -> p (r c)"), in_=bank[zb][:, 0:256]),
                                 reads=[PB[zb]], writes=[bz])
                        n += 1
                    for q4 in range(8):
                        fb = q4 % 2
                        for i in range(4):
                            dp = 4 * q4 + i
                            S.op("pe", lambda e, dp=dp, i=i, fb=fb, zg=zg: e.matmul(bank[fb][:, i * 128:(i + 1) * 128], lhsT=F2[:, 0:128], rhs=zg[:, dp, 0, :], start=True, stop=False),
                                 reads=[bz, BF2], writes=[PB[fb]], signal=False)
                            S.op("pe", lambda e, dp=dp, i=i, fb=fb, zg=zg: e.matmul(bank[fb][:, i * 128:(i + 1) * 128], lhsT=F2[:, 128:256], rhs=zg[:, dp, 1, :], start=False, stop=True),
                                 reads=[bz, BF2], writes=[PB[fb]], signal=(i == 3))
                        dst = fraw[:, 4 * q4:4 * q4 + 4, g * 128:(g + 1) * 128]
                        src = bank[fb].rearrange("p (i c) -> p i c", i=4)
                        if q4 % 2 == 0:
                            S.op("dve", lambda e, dst=dst, src=src: e.tensor_copy(out=dst, in_=src), reads=[PB[fb]], writes=[Bfraw])
                        else:
                            S.op("act", lambda e, dst=dst, src=src: e.copy(out=dst, in_=src), reads=[PB[fb]], writes=[Bfraw])
                if debug:
                    dstg = [T(ca, "dstg%d" % i, [128, 4, 512], F32) for i in range(2)]
                    Bstg = [Buf(), Buf()]
                    for i in range(8):
                        k = i % 2
                        S.op("dve", lambda e, i=i, k=k: e.tensor_copy(out=dstg[k][:], in_=fraw[:, 4 * i:4 * i + 4, :]),
                             reads=[Bfraw], writes=[Bstg[k]])
                        S.dma("pool", dbg["fraw"][:, 4 * i:4 * i + 4, :], dstg[k][:], reads=[Bstg[k]], writes=[out_buf])
                S.barrier()

        BQTs = [Buf() for _ in range(NQ)]
        BSAs = [Buf() for _ in range(NQ)]
        BGAs = [Buf() for _ in range(NQ)]
        BMFs = [Buf() for _ in range(NQ)]
        if "B" in phases:
            S.pe_scale = 1.0
            with contextlib.ExitStack() as cb:
                Wq = T(cb, "Wq", [128, 8, 1024], BF16)
                Wza = T(cb, "Wza", [128, 8, 1024], BF16)
                Wzf = T(cb, "Wzf", [128, 8, 512], BF16)
                Wmg = T(cb, "Wmg", [128, 8, 2048], BF16)
                Wfp = T(cb, "Wfp", [128, 4, 1024], BF16)
                bm = T(cb, "bm", [128, 2048], F32)
                BWq, BWza, BWzf, BWmg, BWfp = [Buf(multi=True) for _ in range(5)]
                Bbm = Buf()
                S.dma("sp", bm[:], bmd, writes=[Bbm])
                stg = [T(cb, "stgB%d" % i, [128, 8, 256], F32) for i in range(2)]
                Bstg = [Buf(), Buf()]
                load_w(stg, Bstg, Wq, BWq, w_in, 0, 1024, True)
                load_w(stg, Bstg, Wza, BWza, w_in, 1536, 1024, True)
                load_w(stg, Bstg, Wzf, BWzf, w_in, 3072, 512, True)
                load_w(stg, Bstg, Wfp, BWfp, w_fp, 0, 1024, False, kh=4)
                load_w(stg, Bstg, Wmg, BWmg, w_mg, 0, 2048, True)
                xp = XPipe(cb, "B", (7,), rb=4)

                def ring(name, shape, dt, n=2):
                    return [T(cb, "%s%d" % (name, i), shape, dt) for i in range(n)], [Buf() for _ in range(n)]
                tabt, Btab = ring("tabB", [128, 256], F32, 4)
                tabg, Btabg = ring("tabg", [128, 256], F32, 1)
                qn, Bqn = ring("qn", [128, 1024], F32, 2)
                ssq, Bssq = ring("ssq", [128, 32], F32, 2)
                t1, Bt1 = ring("t1", [128, 1024], F32, 1)
                t2, Bt2 = ring("t2", [128, 1024], F32, 1)
                qr, Bqr = ring("qr", [128, 8, 128], BF16, 1)
                QTt, BQT = ring("QTt", [128, 1024], BF16, 1)
                sg, Bsg = ring("sg", [128, 512], F32, 2)
                sa, Bsa = ring("sa", [128, 1024], BF16, 2)
                sf, Bsf = ring("sf", [128, 512], F32, 1)
                fg, Bfg = ring("fg", [128, 512], BF16, 2)
                fT, BfT = ring("fT", [128, 4, 128], BF16, 2)
                gt, Bgt = ring("gt", [128, 512], F32, 1)
                ga, Bga = ring("ga", [128, 1024], BF16, 1)
                gf, Bgf = ring("gf", [128, 512], F32, 1)
                mf, Bmf = ring("mf", [128, 1024], F32, 1)

                def proj(i, pb, Wt, BW, c0_):
                    xb, Bxb = xp.XB(i)
                    for dh in range(8):
                        S.op("pe", lambda e, dh=dh: e.matmul(bank[pb], lhsT=xb[:, dh, :], rhs=Wt[:, dh, c0_:c0_ + 512], start=(dh == 0), stop=(dh == 7)),
                             reads=[Bxb, BW], writes=[PB[pb]], signal=(dh == 7))

                def b_s0(i):
                    xp.load(i, 4 * i)
                    S.dma("sp", tabt[i % 4][:], tab[4 * i], writes=[Btab[i % 4]])

                def b_s1(i):
                    xp.prep(i)

                def b_s2(i):
                    rs, Brs = xp.RS(i)
                    k = i % 2
                    proj(i, 0, Wq, BWq, 0)
                    proj(i, 1, Wq, BWq, 512)
                    for c in range(2):
                        S.op("act", lambda e, c=c: e.activation(out=qn[k][:, c * 512:(c + 1) * 512], in_=bank[c], func=AF.Identity, scale=rs[:, 0:1], bias=0.0),
                             reads=[PB[c], Brs], writes=[Bqn[k]])
                    sq_, Bsq_ = ssq[k], Bssq[k]
                    S.op("dve", lambda e: e.memset(sq_[:, 0:8], 0.0), writes=[Bsq_])
                    for h in range(8):
                        S.op("act", lambda e, h=h: e.activation(out=junk[:], in_=qn[k][:, h * 128:(h + 1) * 128], func=AF.Square, scale=1.0, bias=0.0, accum_out=sq_[:, h:h + 1]),
                             reads=[Bqn[k]], writes=[Bjunk, Bsq_])
                    S.op("act", lambda e: e.activation(out=sq_[:, 8:16], in_=sq_[:, 0:8], func=AF.Sqrt, scale=1.0 / 128, bias=EPS), reads=[Bsq_], writes=[Bsq_])
                    S.op("dve", lambda e: e.reciprocal(out=sq_[:, 16:24], in_=sq_[:, 8:16]), reads=[Bsq_], writes=[Bsq_])
                    tg, Btg = tabg[0], Btabg[0]
                    S.op("dve", lambda e: e.tensor_tensor(out=tg[:, 0:128], in0=tabt[i % 4][:, 0:128], in1=gqk[:, 0:128], op=ALU.mult),
                         reads=[Btab[i % 4], Bc], writes=[Btg])
                    S.op("dve", lambda e: e.tensor_tensor(out=tg[:, 128:256], in0=tabt[i % 4][:, 128:256], in1=gqs[:], op=ALU.mult),
                         reads=[Btab[i % 4], Bc], writes=[Btg])
                    rope(qn[k], Bqn[k], 8, tg, Btg, t1[0], t2[0], Bt1[0], Bt2[0], qr[0], Bqr[0], rq=sq_[:, 16:24], Brq=Bsq_)
                    for h in range(8):
                        S.op("pe", lambda e, h=h: e.transpose(bankb[2][:, h * 128:(h + 1) * 128], qr[0][:, h, :], idb[:]),
                             reads=[Bqr[0], Bc], writes=[PB[2]], signal=(h == 7))
                    S.op("dve", lambda e: e.tensor_copy(out=QTt[0][:], in_=bankb[2][:, 0:1024]), reads=[PB[2]], writes=[BQT[0]])
                    S.dma("pool", QTs[i], QTt[0][:], reads=[BQT[0]], writes=[BQTs[i]])

                def b_s3(i):
                    rs, Brs = xp.RS(i)
                    k = i % 2
                    for c in range(2):
                        pb = 3 + c
                        proj(i, pb, Wza, BWza, c * 512)
                        S.op("act", lambda e, pb=pb, c=c: e.activation(out=sg[c][:], in_=bank[pb], func=AF.Sigmoid, scale=rs[:, 0:1], bias=0.0),
                             reads=[PB[pb], Brs], writes=[Bsg[c]])
                        S.op("dve", lambda e, pb=pb, c=c: e.scalar_tensor_tensor(out=sa[k][:, c * 512:(c + 1) * 512], in0=bank[pb], scalar=rs[:, 0:1], in1=sg[c][:], op0=ALU.mult, op1=ALU.mult),
                             reads=[PB[pb], Brs, Bsg[c]], writes=[Bsa[k]])
                    S.dma("pool", SAs[i], sa[k][:], reads=[Bsa[k]], writes=[BSAs[i]])
                    proj(i, 5, Wzf, BWzf, 0)
                    S.op("act", lambda e: e.activation(out=sg[0][:], in_=bank[5], func=AF.Sigmoid, scale=rs[:, 0:1], bias=0.0),
                         reads=[PB[5], Brs], writes=[Bsg[0]])
                    S.op("dve", lambda e: e.scalar_tensor_tensor(out=sf[0][:], in0=bank[5], scalar=rs[:, 0:1], in1=sg[0][:], op0=ALU.mult, op1=ALU.mult),
                         reads=[PB[5], Brs, Bsg[0]], writes=[Bsf[0]])
                    S.op("dve", lambda e: e.tensor_tensor(out=fg[k][:], in0=sf[0][:], in1=fraw[:, i, :], op=ALU.mult),
                         reads=[Bsf[0], Bfraw], writes=[Bfg[k]])
                    for g in range(4):
                        S.op("pe", lambda e, g=g: e.transpose(bankb[6][:, g * 128:(g + 1) * 128], fg[k][:, g * 128:(g + 1) * 128], idb[:]),
                             reads=[Bfg[k], Bc], writes=[PB[6]], signal=(g == 3))
                    S.op("dve", lambda e: e.tensor_copy(out=fT[k][:].rearrange("p g t -> p (g t)"), in_=bankb[6][:, 0:512]), reads=[PB[6]], writes=[BfT[k]])
                    for c in range(2):
                        for g in range(4):
                            S.op("pe", lambda e, g=g, c=c: e.matmul(bank[c], lhsT=fT[k][:, g, :], rhs=Wfp[:, g, c * 512:(c + 1) * 512], start=(g == 0), stop=(g == 3)),
                                 reads=[BfT[k], BWfp], writes=[PB[c]], signal=(g == 3))
                    for c in range(4):
                        pb = 3 + c % 2
                        kk = c % 2
                        proj(i, pb, Wmg, BWmg, c * 512)
                        S.op("dve", lambda e, pb=pb, c=c, kk=kk: e.scalar_tensor_tensor(out=gt[0][:], in0=bank[pb], scalar=rs[:, 0:1], in1=bm[:, c * 512:(c + 1) * 512], op0=ALU.mult, op1=ALU.add),
                             reads=[PB[pb], Brs, Bbm], writes=[Bgt[0]])
                        if c < 2:
                            S.op("act", lambda e, c=c, kk=kk: e.activation(out=ga[0][:, c * 512:(c + 1) * 512], in_=gt[0][:], func=AF.Sigmoid, scale=1.0, bias=0.0),
                                 reads=[Bgt[0]], writes=[Bga[0]])
                        else:
                            S.op("act", lambda e, kk=kk: e.activation(out=gf[0][:], in_=gt[0][:], func=AF.Sigmoid, scale=1.0, bias=0.0),
                                 reads=[Bgt[0]], writes=[Bgf[0]])
                            S.op("dve", lambda e, c=c, kk=kk: e.tensor_tensor(out=mf[0][:, (c - 2) * 512:(c - 1) * 512], in0=bank[c - 2], in1=gf[0][:], op=ALU.mult),
                                 reads=[PB[c - 2], Bgf[0]], writes=[Bmf[0]])
                    S.dma("pool", GAs[i], ga[0][:], reads=[Bga[0]], writes=[BGAs[i]])
                    S.dma("pool", MFs[i], mf[0][:], reads=[Bmf[0]], writes=[BMFs[i]])

                pipeline([b_s0, b_s1, b_s2, b_s3], NQ)
                S.barrier()
        cab.close()
        if "C" in phases:
            S.pe_scale = 1.5
            with contextlib.ExitStack() as cd:
                KT = T(cd, "KT", [128, 2, NT, 128], BF16)
                Vs = T(cd, "Vs", [128, NT, 2, 129], BF16)
                Wap = T(cd, "Wap", [128, 8, 1024], BF16)
                Wout = T(cd, "Wout", [128, 8, 1024], BF16)
                BKT, BV, BWap, BWout = Buf(multi=True), Buf(multi=True), Buf(multi=True), Buf(multi=True)
                with contextlib.ExitStack() as cc:
                    Wkv = T(cc, "Wkv", [128, 8, 512], BF16)
                    BWkv = Buf(multi=True)
                    stg = [T(cc, "stgC%d" % i, [128, 8, 128], F32) for i in range(2)]
                    Bstg = [Buf(), Buf()]
                    load_w(stg, Bstg, Wkv, BWkv, w_in, 1024, 512, True)
                    load_w(stg, Bstg, Wap, BWap, w_ap, 0, 1024, False)
                    load_w(stg, Bstg, Wout, BWout, w_out, 0, 1024, False)
                    xp = XPipe(cc, "C", (6, 7), rb=3)

                    def ringc(name, shape, dt, n=2):
                        return [T(cc, "%s%d" % (name, i), shape, dt) for i in range(n)], [Buf() for _ in range(n)]
                    tabt, Btab = ringc("tabC", [128, 256], F32, 5)
                    kn, Bkn = ringc("kn", [128, 256], F32, 3)
                    ssq, Bssq = ringc("ssqc", [128, 32], F32, 2)
                    t1, Bt1 = ringc("t1c", [128, 256], F32, 1)
                    t2, Bt2 = ringc("t2c", [128, 256], F32, 1)
                    kr, Bkr = ringc("kr", [128, 2, 128], BF16, 1)
                    S.op("pool", lambda e: e.memset(Vs[:, :, :, 128:129], 1.0), writes=[BV])

                    def c_s0(t):
                        xp.load(t, t)
                        S.dma("sp", tabt[t % 5][:], tab[t], writes=[Btab[t % 5]])

                    def c_s1(t):
                        xp.prep(t)

                    def c_s2(t):
                        xb, Bxb = xp.XB(t)
                        rs, Brs = xp.RS(t)
                        kb_ = t % 2
                        for dh in range(8):
                            S.op("pe", lambda e, dh=dh: e.matmul(bank[kb_], lhsT=xb[:, dh, :], rhs=Wkv[:, dh, :], start=(dh == 0), stop=(dh == 7)),
                                 reads=[Bxb, BWkv], writes=[PB[kb_]], signal=(dh == 7))
                        S.op("act", lambda e: e.activation(out=Vs[:, t, :, 0:128], in_=bank[kb_][:, 256:512].rearrange("p (h d) -> p h d", h=2), func=AF.Identity, scale=rs[:, 0:1], bias=0.0),
                             reads=[PB[kb_], Brs], writes=[BV])
                        heads = [(bank[kb_][:, h * 128:(h + 1) * 128], PB[kb_]) for h in range(2)]
                        head_norm(heads, rs, Brs, 128, kn[t % 3], Bkn[t % 3], ssq[t % 2], Bssq[t % 2])

                    def c_s3(t):
                        k = t % 2
                        rope(kn[t % 3], Bkn[t % 3], 2, tabt[t % 5], Btab[t % 5], t1[0], t2[0], Bt1[0], Bt2[0], kr[0], Bkr[0])
                        tb = 2 + t % 2
                        for h in range(2):
                            S.op("pe", lambda e, h=h: e.transpose(bankb[tb][:, h * 128:(h + 1) * 128], kr[0][:, h, :], idb[:]),
                                 reads=[Bkr[0], Bc], writes=[PB[tb]], signal=(h == 1))
                        S.op("act", lambda e: e.copy(out=KT[:, :, t, :], in_=bankb[tb][:, 0:256].rearrange("p (h k) -> p h k", h=2)),
                             reads=[PB[tb]], writes=[BKT])

                    pipeline([c_s0, c_s1, c_s2, c_s3], NT)
                    S.barrier()

                if "D" in phases:
                    S.pe_scale = 1.0
                    with contextlib.ExitStack() as c4:
                        QTt = [T(c4, "QTd%d" % i, [128, 8, 128], BF16) for i in range(2)]
                        sat = T(c4, "sat", [128, 8, 128], BF16)
                        gat = T(c4, "gat", [128, 1024], BF16)
                        mft = T(c4, "mft", [128, 1024], F32)
                        xTt = T(c4, "xTt", [128, 8, 128], F32)
                        pT = [T(c4, "pT%d" % i, [128, 1024], BF16) for i in range(4)]
                        asb = T(c4, "asb", [128, 8, 128], BF16)
                        aT = T(c4, "aT", [128, 8, 128], BF16)
                        tmpf = T(c4, "tmpf", [128, 1024], F32)
                        mg = T(c4, "mg", [128, 1024], BF16)
                        mT = T(c4, "mT", [128, 8, 128], BF16)
                        osb = T(c4, "osb", [128, 8, 128], F32)
                        rinv = T(c4, "rinv", [128, 4], F32)
                        BQd = [Buf(), Buf()]
                        Bsat, Bgat, Bmft, BxT, Basb, BaT, Btmp, Bmg, BmT, Bosb, Brinv = [Buf() for _ in range(11)]
                        BpT = [Buf() for _ in range(4)]
                        obank = [bank[6][:, 0:129], bank[6][:, 256:385], bank[7][:, 0:129], bank[7][:, 256:385]]
                        POB = [PB[6], PB[6], PB[7], PB[7]]
                        sc = float(1.0 / np.sqrt(128.0))
                        S.dma("sp", QTt[0][:].rearrange("p h q -> p (h q)"), QTs[0], reads=[BQTs[0]], writes=[BQd[0]])
                        for dp in range(NQ):
                            sl = dp % 2
                            if dp + 1 < NQ:
                                S.dma("sp", QTt[1 - sl][:].rearrange("p h q -> p (h q)"), QTs[dp + 1], reads=[BQTs[dp + 1]], writes=[BQd[1 - sl]])
                            S.dma("sp", sat[:].rearrange("p h q -> p (h q)"), SAs[dp], reads=[BSAs[dp]], writes=[Bsat])
                            S.dma("sp", gat[:], GAs[dp], reads=[BGAs[dp]], writes=[Bgat])
                            S.dma("sp", mft[:], MFs[dp], reads=[BMFs[dp]], writes=[Bmft])
                            S.dma("sp", xTt[:], xs[4 * dp], writes=[BxT])
                            for kvh in range(2):
                                def qk(j, kvh=kvh, sl=sl):
                                    p2 = j % 2
                                    for i in range(2):
                                        kb = 2 * j + i
                                        S.op("pe", lambda e, i=i, kb=kb: e.matmul(pk[p2][:, i * 512:(i + 1) * 512], lhsT=KT[:, kvh, kb, :],
                                                                                rhs=QTt[sl][:, 4 * kvh:4 * kvh + 4, :].rearrange("p h q -> p (h q)"), start=True, stop=True),
                                             reads=[BKT, BQd[sl]], writes=[PB[2 * p2 + i]], signal=(i == 1))
                                    S.op("act", lambda e: e.activation(out=pT[j % 4][:], in_=pk[p2][:], func=AF.Exp, scale=sc, bias=negB[:, 0:1]),
                                         reads=[PB[2 * p2], PB[2 * p2 + 1], Bc], writes=[BpT[j % 4]])

                                def pv(j, kvh=kvh):
                                    for i in range(2):
                                        kb = 2 * j + i
                                        for hh in range(4):
                                            S.op("pe", lambda e, i=i, kb=kb, hh=hh: e.matmul(obank[hh], lhsT=pT[j % 4][:, i * 512 + hh * 128:i * 512 + hh * 128 + 128],
                                                                                           rhs=Vs[:, kb, kvh, :], start=(kb == 0 and hh % 2 == 0), stop=(kb == NT - 1), skip_group_check=True),
                                                 reads=[BpT[j % 4], BV], writes=[POB[hh]], signal=(i == 1 and hh == 3))
                                qk(0)
                                qk(1)
                                for j in range(NT // 2):
                                    if j + 2 < NT // 2:
                                        qk(j + 2)
                                    pv(j)
                                for hh in range(4):
                                    h = 4 * kvh + hh
                                    S.op("dve", lambda e, hh=hh: e.reciprocal(out=rinv[:, hh:hh + 1], in_=obank[hh][:, 128:129]),
                                         reads=[POB[hh]], writes=[Brinv])
                                    S.op("dve", lambda e, hh=hh, h=h: e.scalar_tensor_tensor(out=asb[:, h, :], in0=obank[hh][:, 0:128], scalar=rinv[:, hh:hh + 1], in1=sat[:, h, :], op0=ALU.mult, op1=ALU.mult),
                                         reads=[POB[hh], Brinv, Bsat], writes=[Basb])
                            S.cur_prio = 1
                            for h in range(8):
                                S.op("pe", lambda e, h=h: e.transpose(bankb[4][:, h * 128:(h + 1) * 128], asb[:, h, :], idb[:]),
                                     reads=[Basb, Bc], writes=[PB[4]], signal=True)
                            S.op("dve", lambda e: e.tensor_copy(out=aT[:].rearrange("p h q -> p (h q)"), in_=bankb[4][:, 0:1024]), reads=[PB[4]], writes=[BaT])
                            for c in range(2):
                                for eh in range(8):
                                    S.op("pe", lambda e, c=c, eh=eh: e.matmul(bank[5], lhsT=aT[:, eh, :], rhs=Wap[:, eh, c * 512:(c + 1) * 512], start=(eh == 0), stop=(eh == 7)),
                                         reads=[BaT, BWap], writes=[PB[5]], signal=True)
                                S.op("dve", lambda e, c=c: e.tensor_tensor(out=tmpf[:, c * 512:(c + 1) * 512], in0=bank[5], in1=gat[:, c * 512:(c + 1) * 512], op=ALU.mult),
                                     reads=[PB[5], Bgat], writes=[Btmp])
                            S.op("dve", lambda e: e.tensor_tensor(out=mg[:], in0=tmpf[:], in1=mft[:], op=ALU.add),
                                 reads=[Btmp, Bmft], writes=[Bmg])
                            for h in range(8):
                                S.op("pe", lambda e, h=h: e.transpose(bankb[4][:, h * 128:(h + 1) * 128], mg[:, h * 128:(h + 1) * 128], idb[:]),
                                     reads=[Bmg, Bc], writes=[PB[4]], signal=True)
                            S.op("dve", lambda e: e.tensor_copy(out=mT[:].rearrange("p h q -> p (h q)"), in_=bankb[4][:, 0:1024]), reads=[PB[4]], writes=[BmT])
                            for half in range(2):
                                for e4 in range(4):
                                    eo = 4 * half + e4
                                    for dh in range(8):
                                        S.op("pe", lambda e, eo=eo, e4=e4, dh=dh: e.matmul(bank[5][:, e4 * 128:(e4 + 1) * 128], lhsT=Wout[:, dh, eo * 128:(eo + 1) * 128], rhs=mT[:, dh, :], start=(dh == 0), stop=(dh == 7)),
                                             reads=[BmT, BWout], writes=[PB[5]], signal=True)
                                S.op("dve", lambda e, half=half: e.tensor_tensor(out=osb[:, 4 * half:4 * half + 4, :].rearrange("p h q -> p (h q)"), in0=bank[5],
                                                                                   in1=xTt[:, 4 * half:4 * half + 4, :].rearrange("p h q -> p (h q)"), op=ALU.add),
                                     reads=[PB[5], BxT], writes=[Bosb])
                            S.dma("pool", outT[dp], osb[:], reads=[Bosb], writes=[out_buf])
                            S.cur_prio = 0

        S.wait_all("sp", [out_buf])
        S.wait_all("pool", [out_buf])
        S.emit()
    return nc


def _alpha(j):
    return np.array([4 * (t // 4) + ((j + t % 4) % 4) for t in range(NT)], dtype=np.int64)


def _consts(j):
    al = _alpha(j)
    d = 4 * np.arange(32, dtype=np.int64) + j
    bt = np.arange(128, dtype=np.int64)
    num = (128 * bt[:, None, None] * d[None, None, :] + al[None, :, None] * d[None, None, :]) % 16384
    ang = 2.0 * np.pi * num.astype(np.float64) / 16384.0
    sc = 1.0 / np.sqrt(128.0)
    ec = np.concatenate([np.cos(ang) * sc, -np.sin(ang) * sc], axis=2).astype(np.float32)
    a2 = 2.0 * np.pi * ((bt[:, None] * bt[None, :]) % 128).astype(np.float64) / 128.0
    Cc, Sc = np.cos(a2) * sc, np.sin(a2) * sc
    wc = np.concatenate([Cc, -Sc, Sc, Cc], axis=1).astype(np.float32)
    a3 = 2.0 * np.pi * ((al[:, None] * bt[None, :]) % 128).astype(np.float64) / 128.0
    f2 = np.concatenate([np.cos(a3) * sc, np.sin(a3) * sc], axis=1).astype(np.float32)
    inv = (np.float32(10000.0) ** (-np.arange(0, 64, 2, dtype=np.float32) / np.float32(64))).astype(np.float32)
    s = al[:, None] + 128 * bt[None, :]
    row = (s // 64).astype(np.float32)
    col = (s % 64).astype(np.float32)
    ar = (row[:, :, None] * inv[None, None, :]).astype(np.float32)
    ac = (col[:, :, None] * inv[None, None, :]).astype(np.float32)
    cr, sr, cc, scn = np.cos(ar), np.sin(ar), np.cos(ac), np.sin(ac)
    tab = np.concatenate([cr, cr, cc, cc, -sr, sr, -scn, scn], axis=2).astype(np.float32)
    return al, ec, wc, f2, np.ascontiguousarray(tab)


def kernel(x, norm_g, w_in, q_norm_g, k_norm_g, w_attn_proj, w_fourier_proj, w_merge, b_merge, w_out, _phases=PHASES, _debug=False):
    x = np.asarray(x, dtype=np.float32)
    f = lambda a: np.ascontiguousarray(np.asarray(a, dtype=np.float32))
    gl = f(np.asarray(norm_g)[0].reshape(8, 128).T)
    gqk = f(np.concatenate([np.broadcast_to(np.asarray(q_norm_g)[0][None, :], (128, 128)),
                            np.broadcast_to(np.asarray(k_norm_g)[0][None, :], (128, 128))], axis=1))
    bm = f(np.broadcast_to(np.asarray(b_merge)[0][None, :], (128, 2048)))
    common = {"idn": np.eye(128, dtype=np.float32), "gl": gl, "gqk": gqk, "bm": bm,
              "w_in": f(np.asarray(w_in)[0]), "w_ap": f(np.asarray(w_attn_proj)[0]), "w_fp": f(np.asarray(w_fourier_proj)[0]),
              "w_mg": f(np.asarray(w_merge)[0]), "w_out": f(np.asarray(w_out)[0])}
    in_maps = []
    for core in range(8):
        b, j = core // 4, core % 4
        al, ec, wc, f2, tab = _consts(j)
        xv = x[b].reshape(128, 128, 8, 128).transpose(1, 3, 2, 0)
        xsv = np.ascontiguousarray(xv[al])
        m = dict(common)
        m.update({"xs": xsv, "tab": tab, "ec": ec, "wc": wc, "f2": f2})
        in_maps.append(m)
    nc = build_nc(_phases, _debug)
    res = run_bass_kernel_spmd(nc, in_maps, core_ids=list(range(8)))
    out = np.empty((B_, S_, D_), dtype=np.float32)
    for core in range(8):
        b, j = core // 4, core % 4
        o = res.results[core]["outT"].transpose(3, 0, 2, 1).reshape(128, NQ, 1024)
        out[b].reshape(128, NQ, 4, 1024)[:, :, j, :] = o
    if _debug:
        return out, res
    return out
```

```python
import contextlib
import numpy as np
import concourse.bass as bass
import concourse.mybir as mybir
from concourse.bass_utils import run_bass_kernel_spmd

F32 = mybir.dt.float32
BF16 = mybir.dt.bfloat16
AF = mybir.ActivationFunctionType
ALU = mybir.AluOpType
AX = mybir.AxisListType

B_, S_, D_ = 2, 16384, 1024
NT = 128
NQ = 32
EPS = 1e-6
PHASES = "ABCD"


class Buf:
    __slots__ = ("w", "r", "multi", "excl")

    def __init__(self, multi=False, excl=False):
        self.w = []
        self.r = []
        self.multi = multi
        self.excl = excl
        _ALL_BUFS.append(self)


_ALL_BUFS = []


class _Rec:
    def __getattr__(self, name):
        def f(*a, **k):
            self.call = (name, a, k)
            return self
        return f


def _free(ap):
    n = 1
    for d in ap.shape[1:]:
        n *= d
    return n


class _Op:
    __slots__ = ("idx", "eng", "calls", "preds", "succs", "dur", "lat", "dma", "pos", "tok", "start", "nready", "indeg", "prio")


class Sched:
    ENG = ("pe", "act", "dve", "pool", "sp")
    XLAT = 250.0

    def __init__(self, nc, ctx, n_dma_sems=24):
        self.nc = nc
        self.streams = {e: [] for e in self.ENG}
        self.sems = {}
        self.cnt = {e: 0 for e in self.ENG}
        self.seen = {e: {} for e in self.ENG}
        for e in self.ENG:
            self.sems[e] = ctx.enter_context(nc.semaphore("s_" + e))
        self.dma_pool = {}
        for q in ("sp", "pool"):
            lst = []
            for i in range(n_dma_sems):
                key = "d_%s_%d" % (q, i)
                self.sems[key] = ctx.enter_context(nc.semaphore(key))
                lst.append(key)
            self.dma_pool[q] = lst
        self.dma_rr = {"sp": 0, "pool": 0}
        self.dma_val = {}
        self.ops = []
        self.pend = {e: None for e in self.ENG}
        self.simlog = {e: [] for e in self.ENG}
        self.cur_prio = 0
        self.pe_scale = 1.0
        del _ALL_BUFS[:]

    def _est(self, eng, call):
        name, a, k = call
        try:
            if eng == "pe":
                if name == "transpose":
                    return 70.0
                return (12.0 + _free(k["rhs"]) / 2.3) * self.pe_scale
            out = k.get("out", a[0] if a else None)
            n = _free(out)
            if eng == "act":
                return 224.0 + 0.833 * n
            if eng == "dve":
                return 70.0 + 1.0 * n
            return 120.0 + 1.8 * n
        except Exception:
            return 300.0

    def _deps(self, op, reads, writes):
        preds = op.preds
        i = op.idx
        for b in reads:
            for p in b.w:
                if p != i:
                    preds[p] = True
            if b.excl:
                for p in b.r:
                    if p != i and p not in preds:
                        preds[p] = False
        for b in writes:
            if not b.multi:
                for p in b.w:
                    if p != i:
                        preds[p] = True
            for p in b.r:
                if p != i and p not in preds:
                    preds[p] = False
        for b in reads:
            if not b.r or b.r[-1] != i:
                b.r.append(i)
        for b in writes:
            if b.multi:
                if not b.w or b.w[-1] != i:
                    b.w.append(i)
            else:
                b.w = [i]
                b.r = []

    def _new(self, eng):
        op = _Op()
        op.idx = len(self.ops)
        op.eng = eng
        op.calls = []
        op.preds = {}
        op.dur = 0.0
        op.lat = 0.0
        op.dma = None
        op.prio = self.cur_prio
        self.ops.append(op)
        return op

    def op(self, eng, fn, reads=(), writes=(), signal=True):
        rec = _Rec()
        fn(rec)
        call = rec.call
        op = self.pend[eng]
        if op is None:
            op = self._new(eng)
            self.pend[eng] = op
        op.calls.append(call)
        op.dur += self._est(eng, call)
        self._deps(op, reads, writes)
        if signal:
            self.pend[eng] = None

    def dma(self, q, out, in_, reads=(), writes=()):
        assert self.pend[q] is None
        op = self._new(q)
        op.dma = (out, in_)
        op.dur = 60.0
        try:
            nbytes = out.shape[0] * _free(out) * (2 if out.dtype == BF16 else 4)
        except Exception:
            nbytes = 1 << 19
        op.lat = 2000.0 + nbytes / 120.0
        self._deps(op, reads, writes)

    def _flush(self):
        import heapq
        ops = self.ops
        if not ops:
            return
        for e in self.ENG:
            assert self.pend[e] is None, "unterminated instruction group on " + e
        n = len(ops)
        for o in ops:
            o.succs = []
            o.nready = 0.0
        for o in ops:
            for p in o.preds:
                ops[p].succs.append(o.idx)
            o.indeg = len(o.preds)
        free = {e: 0.0 for e in self.ENG}
        heap = [(0.0, o.prio, o.idx) for o in ops if o.indeg == 0]
        heapq.heapify(heap)
        order = {e: [] for e in self.ENG}
        done = 0
        while heap:
            key, _pr, i = heapq.heappop(heap)
            o = ops[i]
            st = max(o.nready, free[o.eng])
            if st > key + 1e-9:
                heapq.heappush(heap, (st, o.prio, i))
                continue
            o.start = st
            fin_eng = st + o.dur
            free[o.eng] = fin_eng
            fin = fin_eng + o.lat
            order[o.eng].append(o)
            done += 1
            for sidx in o.succs:
                s2 = ops[sidx]
                lat = 0.0 if (s2.eng == o.eng and o.dma is None) else self.XLAT
                if fin + lat > s2.nready:
                    s2.nready = fin + lat
                s2.indeg -= 1
                if s2.indeg == 0:
                    heapq.heappush(heap, (s2.nready, s2.prio, sidx))
        assert done == n, "dependency cycle"
        for e in self.ENG:
            for o in order[e]:
                if o.dma is None:
                    self.cnt[e] += 1
                    o.pos = self.cnt[e]
                    o.tok = (e, o.pos)
                else:
                    pool = self.dma_pool[e]
                    key = pool[self.dma_rr[e] % len(pool)]
                    self.dma_rr[e] += 1
                    prev = self.dma_val.get(key, 0)
                    self.dma_val[key] = prev + 16
                    o.tok = (key, prev + 16)
                    o.pos = prev
        sems = self.sems
        for e in self.ENG:
            seen = self.seen[e]
            for o in order[e]:
                waits = []
                for p, hard in o.preds.items():
                    po = ops[p]
                    if po.eng == e and po.dma is None and e == "pe":
                        continue
                    k, v = po.tok
                    if seen.get(k, 0) < v:
                        seen[k] = v
                        waits.append((k, v))
                if o.dma is not None:
                    k, v = o.tok
                    if o.pos > 0 and seen.get(k, 0) < o.pos:
                        seen[k] = o.pos
                        waits.append((k, o.pos))
                self.streams[e].append(self._mk(e, o, waits))
                self.simlog[e].append((waits, o.tok[0], 16 if o.dma is not None else 1))
        self.ops = []
        for b in _ALL_BUFS:
            b.w = []
            b.r = []

    def _mk(self, eng, o, waits):
        sems = self.sems
        semh = sems[eng]
        if o.dma is not None:
            out, in_ = o.dma
            key = o.tok[0]

            def emit(e):
                for (k, v) in waits:
                    e.wait_ge(sems[k], v)
                e.dma_start(out=out, in_=in_).then_inc(sems[key], 16)
            return emit
        calls = o.calls

        def emit(e):
            for (k, v) in waits:
                e.wait_ge(sems[k], v)
            ins = None
            for (cname, cargs, ckw) in calls:
                ins = getattr(e, cname)(*cargs, **ckw)
            ins.then_inc(semh, 1)
        return emit

    def barrier(self):
        self._flush()
        targets = [(e, self.cnt[e]) for e in self.ENG if self.cnt[e] > 0]
        targets += list(self.dma_val.items())
        sems = self.sems
        for eng in self.ENG:
            waits = []
            seen = self.seen[eng]
            for key, val in targets:
                if key == eng and eng == "pe":
                    continue
                if seen.get(key, 0) < val:
                    seen[key] = val
                    waits.append((key, val))

            def emit(e, waits=waits):
                for (k, v) in waits:
                    e.wait_ge(sems[k], v)

            self.streams[eng].append(emit)
            self.simlog[eng].append((waits, None, 0))

    def check(self):
        val = {}
        pc = {e: 0 for e in self.ENG}
        prog = True
        while prog:
            prog = False
            for e in self.ENG:
                lg = self.simlog[e]
                while pc[e] < len(lg):
                    waits, k, inc = lg[pc[e]]
                    if any(val.get(wk, 0) < wv for wk, wv in waits):
                        break
                    if k is not None:
                        val[k] = val.get(k, 0) + inc
                    pc[e] += 1
                    prog = True
        stuck = {e: (pc[e], len(self.simlog[e])) for e in self.ENG if pc[e] < len(self.simlog[e])}
        for e, (p, n) in stuck.items():
            waits, k, inc = self.simlog[e][p]
            print("STUCK", e, p, n, [(wk, wv, val.get(wk, 0)) for wk, wv in waits if val.get(wk, 0) < wv])
        return not stuck

    def wait_all(self, eng, bufs):
        pass

    def emit(self):
        self.barrier()
        assert self.check(), "semaphore deadlock in generated program"
        nc = self.nc
        st = self.streams
        with nc.Block() as block:
            @block.tensor
            def _(e):
                for f in st["pe"]:
                    f(e)

            @block.scalar
            def _(e):
                for f in st["act"]:
                    f(e)

            @block.vector
            def _(e):
                for f in st["dve"]:
                    f(e)

            @block.gpsimd
            def _(e):
                for f in st["pool"]:
                    f(e)

            @block.sync
            def _(e):
                for f in st["sp"]:
                    f(e)


def build_nc(phases=PHASES, debug=False):
    nc = bass.Bass("TRN2", target_bir_lowering=False)
    din = lambda name, shape: nc.dram_tensor(name, shape, F32, kind="ExternalInput").ap()
    xs = din("xs", [NT, 128, 8, 128])
    tab = din("tab", [NT, 128, 256])
    ecd = din("ec", [128, NT, 64])
    wcd = din("wc", [128, 512])
    f2d = din("f2", [128, 256])
    idnd = din("idn", [128, 128])
    gld = din("gl", [128, 8])
    gqkd = din("gqk", [128, 256])
    bmd = din("bm", [128, 2048])
    w_in = din("w_in", [1024, 3584])
    w_ap = din("w_ap", [1024, 1024])
    w_fp = din("w_fp", [512, 1024])
    w_mg = din("w_mg", [1024, 2048])
    w_out = din("w_out", [1024, 1024])
    outT = nc.dram_tensor("outT", [NQ, 128, 8, 128], F32, kind="ExternalOutput").ap()
    QTs = nc.dram_tensor("QTs", [NQ, 128, 1024], BF16, kind="Internal").ap()
    SAs = nc.dram_tensor("SAs", [NQ, 128, 1024], BF16, kind="Internal").ap()
    GAs = nc.dram_tensor("GAs", [NQ, 128, 1024], BF16, kind="Internal").ap()
    MFs = nc.dram_tensor("MFs", [NQ, 128, 1024], F32, kind="Internal").ap()
    dbg = {}
    if debug:
        dbg["fraw"] = nc.dram_tensor("dbg_fraw", [128, NQ, 512], F32, kind="ExternalOutput").ap()

    with contextlib.ExitStack() as ctx:
        S = Sched(nc, ctx)

        def T(c, name, shape, dt):
            return c.enter_context(nc.sbuf_tensor("sb_" + name, shape, dt))

        pk = [ctx.enter_context(nc.psum_tensor("pk%d" % i, [128, 1024], F32)) for i in range(4)]
        bank = [pk[i // 2][:, (i % 2) * 512:(i % 2) * 512 + 512] for i in range(8)]
        bankb = [bank[i].bitcast(BF16) for i in range(8)]
        PB = [Buf(excl=True) for _ in range(8)]

        idb = T(ctx, "idb", [128, 128], BF16)
        ones = T(ctx, "ones", [128, 1], BF16)
        gl = T(ctx, "gl", [128, 8], F32)
        gqk = T(ctx, "gqk", [128, 256], F32)
        negB = T(ctx, "negB", [128, 1], F32)
        gqs = T(ctx, "gqs", [128, 128], F32)
        Bgqs = Buf(multi=True)
        sm = T(ctx, "sm", [128, 8], F32)
        Bc = Buf()
        out_buf = Buf(multi=True)

        with contextlib.ExitStack() as c0:
            idf = T(c0, "idf", [128, 128], F32)
            Bi = Buf()
            S.dma("sp", idf[:], idnd, writes=[Bi])
            S.dma("sp", gl[:], gld, writes=[Bc])
            S.dma("sp", gqk[:], gqkd, writes=[Bc])
            S.op("dve", lambda e: e.tensor_copy(out=idb[:], in_=idf[:]), reads=[Bi], writes=[Bc])
            S.op("pool", lambda e: e.memset(ones[:], 1.0), writes=[Bc])
            for blk in range(2):
                for f in range(2):
                    S.op("dve", lambda e, blk=blk, f=f: e.tensor_copy(out=gqs[:, blk * 64 + f * 32:blk * 64 + f * 32 + 32], in_=gqk[:, blk * 64 + (1 - f) * 32:blk * 64 + (1 - f) * 32 + 32]),
                         reads=[Bc], writes=[Bgqs])
            S.op("dve", lambda e: e.tensor_tensor(out=idf[:, 0:128], in0=gqk[:, 0:128], in1=gqk[:, 0:128], op=ALU.mult),
                 reads=[Bc], writes=[Bi])
            S.op("dve", lambda e: e.tensor_reduce(out=sm[:, 0:1], in_=idf[:, 0:128], axis=AX.X, op=ALU.max),
                 reads=[Bi], writes=[Bc])
            S.op("dve", lambda e: e.tensor_tensor(out=idf[:, 0:128], in0=gqk[:, 128:256], in1=gqk[:, 128:256], op=ALU.mult),
                 reads=[Bc], writes=[Bi])
            S.op("dve", lambda e: e.tensor_reduce(out=sm[:, 1:2], in_=idf[:, 0:128], axis=AX.X, op=ALU.max),
                 reads=[Bi], writes=[Bc])
            S.op("dve", lambda e: e.tensor_tensor(out=sm[:, 2:3], in0=sm[:, 0:1], in1=sm[:, 1:2], op=ALU.mult),
                 reads=[Bc], writes=[Bc])
            S.op("act", lambda e: e.activation(out=sm[:, 3:4], in_=sm[:, 2:3], func=AF.Sqrt, scale=128.0, bias=0.0),
                 reads=[Bc], writes=[Bc])
            S.op("dve", lambda e: e.tensor_scalar(out=negB[:], in0=sm[:, 3:4], scalar1=-1.0, scalar2=None, op0=ALU.mult),
                 reads=[Bc], writes=[Bc])
            S.barrier()

        def pipeline(stages, n):
            ns = len(stages)
            for step in range(n + ns - 1):
                for si, f in enumerate(stages):
                    t = step - si
                    if 0 <= t < n:
                        f(t)

        junk = T(ctx, "junk", [128, 128], BF16)
        Bjunk = Buf(multi=True)

        class XPipe:
            def __init__(self, c, tag, ssb=(7,), rb=3):
                self.ssb = ssb
                self.rb = rb
                self.xt = [T(c, "xt%s%d" % (tag, i), [128, 8, 128], F32) for i in range(2)]
                self.xq = [T(c, "xq%s%d" % (tag, i), [128, 8, 128], BF16) for i in range(2)]
                self.xb = [T(c, "xb%s%d" % (tag, i), [128, 8, 128], BF16) for i in range(rb)]
                self.rs = [T(c, "rs%s%d" % (tag, i), [128, 2], F32) for i in range(rb)]
                self.Bxt = [Buf() for _ in range(2)]
                self.Bxq = [Buf() for _ in range(2)]
                self.Bxb = [Buf() for _ in range(rb)]
                self.Brs = [Buf() for _ in range(rb)]

            def load(self, i, t):
                S.dma("sp", self.xt[i % 2][:], xs[t], writes=[self.Bxt[i % 2]])

            def XB(self, i):
                return self.xb[i % self.rb], self.Bxb[i % self.rb]

            def RS(self, i):
                return self.rs[i % self.rb], self.Brs[i % self.rb]

            def prep(self, i):
                xt, xq, Bxt, Bxq = self.xt[i % 2], self.xq[i % 2], self.Bxt[i % 2], self.Bxq[i % 2]
                xb, Bxb = self.XB(i)
                rs, Brs = self.RS(i)
                sb_ = self.ssb[i % len(self.ssb)]
                S.op("dve", lambda e: e.tensor_copy(out=xb[:], in_=xt[:]), reads=[Bxt], writes=[Bxb])
                S.op("act", lambda e: e.activation(out=xq[:], in_=xt[:], func=AF.Square, scale=1.0, bias=0.0), reads=[Bxt], writes=[Bxq])
                for dh in range(8):
                    S.op("pe", lambda e, dh=dh: e.matmul(bank[sb_][:, 0:1], lhsT=xq[:, dh, :], rhs=ones[:], start=(dh == 0), stop=(dh == 7)),
                         reads=[Bxq, Bc], writes=[PB[sb_]], signal=(dh == 7))
                S.op("act", lambda e: e.activation(out=rs[:, 1:2], in_=bank[sb_][:, 0:1], func=AF.Sqrt, scale=1.0 / D_, bias=EPS),
                     reads=[PB[sb_]], writes=[Brs])
                S.op("dve", lambda e: e.reciprocal(out=rs[:, 0:1], in_=rs[:, 1:2]), reads=[Brs], writes=[Brs])

        def head_norm(heads, rs, Brs, gcol, dst, Bdst, ssq, Bss):
            H = len(heads)
            S.op("dve", lambda e: e.memset(ssq[:, 0:8], 0.0), writes=[Bss])
            for h, (pa, pbuf) in enumerate(heads):
                S.op("act", lambda e, h=h, pa=pa: e.activation(out=junk[:], in_=pa, func=AF.Square, scale=rs[:, 0:1], bias=0.0, accum_out=ssq[:, h:h + 1]),
                     reads=[pbuf, Brs], writes=[Bjunk, Bss])
            S.op("act", lambda e: e.activation(out=ssq[:, 8:8 + H], in_=ssq[:, 0:H], func=AF.Sqrt, scale=1.0 / 128, bias=EPS),
                 reads=[Bss], writes=[Bss])
            S.op("dve", lambda e: e.reciprocal(out=ssq[:, 16:16 + H], in_=ssq[:, 8:8 + H]), reads=[Bss], writes=[Bss])
            S.op("dve", lambda e: e.tensor_scalar(out=ssq[:, 24:24 + H], in0=ssq[:, 16:16 + H], scalar1=rs[:, 0:1], scalar2=None, op0=ALU.mult),
                 reads=[Bss, Brs], writes=[Bss])
            for h, (pa, pbuf) in enumerate(heads):
                S.op("dve", lambda e, h=h, pa=pa: e.scalar_tensor_tensor(out=dst[:, h * 128:(h + 1) * 128], in0=pa, scalar=ssq[:, 24 + h:25 + h],
                                                                        in1=gqk[:, gcol:gcol + 128], op0=ALU.mult, op1=ALU.mult),
                     reads=[pbuf, Bss, Bc], writes=[Bdst])

        def rope(src, Bsrc, H, tabt, Btab, t1, t2, Bt1, Bt2, dst, Bdst, rq=None, Brq=None):
            W = H * 128
            s3 = src[:, 0:W].rearrange("p (h d) -> p h d", h=H)
            cos_b = tabt[:, 0:128].unsqueeze(1).broadcast_to([128, H, 128])
            S.op("dve", lambda e: e.tensor_tensor(out=t1[:, 0:W].rearrange("p (h d) -> p h d", h=H), in0=s3, in1=cos_b, op=ALU.mult),
                 reads=[Bsrc, Btab], writes=[Bt1])
            s5 = src[:, 0:W].rearrange("p (h b f w) -> p h b f w", h=H, b=2, f=2)
            t5 = t2[:, 0:W].rearrange("p (h b f w) -> p h b f w", h=H, b=2, f=2)
            sn4 = tabt[:, 128:256].rearrange("p (b f w) -> p b f w", b=2, f=2)
            for f in range(2):
                sin_b = sn4[:, :, f, :].unsqueeze(1).broadcast_to([128, H, 2, 32])
                S.op("dve", lambda e, f=f, sin_b=sin_b: e.tensor_tensor(out=t5[:, :, :, f, :], in0=s5[:, :, :, 1 - f, :], in1=sin_b, op=ALU.mult),
                     reads=[Bsrc, Btab], writes=[Bt2])
            if rq is None:
                S.op("dve", lambda e: e.tensor_tensor(out=dst[:].rearrange("p h d -> p (h d)"), in0=t1[:, 0:W], in1=t2[:, 0:W], op=ALU.add),
                     reads=[Bt1, Bt2], writes=[Bdst])
            else:
                S.op("dve", lambda e: e.tensor_tensor(out=t1[:, 0:W], in0=t1[:, 0:W], in1=t2[:, 0:W], op=ALU.add),
                     reads=[Bt1, Bt2], writes=[Bt1])
                S.op("dve", lambda e: e.tensor_tensor(out=dst[:], in0=t1[:, 0:W].rearrange("p (h d) -> p h d", h=H), in1=rq.unsqueeze(2).broadcast_to([128, H, 128]), op=ALU.mult),
                     reads=[Bt1, Brq], writes=[Bdst])

        def load_w(stg, Bstg, dst, Bdst, wd, col0, ncols, fold, kh=8, state=[0]):
            cw = stg[0].shape[2]
            for c0_ in range(0, ncols, cw):
                w = min(cw, ncols - c0_)
                i = state[0] % 2
                state[0] += 1
                src = wd[:, col0 + c0_:col0 + c0_ + w].rearrange("(dh dl) e -> dl dh e", dl=128)
                S.dma("sp", stg[i][:, 0:kh, 0:w], src, writes=[Bstg[i]])
                if fold:
                    for dh in range(kh):
                        S.op("dve", lambda e, dh=dh, i=i, c0_=c0_, w=w: e.tensor_scalar(
                            out=dst[:, dh, c0_:c0_ + w], in0=stg[i][:, dh, 0:w], scalar1=gl[:, dh:dh + 1], scalar2=None, op0=ALU.mult),
                            reads=[Bstg[i], Bc], writes=[Bdst])
                else:
                    S.op("dve", lambda e, i=i, c0_=c0_, w=w: e.tensor_copy(out=dst[:, :, c0_:c0_ + w], in_=stg[i][:, 0:kh, 0:w]),
                         reads=[Bstg[i]], writes=[Bdst])

        cab = contextlib.ExitStack()
        fraw = T(cab, "fraw", [128, NQ, 512], BF16)
        Bfraw = Buf(multi=True)

        if "A" in phases:
            S.pe_scale = 1.5
            with contextlib.ExitStack() as ca:
                Wu = T(ca, "Wu", [128, 8, 512], BF16)
                Ec = T(ca, "Ec", [128, NT, 64], BF16)
                Wc = T(ca, "Wc", [128, 512], BF16)
                F2 = T(ca, "F2", [128, 256], BF16)
                BWu, BEc, BWc, BF2, BY = Buf(multi=True), Buf(multi=True), Buf(), Buf(), Buf(multi=True)
                stg = [T(ca, "stgA%d" % i, [128, 8, 256], F32) for i in range(2)]
                Bstg = [Buf(), Buf()]
                load_w(stg, Bstg, Wu, BWu, w_in, 2560, 512, True)
                for i in range(4):
                    sf = stg[i % 2][:].rearrange("p a b -> p (a b)")
                    S.dma("sp", sf, ecd[:, 32 * i:32 * i + 32, :].rearrange("p a b -> p (a b)"), writes=[Bstg[i % 2]])
                    S.op("dve", lambda e, i=i, sf=sf: e.tensor_copy(out=Ec[:, 32 * i:32 * i + 32, :].rearrange("p a b -> p (a b)"), in_=sf),
                         reads=[Bstg[i % 2]], writes=[BEc])
                S.dma("sp", stg[0][:, 0:2, :].rearrange("p a b -> p (a b)"), wcd, writes=[Bstg[0]])
                S.op("dve", lambda e: e.tensor_copy(out=Wc[:], in_=stg[0][:, 0:2, :].rearrange("p a b -> p (a b)")), reads=[Bstg[0]], writes=[BWc])
                S.dma("sp", stg[1][:, 0, 0:256], f2d, writes=[Bstg[1]])
                S.op("dve", lambda e: e.tensor_copy(out=F2[:], in_=stg[1][:, 0, 0:256]), reads=[Bstg[1]], writes=[BF2])
                Ysb = T(ca, "Ysb", [128, NT, 4, 2, 32], BF16)
                Zg = [T(ca, "Zg%d" % i, [128, 32, 2, 128], BF16) for i in range(2)]
                BZ = [Buf(multi=True), Buf(multi=True)]
                usb = [T(ca, "usb%d" % i, [128, 512], BF16) for i in range(3)]
                Bus = [Buf() for _ in range(3)]
                xp = XPipe(ca, "A", (6, 7), rb=3)

                def a_s0(t):
                    xp.load(t, t)

                def a_s1(t):
                    xp.prep(t)

                def a_s2(t):
                    xb, Bxb = xp.XB(t)
                    rs, Brs = xp.RS(t)
                    ub = t % 2
                    for dh in range(8):
                        S.op("pe", lambda e, dh=dh: e.matmul(bank[ub], lhsT=xb[:, dh, :], rhs=Wu[:, dh, :], start=(dh == 0), stop=(dh == 7)),
                             reads=[Bxb, BWu], writes=[PB[ub]], signal=(dh == 7))
                    S.op("dve", lambda e: e.tensor_scalar(out=usb[t % 3][:], in0=bank[ub], scalar1=rs[:, 0:1], scalar2=None, op0=ALU.mult),
                         reads=[PB[ub], Brs], writes=[Bus[t % 3]])

                def a_s3(t):
                    yb = 2 + t % 2
                    for g in range(4):
                        S.op("pe", lambda e, g=g: e.matmul(bank[yb][:, g * 64:(g + 1) * 64], lhsT=usb[t % 3][:, g * 128:(g + 1) * 128], rhs=Ec[:, t, :], start=True, stop=True),
                             reads=[Bus[t % 3], BEc], writes=[PB[yb]], signal=(g == 3))
                    S.op("act", lambda e: e.copy(out=Ysb[:, t, :, :, :].rearrange("p g r d -> p (g r d)"), in_=bank[yb][:, 0:256]),
                         reads=[PB[yb]], writes=[BY])

                pipeline([a_s0, a_s1, a_s2, a_s3], NT)
                n = 0
                for g in range(4):
                    zg, bz = Zg[g % 2], BZ[g % 2]
                    for dp in range(32):
                        zb = 4 + n % 2
                        S.op("pe", lambda e, g=g, dp=dp, zb=zb: e.matmul(bank[zb][:, 0:256], lhsT=Ysb[:, :, g, 0, dp], rhs=Wc[:, 0:256], start=True, stop=False),
                             reads=[BY, BWc], writes=[PB[zb]], signal=False)
                        S.op("pe", lambda e, g=g, dp=dp, zb=zb: e.matmul(bank[zb][:, 0:256], lhsT=Ysb[:, :, g, 1, dp], rhs=Wc[:, 256:512], start=False, stop=True),
                             reads=[BY, BWc], writes=[PB[zb]])
                        if n % 2 == 0:
                            S.op("dve", lambda e, dp=dp, zb=zb, zg=zg: e.tensor_copy(out=zg[:, dp, :, :].rearrange("p r c -> p (r c)"), in_=bank[zb][:, 0:256]),
                                 reads=[PB[zb]], writes=[bz])
                        else:
                            S.op("act", lambda e, dp=dp, zb=zb, zg=zg: e.copy(out=zg[:, dp, :, :].rearrange("p r c -> p (r c)"), in_=bank[zb][:, 0:256]),
                                 reads=[PB[zb]], writes=[bz])
                        n += 1
                    for q4 in range(8):
                        fb = q4 % 2
                        for i in range(4):
                            dp = 4 * q4 + i
                            S.op("pe", lambda e, dp=dp, i=i, fb=fb, zg=zg: e.matmul(bank[fb][:, i * 128:(i + 1) * 128], lhsT=F2[:, 0:128], rhs=zg[:, dp, 0, :], start=True, stop=False),
                                 reads=[bz, BF2], writes=[PB[fb]], signal=False)
                            S.op("pe", lambda e, dp=dp, i=i, fb=fb, zg=zg: e.matmul(bank[fb][:, i * 128:(i + 1) * 128], lhsT=F2[:, 128:256], rhs=zg[:, dp, 1, :], start=False, stop=True),
                                 reads=[bz, BF2], writes=[PB[fb]], signal=(i == 3))
                        dst = fraw[:, 4 * q4:4 * q4 + 4, g * 128:(g + 1) * 128]
                        src = bank[fb].rearrange("p (i c) -> p i c", i=4)
                        if q4 % 2 == 0:
                            S.op("dve", lambda e, dst=dst, src=src: e.tensor_copy(out=dst, in_=src), reads=[PB[fb]], writes=[Bfraw])
                        else:
                            S.op("act", lambda e, dst=dst, src=src: e.copy(out=dst, in_=src), reads=[PB[fb]], writes=[Bfraw])
                if debug:
                    dstg = [T(ca, "dstg%d" % i, [128, 4, 512], F32) for i in range(2)]
                    Bstg = [Buf(), Buf()]
                    for i in range(8):
                        k = i % 2
                        S.op("dve", lambda e, i=i, k=k: e.tensor_copy(out=dstg[k][:], in_=fraw[:, 4 * i:4 * i + 4, :]),
                             reads=[Bfraw], writes=[Bstg[k]])
                        S.dma("pool", dbg["fraw"][:, 4 * i:4 * i + 4, :], dstg[k][:], reads=[Bstg[k]], writes=[out_buf])
                S.barrier()

        BQTs = [Buf() for _ in range(NQ)]
        BSAs = [Buf() for _ in range(NQ)]
        BGAs = [Buf() for _ in range(NQ)]
        BMFs = [Buf() for _ in range(NQ)]
        if "B" in phases:
            S.pe_scale = 1.0
            with contextlib.ExitStack() as cb:
                Wq = T(cb, "Wq", [128, 8, 1024], BF16)
                Wza = T(cb, "Wza", [128, 8, 1024], BF16)
                Wzf = T(cb, "Wzf", [128, 8, 512], BF16)
                Wmg = T(cb, "Wmg", [128, 8, 2048], BF16)
                Wfp = T(cb, "Wfp", [128, 4, 1024], BF16)
                bm = T(cb, "bm", [128, 2048], F32)
                BWq, BWza, BWzf, BWmg, BWfp = [Buf(multi=True) for _ in range(5)]
                Bbm = Buf()
                S.dma("sp", bm[:], bmd, writes=[Bbm])
                stg = [T(cb, "stgB%d" % i, [128, 8, 256], F32) for i in range(2)]
                Bstg = [Buf(), Buf()]
                load_w(stg, Bstg, Wq, BWq, w_in, 0, 1024, True)
                load_w(stg, Bstg, Wza, BWza, w_in, 1536, 1024, True)
                load_w(stg, Bstg, Wzf, BWzf, w_in, 3072, 512, True)
                load_w(stg, Bstg, Wfp, BWfp, w_fp, 0, 1024, False, kh=4)
                load_w(stg, Bstg, Wmg, BWmg, w_mg, 0, 2048, True)
                xp = XPipe(cb, "B", (7,), rb=4)

                def ring(name, shape, dt, n=2):
                    return [T(cb, "%s%d" % (name, i), shape, dt) for i in range(n)], [Buf() for _ in range(n)]
                tabt, Btab = ring("tabB", [128, 256], F32, 4)
                tabg, Btabg = ring("tabg", [128, 256], F32, 1)
                qn, Bqn = ring("qn", [128, 1024], F32, 2)
                ssq, Bssq = ring("ssq", [128, 32], F32, 2)
                t1, Bt1 = ring("t1", [128, 1024], F32, 1)
                t2, Bt2 = ring("t2", [128, 1024], F32, 1)
                qr, Bqr = ring("qr", [128, 8, 128], BF16, 1)
                QTt, BQT = ring("QTt", [128, 1024], BF16, 1)
                sg, Bsg = ring("sg", [128, 512], F32, 2)
                sa, Bsa = ring("sa", [128, 1024], BF16, 2)
                sf, Bsf = ring("sf", [128, 512], F32, 1)
                fg, Bfg = ring("fg", [128, 512], BF16, 2)
                fT, BfT = ring("fT", [128, 4, 128], BF16, 2)
                gt, Bgt = ring("gt", [128, 512], F32, 1)
                ga, Bga = ring("ga", [128, 1024], BF16, 1)
                gf, Bgf = ring("gf", [128, 512], F32, 1)
                mf, Bmf = ring("mf", [128, 1024], F32, 1)

                def proj(i, pb, Wt, BW, c0_):
                    xb, Bxb = xp.XB(i)
                    for dh in range(8):
                        S.op("pe", lambda e, dh=dh: e.matmul(bank[pb], lhsT=xb[:, dh, :], rhs=Wt[:, dh, c0_:c0_ + 512], start=(dh == 0), stop=(dh == 7)),
                             reads=[Bxb, BW], writes=[PB[pb]], signal=(dh == 7))

                def b_s0(i):
                    xp.load(i, 4 * i)
                    S.dma("sp", tabt[i % 4][:], tab[4 * i], writes=[Btab[i % 4]])

                def b_s1(i):
                    xp.prep(i)

                def b_s2(i):
                    rs, Brs = xp.RS(i)
                    k = i % 2
                    proj(i, 0, Wq, BWq, 0)
                    proj(i, 1, Wq, BWq, 512)
                    for c in range(2):
                        S.op("act", lambda e, c=c: e.activation(out=qn[k][:, c * 512:(c + 1) * 512], in_=bank[c], func=AF.Identity, scale=rs[:, 0:1], bias=0.0),
                             reads=[PB[c], Brs], writes=[Bqn[k]])
                    sq_, Bsq_ = ssq[k], Bssq[k]
                    S.op("dve", lambda e: e.memset(sq_[:, 0:8], 0.0), writes=[Bsq_])
                    for h in range(8):
                        S.op("act", lambda e, h=h: e.activation(out=junk[:], in_=qn[k][:, h * 128:(h + 1) * 128], func=AF.Square, scale=1.0, bias=0.0, accum_out=sq_[:, h:h + 1]),
                             reads=[Bqn[k]], writes=[Bjunk, Bsq_])
                    S.op("act", lambda e: e.activation(out=sq_[:, 8:16], in_=sq_[:, 0:8], func=AF.Sqrt, scale=1.0 / 128, bias=EPS), reads=[Bsq_], writes=[Bsq_])
                    S.op("dve", lambda e: e.reciprocal(out=sq_[:, 16:24], in_=sq_[:, 8:16]), reads=[Bsq_], writes=[Bsq_])
                    tg, Btg = tabg[0], Btabg[0]
                    S.op("dve", lambda e: e.tensor_tensor(out=tg[:, 0:128], in0=tabt[i % 4][:, 0:128], in1=gqk[:, 0:128], op=ALU.mult),
                         reads=[Btab[i % 4], Bc], writes=[Btg])
                    S.op("dve", lambda e: e.tensor_tensor(out=tg[:, 128:256], in0=tabt[i % 4][:, 128:256], in1=gqs[:], op=ALU.mult),
                         reads=[Btab[i % 4], Bc], writes=[Btg])
                    rope(qn[k], Bqn[k], 8, tg, Btg, t1[0], t2[0], Bt1[0], Bt2[0], qr[0], Bqr[0], rq=sq_[:, 16:24], Brq=Bsq_)
                    for h in range(8):
                        S.op("pe", lambda e, h=h: e.transpose(bankb[2][:, h * 128:(h + 1) * 128], qr[0][:, h, :], idb[:]),
                             reads=[Bqr[0], Bc], writes=[PB[2]], signal=(h == 7))
                    S.op("dve", lambda e: e.tensor_copy(out=QTt[0][:], in_=bankb[2][:, 0:1024]), reads=[PB[2]], writes=[BQT[0]])
                    S.dma("pool", QTs[i], QTt[0][:], reads=[BQT[0]], writes=[BQTs[i]])

                def b_s3(i):
                    rs, Brs = xp.RS(i)
                    k = i % 2
                    for c in range(2):
                        pb = 3 + c
                        proj(i, pb, Wza, BWza, c * 512)
                        S.op("act", lambda e, pb=pb, c=c: e.activation(out=sg[c][:], in_=bank[pb], func=AF.Sigmoid, scale=rs[:, 0:1], bias=0.0),
                             reads=[PB[pb], Brs], writes=[Bsg[c]])
                        S.op("dve", lambda e, pb=pb, c=c: e.scalar_tensor_tensor(out=sa[k][:, c * 512:(c + 1) * 512], in0=bank[pb], scalar=rs[:, 0:1], in1=sg[c][:], op0=ALU.mult, op1=ALU.mult),
                             reads=[PB[pb], Brs, Bsg[c]], writes=[Bsa[k]])
                    S.dma("pool", SAs[i], sa[k][:], reads=[Bsa[k]], writes=[BSAs[i]])
                    proj(i, 5, Wzf, BWzf, 0)
                    S.op("act", lambda e: e.activation(out=sg[0][:], in_=bank[5], func=AF.Sigmoid, scale=rs[:, 0:1], bias=0.0),
                         reads=[PB[5], Brs], writes=[Bsg[0]])
                    S.op("dve", lambda e: e.scalar_tensor_tensor(out=sf[0][:], in0=bank[5], scalar=rs[:, 0:1], in1=sg[0][:], op0=ALU.mult, op1=ALU.mult),
                         reads=[PB[5], Brs, Bsg[0]], writes=[Bsf[0]])
                    S.op("dve", lambda e: e.tensor_tensor(out=fg[k][:], in0=sf[0][:], in1=fraw[:, i, :], op=ALU.mult),
                         reads=[Bsf[0], Bfraw], writes=[Bfg[k]])
                    for g in range(4):
                        S.op("pe", lambda e, g=g: e.transpose(bankb[6][:, g * 128:(g + 1) * 128], fg[k][:, g * 128:(g + 1) * 128], idb[:]),
                             reads=[Bfg[k], Bc], writes=[PB[6]], signal=(g == 3))
                    S.op("dve", lambda e: e.tensor_copy(out=fT[k][:].rearrange("p g t -> p (g t)"), in_=bankb[6][:, 0:512]), reads=[PB[6]], writes=[BfT[k]])
                    for c in range(2):
                        for g in range(4):
                            S.op("pe", lambda e, g=g, c=c: e.matmul(bank[c], lhsT=fT[k][:, g, :], rhs=Wfp[:, g, c * 512:(c + 1) * 512], start=(g == 0), stop=(g == 3)),
                                 reads=[BfT[k], BWfp], writes=[PB[c]], signal=(g == 3))
                    for c in range(4):
                        pb = 3 + c % 2
                        kk = c % 2
                        proj(i, pb, Wmg, BWmg, c * 512)
                        S.op("dve", lambda e, pb=pb, c=c, kk=kk: e.scalar_tensor_tensor(out=gt[0][:], in0=bank[pb], scalar=rs[:, 0:1], in1=bm[:, c * 512:(c + 1) * 512], op0=ALU.mult, op1=ALU.add),
                             reads=[PB[pb], Brs, Bbm], writes=[Bgt[0]])
                        if c < 2:
                            S.op("act", lambda e, c=c, kk=kk: e.activation(out=ga[0][:, c * 512:(c + 1) * 512], in_=gt[0][:], func=AF.Sigmoid, scale=1.0, bias=0.0),
                                 reads=[Bgt[0]], writes=[Bga[0]])
                        else:
                            S.op("act", lambda e, kk=kk: e.activation(out=gf[0][:], in_=gt[0][:], func=AF.Sigmoid, scale=1.0, bias=0.0),
                                 reads=[Bgt[0]], writes=[Bgf[0]])
                            S.op("dve", lambda e, c=c, kk=kk: e.tensor_tensor(out=mf[0][:, (c - 2) * 512:(c - 1) * 512], in0=bank[c - 2], in1=gf[0][:], op=ALU.mult),
                                 reads=[PB[c - 2], Bgf[0]], writes=[Bmf[0]])
                    S.dma("pool", GAs[i], ga[0][:], reads=[Bga[0]], writes=[BGAs[i]])
                    S.dma("pool", MFs[i], mf[0][:], reads=[Bmf[0]], writes=[BMFs[i]])

                pipeline([b_s0, b_s1, b_s2, b_s3], NQ)
                S.barrier()
        cab.close()
        if "C" in phases:
            S.pe_scale = 1.5
            with contextlib.ExitStack() as cd:
                KT = T(cd, "KT", [128, 2, NT, 128], BF16)
                Vs = T(cd, "Vs", [128, NT, 2, 129], BF16)
                Wap = T(cd, "Wap", [128, 8, 1024], BF16)
                Wout = T(cd, "Wout", [128, 8, 1024], BF16)
                BKT, BV, BWap, BWout = Buf(multi=True), Buf(multi=True), Buf(multi=True), Buf(multi=True)
                with contextlib.ExitStack() as cc:
                    Wkv = T(cc, "Wkv", [128, 8, 512], BF16)
                    BWkv = Buf(multi=True)
                    stg = [T(cc, "stgC%d" % i, [128, 8, 128], F32) for i in range(2)]
                    Bstg = [Buf(), Buf()]
                    load_w(stg, Bstg, Wkv, BWkv, w_in, 1024, 512, True)
                    load_w(stg, Bstg, Wap, BWap, w_ap, 0, 1024, False)
                    load_w(stg, Bstg, Wout, BWout, w_out, 0, 1024, False)
                    xp = XPipe(cc, "C", (6, 7), rb=3)

                    def ringc(name, shape, dt, n=2):
                        return [T(cc, "%s%d" % (name, i), shape, dt) for i in range(n)], [Buf() for _ in range(n)]
                    tabt, Btab = ringc("tabC", [128, 256], F32, 5)
                    kn, Bkn = ringc("kn", [128, 256], F32, 3)
                    ssq, Bssq = ringc("ssqc", [128, 32], F32, 2)
                    t1, Bt1 = ringc("t1c", [128, 256], F32, 1)
                    t2, Bt2 = ringc("t2c", [128, 256], F32, 1)
                    kr, Bkr = ringc("kr", [128, 2, 128], BF16, 1)
                    S.op("pool", lambda e: e.memset(Vs[:, :, :, 128:129], 1.0), writes=[BV])

                    def c_s0(t):
                        xp.load(t, t)
                        S.dma("sp", tabt[t % 5][:], tab[t], writes=[Btab[t % 5]])

                    def c_s1(t):
                        xp.prep(t)

                    def c_s2(t):
                        xb, Bxb = xp.XB(t)
                        rs, Brs = xp.RS(t)
                        kb_ = t % 2
                        for dh in range(8):
                            S.op("pe", lambda e, dh=dh: e.matmul(bank[kb_], lhsT=xb[:, dh, :], rhs=Wkv[:, dh, :], start=(dh == 0), stop=(dh == 7)),
                                 reads=[Bxb, BWkv], writes=[PB[kb_]], signal=(dh == 7))
                        S.op("act", lambda e: e.activation(out=Vs[:, t, :, 0:128], in_=bank[kb_][:, 256:512].rearrange("p (h d) -> p h d", h=2), func=AF.Identity, scale=rs[:, 0:1], bias=0.0),
                             reads=[PB[kb_], Brs], writes=[BV])
                        heads = [(bank[kb_][:, h * 128:(h + 1) * 128], PB[kb_]) for h in range(2)]
                        head_norm(heads, rs, Brs, 128, kn[t % 3], Bkn[t % 3], ssq[t % 2], Bssq[t % 2])

                    def c_s3(t):
                        k = t % 2
                        rope(kn[t % 3], Bkn[t % 3], 2, tabt[t % 5], Btab[t % 5], t1[0], t2[0], Bt1[0], Bt2[0], kr[0], Bkr[0])
                        tb = 2 + t % 2
                        for h in range(2):
                            S.op("pe", lambda e, h=h: e.transpose(bankb[tb][:, h * 128:(h + 1) * 128], kr[0][:, h, :], idb[:]),
                                 reads=[Bkr[0], Bc], writes=[PB[tb]], signal=(h == 1))
                        S.op("act", lambda e: e.copy(out=KT[:, :, t, :], in_=bankb[tb][:, 0:256].rearrange("p (h k) -> p h k", h=2)),
                             reads=[PB[tb]], writes=[BKT])

                    pipeline([c_s0, c_s1, c_s2, c_s3], NT)
                    S.barrier()

                if "D" in phases:
                    S.pe_scale = 1.0
                    with contextlib.ExitStack() as c4:
                        QTt = [T(c4, "QTd%d" % i, [128, 8, 128], BF16) for i in range(2)]
                        sat = T(c4, "sat", [128, 8, 128], BF16)
                        gat = T(c4, "gat", [128, 1024], BF16)
                        mft = T(c4, "mft", [128, 1024], F32)
                        xTt = T(c4, "xTt", [128, 8, 128], F32)
                        pT = [T(c4, "pT%d" % i, [128, 1024], BF16) for i in range(4)]
                        asb = T(c4, "asb", [128, 8, 128], BF16)
                        aT = T(c4, "aT", [128, 8, 128], BF16)
                        tmpf = T(c4, "tmpf", [128, 1024], F32)
                        mg = T(c4, "mg", [128, 1024], BF16)
                        mT = T(c4, "mT", [128, 8, 128], BF16)
                        osb = T(c4, "osb", [128, 8, 128], F32)
                        rinv = T(c4, "rinv", [128, 4], F32)
                        BQd = [Buf(), Buf()]
                        Bsat, Bgat, Bmft, BxT, Basb, BaT, Btmp, Bmg, BmT, Bosb, Brinv = [Buf() for _ in range(11)]
                        BpT = [Buf() for _ in range(4)]
                        obank = [bank[6][:, 0:129], bank[6][:, 256:385], bank[7][:, 0:129], bank[7][:, 256:385]]
                        POB = [PB[6], PB[6], PB[7], PB[7]]
                        sc = float(1.0 / np.sqrt(128.0))
                        S.dma("sp", QTt[0][:].rearrange("p h q -> p (h q)"), QTs[0], reads=[BQTs[0]], writes=[BQd[0]])
                        for dp in range(NQ):
                            sl = dp % 2
                            if dp + 1 < NQ:
                                S.dma("sp", QTt[1 - sl][:].rearrange("p h q -> p (h q)"), QTs[dp + 1], reads=[BQTs[dp + 1]], writes=[BQd[1 - sl]])
                            S.dma("sp", sat[:].rearrange("p h q -> p (h q)"), SAs[dp], reads=[BSAs[dp]], writes=[Bsat])
                            S.dma("sp", gat[:], GAs[dp], reads=[BGAs[dp]], writes=[Bgat])
                            S.dma("sp", mft[:], MFs[dp], reads=[BMFs[dp]], writes=[Bmft])
                            S.dma("sp", xTt[:], xs[4 * dp], writes=[BxT])
                            for kvh in range(2):
                                def qk(j, kvh=kvh, sl=sl):
                                    p2 = j % 2
                                    for i in range(2):
                                        kb = 2 * j + i
                                        S.op("pe", lambda e, i=i, kb=kb: e.matmul(pk[p2][:, i * 512:(i + 1) * 512], lhsT=KT[:, kvh, kb, :],
                                                                                rhs=QTt[sl][:, 4 * kvh:4 * kvh + 4, :].rearrange("p h q -> p (h q)"), start=True, stop=True),
                                             reads=[BKT, BQd[sl]], writes=[PB[2 * p2 + i]], signal=(i == 1))
                                    S.op("act", lambda e: e.activation(out=pT[j % 4][:], in_=pk[p2][:], func=AF.Exp, scale=sc, bias=negB[:, 0:1]),
                                         reads=[PB[2 * p2], PB[2 * p2 + 1], Bc], writes=[BpT[j % 4]])

                                def pv(j, kvh=kvh):
                                    for i in range(2):
                                        kb = 2 * j + i
                                        for hh in range(4):
                                            S.op("pe", lambda e, i=i, kb=kb, hh=hh: e.matmul(obank[hh], lhsT=pT[j % 4][:, i * 512 + hh * 128:i * 512 + hh * 128 + 128],
                                                                                           rhs=Vs[:, kb, kvh, :], start=(kb == 0 and hh % 2 == 0), stop=(kb == NT - 1), skip_group_check=True),
                                                 reads=[BpT[j % 4], BV], writes=[POB[hh]], signal=(i == 1 and hh == 3))
                                qk(0)
                                qk(1)
                                for j in range(NT // 2):
                                    if j + 2 < NT // 2:
                                        qk(j + 2)
                                    pv(j)
                                for hh in range(4):
                                    h = 4 * kvh + hh
                                    S.op("dve", lambda e, hh=hh: e.reciprocal(out=rinv[:, hh:hh + 1], in_=obank[hh][:, 128:129]),
                                         reads=[POB[hh]], writes=[Brinv])
                                    S.op("dve", lambda e, hh=hh, h=h: e.scalar_tensor_tensor(out=asb[:, h, :], in0=obank[hh][:, 0:128], scalar=rinv[:, hh:hh + 1], in1=sat[:, h, :], op0=ALU.mult, op1=ALU.mult),
                                         reads=[POB[hh], Brinv, Bsat], writes=[Basb])
                            S.cur_prio = 1
                            for h in range(8):
                                S.op("pe", lambda e, h=h: e.transpose(bankb[4][:, h * 128:(h + 1) * 128], asb[:, h, :], idb[:]),
                                     reads=[Basb, Bc], writes=[PB[4]], signal=True)
                            S.op("dve", lambda e: e.tensor_copy(out=aT[:].rearrange("p h q -> p (h q)"), in_=bankb[4][:, 0:1024]), reads=[PB[4]], writes=[BaT])
                            for c in range(2):
                                for eh in range(8):
                                    S.op("pe", lambda e, c=c, eh=eh: e.matmul(bank[5], lhsT=aT[:, eh, :], rhs=Wap[:, eh, c * 512:(c + 1) * 512], start=(eh == 0), stop=(eh == 7)),
                                         reads=[BaT, BWap], writes=[PB[5]], signal=True)
                                S.op("dve", lambda e, c=c: e.tensor_tensor(out=tmpf[:, c * 512:(c + 1) * 512], in0=bank[5], in1=gat[:, c * 512:(c + 1) * 512], op=ALU.mult),
                                     reads=[PB[5], Bgat], writes=[Btmp])
                            S.op("dve", lambda e: e.tensor_tensor(out=mg[:], in0=tmpf[:], in1=mft[:], op=ALU.add),
                                 reads=[Btmp, Bmft], writes=[Bmg])
                            for h in range(8):
                                S.op("pe", lambda e, h=h: e.transpose(bankb[4][:, h * 128:(h + 1) * 128], mg[:, h * 128:(h + 1) * 128], idb[:]),
                                     reads=[Bmg, Bc], writes=[PB[4]], signal=True)
                            S.op("dve", lambda e: e.tensor_copy(out=mT[:].rearrange("p h q -> p (h q)"), in_=bankb[4][:, 0:1024]), reads=[PB[4]], writes=[BmT])
                            for half in range(2):
                                for e4 in range(4):
                                    eo = 4 * half + e4
                                    for dh in range(8):
                                        S.op("pe", lambda e, eo=eo, e4=e4, dh=dh: e.matmul(bank[5][:, e4 * 128:(e4 + 1) * 128], lhsT=Wout[:, dh, eo * 128:(eo + 1) * 128], rhs=mT[:, dh, :], start=(dh == 0), stop=(dh == 7)),
                                             reads=[BmT, BWout], writes=[PB[5]], signal=True)
                                S.op("dve", lambda e, half=half: e.tensor_tensor(out=osb[:, 4 * half:4 * half + 4, :].rearrange("p h q -> p (h q)"), in0=bank[5],
                                                                                   in1=xTt[:, 4 * half:4 * half + 4, :].rearrange("p h q -> p (h q)"), op=ALU.add),
                                     reads=[PB[5], BxT], writes=[Bosb])
                            S.dma("pool", outT[dp], osb[:], reads=[Bosb], writes=[out_buf])
                            S.cur_prio = 0

        S.wait_all("sp", [out_buf])
        S.wait_all("pool", [out_buf])
        S.emit()
    return nc


def _alpha(j):
    return np.array([4 * (t // 4) + ((j + t % 4) % 4) for t in range(NT)], dtype=np.int64)


def _consts(j):
    al = _alpha(j)
    d = 4 * np.arange(32, dtype=np.int64) + j
    bt = np.arange(128, dtype=np.int64)
    num = (128 * bt[:, None, None] * d[None, None, :] + al[None, :, None] * d[None, None, :]) % 16384
    ang = 2.0 * np.pi * num.astype(np.float64) / 16384.0
    sc = 1.0 / np.sqrt(128.0)
    ec = np.concatenate([np.cos(ang) * sc, -np.sin(ang) * sc], axis=2).astype(np.float32)
    a2 = 2.0 * np.pi * ((bt[:, None] * bt[None, :]) % 128).astype(np.float64) / 128.0
    Cc, Sc = np.cos(a2) * sc, np.sin(a2) * sc
    wc = np.concatenate([Cc, -Sc, Sc, Cc], axis=1).astype(np.float32)
    a3 = 2.0 * np.pi * ((al[:, None] * bt[None, :]) % 128).astype(np.float64) / 128.0
    f2 = np.concatenate([np.cos(a3) * sc, np.sin(a3) * sc], axis=1).astype(np.float32)
    inv = (np.float32(10000.0) ** (-np.arange(0, 64, 2, dtype=np.float32) / np.float32(64))).astype(np.float32)
    s = al[:, None] + 128 * bt[None, :]
    row = (s // 64).astype(np.float32)
    col = (s % 64).astype(np.float32)
    ar = (row[:, :, None] * inv[None, None, :]).astype(np.float32)
    ac = (col[:, :, None] * inv[None, None, :]).astype(np.float32)
    cr, sr, cc, scn = np.cos(ar), np.sin(ar), np.cos(ac), np.sin(ac)
    tab = np.concatenate([cr, cr, cc, cc, -sr, sr, -scn, scn], axis=2).astype(np.float32)
    return al, ec, wc, f2, np.ascontiguousarray(tab)


def kernel(x, norm_g, w_in, q_norm_g, k_norm_g, w_attn_proj, w_fourier_proj, w_merge, b_merge, w_out, _phases=PHASES, _debug=False):
    x = np.asarray(x, dtype=np.float32)
    f = lambda a: np.ascontiguousarray(np.asarray(a, dtype=np.float32))
    gl = f(np.asarray(norm_g)[0].reshape(8, 128).T)
    gqk = f(np.concatenate([np.broadcast_to(np.asarray(q_norm_g)[0][None, :], (128, 128)),
                            np.broadcast_to(np.asarray(k_norm_g)[0][None, :], (128, 128))], axis=1))
    bm = f(np.broadcast_to(np.asarray(b_merge)[0][None, :], (128, 2048)))
    common = {"idn": np.eye(128, dtype=np.float32), "gl": gl, "gqk": gqk, "bm": bm,
              "w_in": f(np.asarray(w_in)[0]), "w_ap": f(np.asarray(w_attn_proj)[0]), "w_fp": f(np.asarray(w_fourier_proj)[0]),
              "w_mg": f(np.asarray(w_merge)[0]), "w_out": f(np.asarray(w_out)[0])}
    in_maps = []
    for core in range(8):
        b, j = core // 4, core % 4
        al, ec, wc, f2, tab = _consts(j)
        xv = x[b].reshape(128, 128, 8, 128).transpose(1, 3, 2, 0)
        xsv = np.ascontiguousarray(xv[al])
        m = dict(common)
        m.update({"xs": xsv, "tab": tab, "ec": ec, "wc": wc, "f2": f2})
        in_maps.append(m)
    nc = build_nc(_phases, _debug)
    res = run_bass_kernel_spmd(nc, in_maps, core_ids=list(range(8)))
    out = np.empty((B_, S_, D_), dtype=np.float32)
    for core in range(8):
        b, j = core // 4, core % 4
        o = res.results[core]["outT"].transpose(3, 0, 2, 1).reshape(128, NQ, 1024)
        out[b].reshape(128, NQ, 4, 1024)[:, :, j, :] = o
    if _debug:
        return out, res
    return out
```

```python
import contextlib
import numpy as np
import concourse.bass as bass
import concourse.mybir as mybir
from concourse.bass_utils import run_bass_kernel_spmd

F32 = mybir.dt.float32
BF16 = mybir.dt.bfloat16
AF = mybir.ActivationFunctionType
ALU = mybir.AluOpType
AX = mybir.AxisListType

B_, S_, D_ = 2, 16384, 1024
NT = 128
NQ = 32
EPS = 1e-6
PHASES = "ABCD"


class Buf:
    __slots__ = ("w", "r", "multi", "excl")

    def __init__(self, multi=False, excl=False):
        self.w = []
        self.r = []
        self.multi = multi
        self.excl = excl
        _ALL_BUFS.append(self)


_ALL_BUFS = []


class _Rec:
    def __getattr__(self, name):
        def f(*a, **k):
            self.call = (name, a, k)
            return self
        return f


def _free(ap):
    n = 1
    for d in ap.shape[1:]:
        n *= d
    return n


class _Op:
    __slots__ = ("idx", "eng", "calls", "preds", "succs", "dur", "lat", "dma", "pos", "tok", "start", "nready", "indeg", "prio")


class Sched:
    ENG = ("pe", "act", "dve", "pool", "sp")
    XLAT = 250.0

    def __init__(self, nc, ctx, n_dma_sems=24):
        self.nc = nc
        self.streams = {e: [] for e in self.ENG}
        self.sems = {}
        self.cnt = {e: 0 for e in self.ENG}
        self.seen = {e: {} for e in self.ENG}
        for e in self.ENG:
            self.sems[e] = ctx.enter_context(nc.semaphore("s_" + e))
        self.dma_pool = {}
        for q in ("sp", "pool"):
            lst = []
            for i in range(n_dma_sems):
                key = "d_%s_%d" % (q, i)
                self.sems[key] = ctx.enter_context(nc.semaphore(key))
                lst.append(key)
            self.dma_pool[q] = lst
        self.dma_rr = {"sp": 0, "pool": 0}
        self.dma_val = {}
        self.ops = []
        self.pend = {e: None for e in self.ENG}
        self.simlog = {e: [] for e in self.ENG}
        self.cur_prio = 0
        self.pe_scale = 1.0
        del _ALL_BUFS[:]

    def _est(self, eng, call):
        name, a, k = call
        try:
            if eng == "pe":
                if name == "transpose":
                    return 70.0
                return (12.0 + _free(k["rhs"]) / 2.3) * self.pe_scale
            out = k.get("out", a[0] if a else None)
            n = _free(out)
            if eng == "act":
                return 224.0 + 0.833 * n
            if eng == "dve":
                return 200.0 + 1.05 * n
            return 120.0 + 1.8 * n
        except Exception:
            return 300.0

    def _deps(self, op, reads, writes):
        preds = op.preds
        i = op.idx
        for b in reads:
            for p in b.w:
                if p != i:
                    preds[p] = True
            if b.excl:
                for p in b.r:
                    if p != i and p not in preds:
                        preds[p] = False
        for b in writes:
            if not b.multi:
                for p in b.w:
                    if p != i:
                        preds[p] = True
            for p in b.r:
                if p != i and p not in preds:
                    preds[p] = False
        for b in reads:
            if not b.r or b.r[-1] != i:
                b.r.append(i)
        for b in writes:
            if b.multi:
                if not b.w or b.w[-1] != i:
                    b.w.append(i)
            else:
                b.w = [i]
                b.r = []

    def _new(self, eng):
        op = _Op()
        op.idx = len(self.ops)
        op.eng = eng
        op.calls = []
        op.preds = {}
        op.dur = 0.0
        op.lat = 0.0
        op.dma = None
        op.prio = self.cur_prio
        self.ops.append(op)
        return op

    def op(self, eng, fn, reads=(), writes=(), signal=True):
        rec = _Rec()
        fn(rec)
        call = rec.call
        op = self.pend[eng]
        if op is None:
            op = self._new(eng)
            self.pend[eng] = op
        op.calls.append(call)
        op.dur += self._est(eng, call)
        self._deps(op, reads, writes)
        if signal:
            self.pend[eng] = None

    def dma(self, q, out, in_, reads=(), writes=()):
        assert self.pend[q] is None
        op = self._new(q)
        op.dma = (out, in_)
        op.dur = 60.0
        try:
            nbytes = out.shape[0] * _free(out) * (2 if out.dtype == BF16 else 4)
        except Exception:
            nbytes = 1 << 19
        op.lat = 2000.0 + nbytes / 120.0
        self._deps(op, reads, writes)

    def _flush(self):
        import heapq
        ops = self.ops
        if not ops:
            return
        for e in self.ENG:
            assert self.pend[e] is None, "unterminated instruction group on " + e
        n = len(ops)
        for o in ops:
            o.succs = []
            o.nready = 0.0
        for o in ops:
            for p in o.preds:
                ops[p].succs.append(o.idx)
            o.indeg = len(o.preds)
        free = {e: 0.0 for e in self.ENG}
        heap = [(0.0, o.prio, o.idx) for o in ops if o.indeg == 0]
        heapq.heapify(heap)
        order = {e: [] for e in self.ENG}
        done = 0
        while heap:
            key, _pr, i = heapq.heappop(heap)
            o = ops[i]
            st = max(o.nready, free[o.eng])
            if st > key + 1e-9:
                heapq.heappush(heap, (st, o.prio, i))
                continue
            o.start = st
            fin_eng = st + o.dur
            free[o.eng] = fin_eng
            fin = fin_eng + o.lat
            order[o.eng].append(o)
            done += 1
            for sidx in o.succs:
                s2 = ops[sidx]
                lat = 0.0 if (s2.eng == o.eng and o.dma is None) else self.XLAT
                if fin + lat > s2.nready:
                    s2.nready = fin + lat
                s2.indeg -= 1
                if s2.indeg == 0:
                    heapq.heappush(heap, (s2.nready, s2.prio, sidx))
        assert done == n, "dependency cycle"
        for e in self.ENG:
            for o in order[e]:
                if o.dma is None:
                    self.cnt[e] += 1
                    o.pos = self.cnt[e]
                    o.tok = (e, o.pos)
                else:
                    pool = self.dma_pool[e]
                    key = pool[self.dma_rr[e] % len(pool)]
                    self.dma_rr[e] += 1
                    prev = self.dma_val.get(key, 0)
                    self.dma_val[key] = prev + 16
                    o.tok = (key, prev + 16)
                    o.pos = prev
        sems = self.sems
        for e in self.ENG:
            seen = self.seen[e]
            for o in order[e]:
                waits = []
                for p, hard in o.preds.items():
                    po = ops[p]
                    if po.eng == e and po.dma is None and e == "pe":
                        continue
                    k, v = po.tok
                    if seen.get(k, 0) < v:
                        seen[k] = v
                        waits.append((k, v))
                if o.dma is not None:
                    k, v = o.tok
                    if o.pos > 0 and seen.get(k, 0) < o.pos:
                        seen[k] = o.pos
                        waits.append((k, o.pos))
                self.streams[e].append(self._mk(e, o, waits))
                self.simlog[e].append((waits, o.tok[0], 16 if o.dma is not None else 1))
        self.ops = []
        for b in _ALL_BUFS:
            b.w = []
            b.r = []

    def _mk(self, eng, o, waits):
        sems = self.sems
        semh = sems[eng]
        if o.dma is not None:
            out, in_ = o.dma
            key = o.tok[0]

            def emit(e):
                for (k, v) in waits:
                    e.wait_ge(sems[k], v)
                e.dma_start(out=out, in_=in_).then_inc(sems[key], 16)
            return emit
        calls = o.calls

        def emit(e):
            for (k, v) in waits:
                e.wait_ge(sems[k], v)
            ins = None
            for (cname, cargs, ckw) in calls:
                ins = getattr(e, cname)(*cargs, **ckw)
            ins.then_inc(semh, 1)
        return emit

    def barrier(self):
        self._flush()
        targets = [(e, self.cnt[e]) for e in self.ENG if self.cnt[e] > 0]
        targets += list(self.dma_val.items())
        sems = self.sems
        for eng in self.ENG:
            waits = []
            seen = self.seen[eng]
            for key, val in targets:
                if key == eng and eng == "pe":
                    continue
                if seen.get(key, 0) < val:
                    seen[key] = val
                    waits.append((key, val))

            def emit(e, waits=waits):
                for (k, v) in waits:
                    e.wait_ge(sems[k], v)

            self.streams[eng].append(emit)
            self.simlog[eng].append((waits, None, 0))

    def check(self):
        val = {}
        pc = {e: 0 for e in self.ENG}
        prog = True
        while prog:
            prog = False
            for e in self.ENG:
                lg = self.simlog[e]
                while pc[e] < len(lg):
                    waits, k, inc = lg[pc[e]]
                    if any(val.get(wk, 0) < wv for wk, wv in waits):
                        break
                    if k is not None:
                        val[k] = val.get(k, 0) + inc
                    pc[e] += 1
                    prog = True
        stuck = {e: (pc[e], len(self.simlog[e])) for e in self.ENG if pc[e] < len(self.simlog[e])}
        for e, (p, n) in stuck.items():
            waits, k, inc = self.simlog[e][p]
            print("STUCK", e, p, n, [(wk, wv, val.get(wk, 0)) for wk, wv in waits if val.get(wk, 0) < wv])
        return not stuck

    def wait_all(self, eng, bufs):
        pass

    def emit(self):
        self.barrier()
        assert self.check(), "semaphore deadlock in generated program"
        nc = self.nc
        st = self.streams
        with nc.Block() as block:
            @block.tensor
            def _(e):
                for f in st["pe"]:
                    f(e)

            @block.scalar
            def _(e):
                for f in st["act"]:
                    f(e)

            @block.vector
            def _(e):
                for f in st["dve"]:
                    f(e)

            @block.gpsimd
            def _(e):
                for f in st["pool"]:
                    f(e)

            @block.sync
            def _(e):
                for f in st["sp"]:
                    f(e)


def build_nc(phases=PHASES, debug=False):
    nc = bass.Bass("TRN2", target_bir_lowering=False)
    din = lambda name, shape: nc.dram_tensor(name, shape, F32, kind="ExternalInput").ap()
    xs = din("xs", [NT, 128, 8, 128])
    tab = din("tab", [NT, 128, 256])
    ecd = din("ec", [128, NT, 64])
    wcd = din("wc", [128, 512])
    f2d = din("f2", [128, 256])
    idnd = din("idn", [128, 128])
    gld = din("gl", [128, 8])
    gqkd = din("gqk", [128, 256])
    bmd = din("bm", [128, 2048])
    w_in = din("w_in", [1024, 3584])
    w_ap = din("w_ap", [1024, 1024])
    w_fp = din("w_fp", [512, 1024])
    w_mg = din("w_mg", [1024, 2048])
    w_out = din("w_out", [1024, 1024])
    outT = nc.dram_tensor("outT", [NQ, 128, 8, 128], F32, kind="ExternalOutput").ap()
    QTs = nc.dram_tensor("QTs", [NQ, 128, 1024], BF16, kind="Internal").ap()
    SAs = nc.dram_tensor("SAs", [NQ, 128, 1024], BF16, kind="Internal").ap()
    GAs = nc.dram_tensor("GAs", [NQ, 128, 1024], BF16, kind="Internal").ap()
    MFs = nc.dram_tensor("MFs", [NQ, 128, 1024], F32, kind="Internal").ap()
    dbg = {}
    if debug:
        dbg["fraw"] = nc.dram_tensor("dbg_fraw", [128, NQ, 512], F32, kind="ExternalOutput").ap()

    with contextlib.ExitStack() as ctx:
        S = Sched(nc, ctx)

        def T(c, name, shape, dt):
            return c.enter_context(nc.sbuf_tensor("sb_" + name, shape, dt))

        pk = [ctx.enter_context(nc.psum_tensor("pk%d" % i, [128, 1024], F32)) for i in range(4)]
        bank = [pk[i // 2][:, (i % 2) * 512:(i % 2) * 512 + 512] for i in range(8)]
        bankb = [bank[i].bitcast(BF16) for i in range(8)]
        PB = [Buf(excl=True) for _ in range(8)]

        idb = T(ctx, "idb", [128, 128], BF16)
        ones = T(ctx, "ones", [128, 1], BF16)
        gl = T(ctx, "gl", [128, 8], F32)
        gqk = T(ctx, "gqk", [128, 256], F32)
        negB = T(ctx, "negB", [128, 1], F32)
        gqs = T(ctx, "gqs", [128, 128], F32)
        Bgqs = Buf(multi=True)
        sm = T(ctx, "sm", [128, 8], F32)
        Bc = Buf()
        out_buf = Buf(multi=True)

        with contextlib.ExitStack() as c0:
            idf = T(c0, "idf", [128, 128], F32)
            Bi = Buf()
            S.dma("sp", idf[:], idnd, writes=[Bi])
            S.dma("sp", gl[:], gld, writes=[Bc])
            S.dma("sp", gqk[:], gqkd, writes=[Bc])
            S.op("dve", lambda e: e.tensor_copy(out=idb[:], in_=idf[:]), reads=[Bi], writes=[Bc])
            S.op("pool", lambda e: e.memset(ones[:], 1.0), writes=[Bc])
            for blk in range(2):
                for f in range(2):
                    S.op("dve", lambda e, blk=blk, f=f: e.tensor_copy(out=gqs[:, blk * 64 + f * 32:blk * 64 + f * 32 + 32], in_=gqk[:, blk * 64 + (1 - f) * 32:blk * 64 + (1 - f) * 32 + 32]),
                         reads=[Bc], writes=[Bgqs])
            S.op("dve", lambda e: e.tensor_tensor(out=idf[:, 0:128], in0=gqk[:, 0:128], in1=gqk[:, 0:128], op=ALU.mult),
                 reads=[Bc], writes=[Bi])
            S.op("dve", lambda e: e.tensor_reduce(out=sm[:, 0:1], in_=idf[:, 0:128], axis=AX.X, op=ALU.max),
                 reads=[Bi], writes=[Bc])
            S.op("dve", lambda e: e.tensor_tensor(out=idf[:, 0:128], in0=gqk[:, 128:256], in1=gqk[:, 128:256], op=ALU.mult),
                 reads=[Bc], writes=[Bi])
            S.op("dve", lambda e: e.tensor_reduce(out=sm[:, 1:2], in_=idf[:, 0:128], axis=AX.X, op=ALU.max),
                 reads=[Bi], writes=[Bc])
            S.op("dve", lambda e: e.tensor_tensor(out=sm[:, 2:3], in0=sm[:, 0:1], in1=sm[:, 1:2], op=ALU.mult),
                 reads=[Bc], writes=[Bc])
            S.op("act", lambda e: e.activation(out=sm[:, 3:4], in_=sm[:, 2:3], func=AF.Sqrt, scale=128.0, bias=0.0),
                 reads=[Bc], writes=[Bc])
            S.op("dve", lambda e: e.tensor_scalar(out=negB[:], in0=sm[:, 3:4], scalar1=-1.0, scalar2=None, op0=ALU.mult),
                 reads=[Bc], writes=[Bc])
            S.barrier()

        def pipeline(stages, n):
            ns = len(stages)
            for step in range(n + ns - 1):
                for si, f in enumerate(stages):
                    t = step - si
                    if 0 <= t < n:
                        f(t)

        junk = T(ctx, "junk", [128, 128], BF16)
        Bjunk = Buf(multi=True)

        class XPipe:
            def __init__(self, c, tag, ssb=(7,), rb=3):
                self.ssb = ssb
                self.rb = rb
                self.xt = [T(c, "xt%s%d" % (tag, i), [128, 8, 128], F32) for i in range(2)]
                self.xq = [T(c, "xq%s%d" % (tag, i), [128, 8, 128], BF16) for i in range(2)]
                self.xb = [T(c, "xb%s%d" % (tag, i), [128, 8, 128], BF16) for i in range(rb)]
                self.rs = [T(c, "rs%s%d" % (tag, i), [128, 2], F32) for i in range(rb)]
                self.Bxt = [Buf() for _ in range(2)]
                self.Bxq = [Buf() for _ in range(2)]
                self.Bxb = [Buf() for _ in range(rb)]
                self.Brs = [Buf() for _ in range(rb)]

            def load(self, i, t):
                S.dma("sp", self.xt[i % 2][:], xs[t], writes=[self.Bxt[i % 2]])

            def XB(self, i):
                return self.xb[i % self.rb], self.Bxb[i % self.rb]

            def RS(self, i):
                return self.rs[i % self.rb], self.Brs[i % self.rb]

            def prep(self, i):
                xt, xq, Bxt, Bxq = self.xt[i % 2], self.xq[i % 2], self.Bxt[i % 2], self.Bxq[i % 2]
                xb, Bxb = self.XB(i)
                rs, Brs = self.RS(i)
                sb_ = self.ssb[i % len(self.ssb)]
                S.op("dve", lambda e: e.tensor_copy(out=xb[:], in_=xt[:]), reads=[Bxt], writes=[Bxb])
                S.op("act", lambda e: e.activation(out=xq[:], in_=xt[:], func=AF.Square, scale=1.0, bias=0.0), reads=[Bxt], writes=[Bxq])
                for dh in range(8):
                    S.op("pe", lambda e, dh=dh: e.matmul(bank[sb_][:, 0:1], lhsT=xq[:, dh, :], rhs=ones[:], start=(dh == 0), stop=(dh == 7)),
                         reads=[Bxq, Bc], writes=[PB[sb_]], signal=(dh == 7))
                S.op("act", lambda e: e.activation(out=rs[:, 1:2], in_=bank[sb_][:, 0:1], func=AF.Sqrt, scale=1.0 / D_, bias=EPS),
                     reads=[PB[sb_]], writes=[Brs])
                S.op("dve", lambda e: e.reciprocal(out=rs[:, 0:1], in_=rs[:, 1:2]), reads=[Brs], writes=[Brs])

        def head_norm(heads, rs, Brs, gcol, dst, Bdst, ssq, Bss):
            H = len(heads)
            S.op("dve", lambda e: e.memset(ssq[:, 0:8], 0.0), writes=[Bss])
            for h, (pa, pbuf) in enumerate(heads):
                S.op("act", lambda e, h=h, pa=pa: e.activation(out=junk[:], in_=pa, func=AF.Square, scale=rs[:, 0:1], bias=0.0, accum_out=ssq[:, h:h + 1]),
                     reads=[pbuf, Brs], writes=[Bjunk, Bss])
            S.op("act", lambda e: e.activation(out=ssq[:, 8:8 + H], in_=ssq[:, 0:H], func=AF.Sqrt, scale=1.0 / 128, bias=EPS),
                 reads=[Bss], writes=[Bss])
            S.op("dve", lambda e: e.reciprocal(out=ssq[:, 16:16 + H], in_=ssq[:, 8:8 + H]), reads=[Bss], writes=[Bss])
            S.op("dve", lambda e: e.tensor_scalar(out=ssq[:, 24:24 + H], in0=ssq[:, 16:16 + H], scalar1=rs[:, 0:1], scalar2=None, op0=ALU.mult),
                 reads=[Bss, Brs], writes=[Bss])
            for h, (pa, pbuf) in enumerate(heads):
                S.op("dve", lambda e, h=h, pa=pa: e.scalar_tensor_tensor(out=dst[:, h * 128:(h + 1) * 128], in0=pa, scalar=ssq[:, 24 + h:25 + h],
                                                                        in1=gqk[:, gcol:gcol + 128], op0=ALU.mult, op1=ALU.mult),
                     reads=[pbuf, Bss, Bc], writes=[Bdst])

        def rope(src, Bsrc, H, tabt, Btab, t1, t2, Bt1, Bt2, dst, Bdst, rq=None, Brq=None):
            W = H * 128
            s3 = src[:, 0:W].rearrange("p (h d) -> p h d", h=H)
            cos_b = tabt[:, 0:128].unsqueeze(1).broadcast_to([128, H, 128])
            S.op("dve", lambda e: e.tensor_tensor(out=t1[:, 0:W].rearrange("p (h d) -> p h d", h=H), in0=s3, in1=cos_b, op=ALU.mult),
                 reads=[Bsrc, Btab], writes=[Bt1])
            s5 = src[:, 0:W].rearrange("p (h b f w) -> p h b f w", h=H, b=2, f=2)
            t5 = t2[:, 0:W].rearrange("p (h b f w) -> p h b f w", h=H, b=2, f=2)
            sn4 = tabt[:, 128:256].rearrange("p (b f w) -> p b f w", b=2, f=2)
            for f in range(2):
                sin_b = sn4[:, :, f, :].unsqueeze(1).broadcast_to([128, H, 2, 32])
                S.op("dve", lambda e, f=f, sin_b=sin_b: e.tensor_tensor(out=t5[:, :, :, f, :], in0=s5[:, :, :, 1 - f, :], in1=sin_b, op=ALU.mult),
                     reads=[Bsrc, Btab], writes=[Bt2])
            if rq is None:
                S.op("dve", lambda e: e.tensor_tensor(out=dst[:].rearrange("p h d -> p (h d)"), in0=t1[:, 0:W], in1=t2[:, 0:W], op=ALU.add),
                     reads=[Bt1, Bt2], writes=[Bdst])
            else:
                S.op("dve", lambda e: e.tensor_tensor(out=t1[:, 0:W], in0=t1[:, 0:W], in1=t2[:, 0:W], op=ALU.add),
                     reads=[Bt1, Bt2], writes=[Bt1])
                S.op("dve", lambda e: e.tensor_tensor(out=dst[:], in0=t1[:, 0:W].rearrange("p (h d) -> p h d", h=H), in1=rq.unsqueeze(2).broadcast_to([128, H, 128]), op=ALU.mult),
                     reads=[Bt1, Brq], writes=[Bdst])

        def load_w(stg, Bstg, dst, Bdst, wd, col0, ncols, fold, kh=8, state=[0]):
            cw = stg[0].shape[2]
            for c0_ in range(0, ncols, cw):
                w = min(cw, ncols - c0_)
                i = state[0] % 2
                state[0] += 1
                src = wd[:, col0 + c0_:col0 + c0_ + w].rearrange("(dh dl) e -> dl dh e", dl=128)
                S.dma("sp", stg[i][:, 0:kh, 0:w], src, writes=[Bstg[i]])
                if fold:
                    for dh in range(kh):
                        S.op("dve", lambda e, dh=dh, i=i, c0_=c0_, w=w: e.tensor_scalar(
                            out=dst[:, dh, c0_:c0_ + w], in0=stg[i][:, dh, 0:w], scalar1=gl[:, dh:dh + 1], scalar2=None, op0=ALU.mult),
                            reads=[Bstg[i], Bc], writes=[Bdst])
                else:
                    S.op("dve", lambda e, i=i, c0_=c0_, w=w: e.tensor_copy(out=dst[:, :, c0_:c0_ + w], in_=stg[i][:, 0:kh, 0:w]),
                         reads=[Bstg[i]], writes=[Bdst])

        cab = contextlib.ExitStack()
        fraw = T(cab, "fraw", [128, NQ, 512], BF16)
        Bfraw = Buf(multi=True)

        if "A" in phases:
            S.pe_scale = 1.5
            with contextlib.ExitStack() as ca:
                Wu = T(ca, "Wu", [128, 8, 512], BF16)
                Ec = T(ca, "Ec", [128, NT, 64], BF16)
                Wc = T(ca, "Wc", [128, 512], BF16)
                F2 = T(ca, "F2", [128, 256], BF16)
                BWu, BEc, BWc, BF2, BY = Buf(multi=True), Buf(multi=True), Buf(), Buf(), Buf(multi=True)
                stg = [T(ca, "stgA%d" % i, [128, 8, 256], F32) for i in range(2)]
                Bstg = [Buf(), Buf()]
                load_w(stg, Bstg, Wu, BWu, w_in, 2560, 512, True)
                for i in range(4):
                    sf = stg[i % 2][:].rearrange("p a b -> p (a b)")
                    S.dma("sp", sf, ecd[:, 32 * i:32 * i + 32, :].rearrange("p a b -> p (a b)"), writes=[Bstg[i % 2]])
                    S.op("dve", lambda e, i=i, sf=sf: e.tensor_copy(out=Ec[:, 32 * i:32 * i + 32, :].rearrange("p a b -> p (a b)"), in_=sf),
                         reads=[Bstg[i % 2]], writes=[BEc])
                S.dma("sp", stg[0][:, 0:2, :].rearrange("p a b -> p (a b)"), wcd, writes=[Bstg[0]])
                S.op("dve", lambda e: e.tensor_copy(out=Wc[:], in_=stg[0][:, 0:2, :].rearrange("p a b -> p (a b)")), reads=[Bstg[0]], writes=[BWc])
                S.dma("sp", stg[1][:, 0, 0:256], f2d, writes=[Bstg[1]])
                S.op("dve", lambda e: e.tensor_copy(out=F2[:], in_=stg[1][:, 0, 0:256]), reads=[Bstg[1]], writes=[BF2])
                Ysb = T(ca, "Ysb", [128, NT, 4, 2, 32], BF16)
                Zg = [T(ca, "Zg%d" % i, [128, 32, 2, 128], BF16) for i in range(2)]
                BZ = [Buf(multi=True), Buf(multi=True)]
                usb = [T(ca, "usb%d" % i, [128, 512], BF16) for i in range(3)]
                Bus = [Buf() for _ in range(3)]
                xp = XPipe(ca, "A", (6, 7), rb=3)

                def a_s0(t):
                    xp.load(t, t)

                def a_s1(t):
                    xp.prep(t)

                def a_s2(t):
                    xb, Bxb = xp.XB(t)
                    rs, Brs = xp.RS(t)
                    ub = t % 2
                    for dh in range(8):
                        S.op("pe", lambda e, dh=dh: e.matmul(bank[ub], lhsT=xb[:, dh, :], rhs=Wu[:, dh, :], start=(dh == 0), stop=(dh == 7)),
                             reads=[Bxb, BWu], writes=[PB[ub]], signal=(dh == 7))
                    S.op("dve", lambda e: e.tensor_scalar(out=usb[t % 3][:], in0=bank[ub], scalar1=rs[:, 0:1], scalar2=None, op0=ALU.mult),
                         reads=[PB[ub], Brs], writes=[Bus[t % 3]])

                def a_s3(t):
                    yb = 2 + t % 2
                    for g in range(4):
                        S.op("pe", lambda e, g=g: e.matmul(bank[yb][:, g * 64:(g + 1) * 64], lhsT=usb[t % 3][:, g * 128:(g + 1) * 128], rhs=Ec[:, t, :], start=True, stop=True),
                             reads=[Bus[t % 3], BEc], writes=[PB[yb]], signal=(g == 3))
                    S.op("act", lambda e: e.copy(out=Ysb[:, t, :, :, :].rearrange("p g r d -> p (g r d)"), in_=bank[yb][:, 0:256]),
                         reads=[PB[yb]], writes=[BY])

                pipeline([a_s0, a_s1, a_s2, a_s3], NT)
                n = 0
                for g in range(4):
                    zg, bz = Zg[g % 2], BZ[g % 2]
                    for dp in range(32):
                        zb = 4 + n % 2
                        S.op("pe", lambda e, g=g, dp=dp, zb=zb: e.matmul(bank[zb][:, 0:256], lhsT=Ysb[:, :, g, 0, dp], rhs=Wc[:, 0:256], start=True, stop=False),
                             reads=[BY, BWc], writes=[PB[zb]], signal=False)
                        S.op("pe", lambda e, g=g, dp=dp, zb=zb: e.matmul(bank[zb][:, 0:256], lhsT=Ysb[:, :, g, 1, dp], rhs=Wc[:, 256:512], start=False, stop=True),
                             reads=[BY, BWc], writes=[PB[zb]])
                        if n % 2 == 0:
                            S.op("dve", lambda e, dp=dp, zb=zb, zg=zg: e.tensor_copy(out=zg[:, dp, :, :].rearrange("p r c -> p (r c)"), in_=bank[zb][:, 0:256]),
                                 reads=[PB[zb]], writes=[bz])
                        else:
                            S.op("act", lambda e, dp=dp, zb=zb, zg=zg: e.copy(out=zg[:, dp, :, :].rearrange("p r c -> p (r c)"), in_=bank[zb][:, 0:256]),
                                 reads=[PB[zb]], writes=[bz])
                        n += 1
                    for q4 in range(8):
                        fb = q4 % 2
                        for i in range(4):
                            dp = 4 * q4 + i
                            S.op("pe", lambda e, dp=dp, i=i, fb=fb, zg=zg: e.matmul(bank[fb][:, i * 128:(i + 1) * 128], lhsT=F2[:, 0:128], rhs=zg[:, dp, 0, :], start=True, stop=False),
                                 reads=[bz, BF2], writes=[PB[fb]], signal=False)
                            S.op("pe", lambda e, dp=dp, i=i, fb=fb, zg=zg: e.matmul(bank[fb][:, i * 128:(i + 1) * 128], lhsT=F2[:, 128:256], rhs=zg[:, dp, 1, :], start=False, stop=True),
                                 reads=[bz, BF2], writes=[PB[fb]], signal=(i == 3))
                        dst = fraw[:, 4 * q4:4 * q4 + 4, g * 128:(g + 1) * 128]
                        src = bank[fb].rearrange("p (i c) -> p i c", i=4)
                        if q4 % 2 == 0:
                            S.op("dve", lambda e, dst=dst, src=src: e.tensor_copy(out=dst, in_=src), reads=[PB[fb]], writes=[Bfraw])
                        else:
                            S.op("act", lambda e, dst=dst, src=src: e.copy(out=dst, in_=src), reads=[PB[fb]], writes=[Bfraw])
                if debug:
                    dstg = [T(ca, "dstg%d" % i, [128, 4, 512], F32) for i in range(2)]
                    Bstg = [Buf(), Buf()]
                    for i in range(8):
                        k = i % 2
                        S.op("dve", lambda e, i=i, k=k: e.tensor_copy(out=dstg[k][:], in_=fraw[:, 4 * i:4 * i + 4, :]),
                             reads=[Bfraw], writes=[Bstg[k]])
                        S.dma("pool", dbg["fraw"][:, 4 * i:4 * i + 4, :], dstg[k][:], reads=[Bstg[k]], writes=[out_buf])
                S.barrier()

        BQTs = [Buf() for _ in range(NQ)]
        BSAs = [Buf() for _ in range(NQ)]
        BGAs = [Buf() for _ in range(NQ)]
        BMFs = [Buf() for _ in range(NQ)]
        if "B" in phases:
            S.pe_scale = 1.0
            with contextlib.ExitStack() as cb:
                Wq = T(cb, "Wq", [128, 8, 1024], BF16)
                Wza = T(cb, "Wza", [128, 8, 1024], BF16)
                Wzf = T(cb, "Wzf", [128, 8, 512], BF16)
                Wmg = T(cb, "Wmg", [128, 8, 2048], BF16)
                Wfp = T(cb, "Wfp", [128, 4, 1024], BF16)
                bm = T(cb, "bm", [128, 2048], F32)
                BWq, BWza, BWzf, BWmg, BWfp = [Buf(multi=True) for _ in range(5)]
                Bbm = Buf()
                S.dma("sp", bm[:], bmd, writes=[Bbm])
                stg = [T(cb, "stgB%d" % i, [128, 8, 256], F32) for i in range(2)]
                Bstg = [Buf(), Buf()]
                load_w(stg, Bstg, Wq, BWq, w_in, 0, 1024, True)
                load_w(stg, Bstg, Wza, BWza, w_in, 1536, 1024, True)
                load_w(stg, Bstg, Wzf, BWzf, w_in, 3072, 512, True)
                load_w(stg, Bstg, Wfp, BWfp, w_fp, 0, 1024, False, kh=4)
                load_w(stg, Bstg, Wmg, BWmg, w_mg, 0, 2048, True)
                xp = XPipe(cb, "B", (7,), rb=4)

                def ring(name, shape, dt, n=2):
                    return [T(cb, "%s%d" % (name, i), shape, dt) for i in range(n)], [Buf() for _ in range(n)]
                tabt, Btab = ring("tabB", [128, 256], F32, 4)
                tabg, Btabg = ring("tabg", [128, 256], F32, 1)
                qn, Bqn = ring("qn", [128, 1024], F32, 2)
                ssq, Bssq = ring("ssq", [128, 32], F32, 2)
                t1, Bt1 = ring("t1", [128, 1024], F32, 1)
                t2, Bt2 = ring("t2", [128, 1024], F32, 1)
                qr, Bqr = ring("qr", [128, 8, 128], BF16, 1)
                QTt, BQT = ring("QTt", [128, 1024], BF16, 1)
                sg, Bsg = ring("sg", [128, 512], F32, 2)
                sa, Bsa = ring("sa", [128, 1024], BF16, 2)
                sf, Bsf = ring("sf", [128, 512], F32, 1)
                fg, Bfg = ring("fg", [128, 512], BF16, 2)
                fT, BfT = ring("fT", [128, 4, 128], BF16, 2)
                gt, Bgt = ring("gt", [128, 512], F32, 1)
                ga, Bga = ring("ga", [128, 1024], BF16, 1)
                gf, Bgf = ring("gf", [128, 512], F32, 1)
                mf, Bmf = ring("mf", [128, 1024], F32, 1)

                def proj(i, pb, Wt, BW, c0_):
                    xb, Bxb = xp.XB(i)
                    for dh in range(8):
                        S.op("pe", lambda e, dh=dh: e.matmul(bank[pb], lhsT=xb[:, dh, :], rhs=Wt[:, dh, c0_:c0_ + 512], start=(dh == 0), stop=(dh == 7)),
                             reads=[Bxb, BW], writes=[PB[pb]], signal=(dh == 7))

                def b_s0(i):
                    xp.load(i, 4 * i)
                    S.dma("sp", tabt[i % 4][:], tab[4 * i], writes=[Btab[i % 4]])

                def b_s1(i):
                    xp.prep(i)

                def b_s2(i):
                    rs, Brs = xp.RS(i)
                    k = i % 2
                    proj(i, 0, Wq, BWq, 0)
                    proj(i, 1, Wq, BWq, 512)
                    for c in range(2):
                        S.op("act", lambda e, c=c: e.activation(out=qn[k][:, c * 512:(c + 1) * 512], in_=bank[c], func=AF.Identity, scale=rs[:, 0:1], bias=0.0),
                             reads=[PB[c], Brs], writes=[Bqn[k]])
                    sq_, Bsq_ = ssq[k], Bssq[k]
                    S.op("dve", lambda e: e.memset(sq_[:, 0:8], 0.0), writes=[Bsq_])
                    for h in range(8):
                        S.op("act", lambda e, h=h: e.activation(out=junk[:], in_=qn[k][:, h * 128:(h + 1) * 128], func=AF.Square, scale=1.0, bias=0.0, accum_out=sq_[:, h:h + 1]),
                             reads=[Bqn[k]], writes=[Bjunk, Bsq_])
                    S.op("act", lambda e: e.activation(out=sq_[:, 8:16], in_=sq_[:, 0:8], func=AF.Sqrt, scale=1.0 / 128, bias=EPS), reads=[Bsq_], writes=[Bsq_])
                    S.op("dve", lambda e: e.reciprocal(out=sq_[:, 16:24], in_=sq_[:, 8:16]), reads=[Bsq_], writes=[Bsq_])
                    tg, Btg = tabg[0], Btabg[0]
                    S.op("dve", lambda e: e.tensor_tensor(out=tg[:, 0:128], in0=tabt[i % 4][:, 0:128], in1=gqk[:, 0:128], op=ALU.mult),
                         reads=[Btab[i % 4], Bc], writes=[Btg])
                    S.op("dve", lambda e: e.tensor_tensor(out=tg[:, 128:256], in0=tabt[i % 4][:, 128:256], in1=gqs[:], op=ALU.mult),
                         reads=[Btab[i % 4], Bc], writes=[Btg])
                    rope(qn[k], Bqn[k], 8, tg, Btg, t1[0], t2[0], Bt1[0], Bt2[0], qr[0], Bqr[0], rq=sq_[:, 16:24], Brq=Bsq_)
                    for h in range(8):
                        S.op("pe", lambda e, h=h: e.transpose(bankb[2][:, h * 128:(h + 1) * 128], qr[0][:, h, :], idb[:]),
                             reads=[Bqr[0], Bc], writes=[PB[2]], signal=(h == 7))
                    S.op("dve", lambda e: e.tensor_copy(out=QTt[0][:], in_=bankb[2][:, 0:1024]), reads=[PB[2]], writes=[BQT[0]])
                    S.dma("pool", QTs[i], QTt[0][:], reads=[BQT[0]], writes=[BQTs[i]])

                def b_s3(i):
                    rs, Brs = xp.RS(i)
                    k = i % 2
                    for c in range(2):
                        pb = 3 + c
                        proj(i, pb, Wza, BWza, c * 512)
                        S.op("act", lambda e, pb=pb, c=c: e.activation(out=sg[c][:], in_=bank[pb], func=AF.Sigmoid, scale=rs[:, 0:1], bias=0.0),
                             reads=[PB[pb], Brs], writes=[Bsg[c]])
                        S.op("dve", lambda e, pb=pb, c=c: e.scalar_tensor_tensor(out=sa[k][:, c * 512:(c + 1) * 512], in0=bank[pb], scalar=rs[:, 0:1], in1=sg[c][:], op0=ALU.mult, op1=ALU.mult),
                             reads=[PB[pb], Brs, Bsg[c]], writes=[Bsa[k]])
                    S.dma("pool", SAs[i], sa[k][:], reads=[Bsa[k]], writes=[BSAs[i]])
                    proj(i, 5, Wzf, BWzf, 0)
                    S.op("act", lambda e: e.activation(out=sg[0][:], in_=bank[5], func=AF.Sigmoid, scale=rs[:, 0:1], bias=0.0),
                         reads=[PB[5], Brs], writes=[Bsg[0]])
                    S.op("dve", lambda e: e.scalar_tensor_tensor(out=sf[0][:], in0=bank[5], scalar=rs[:, 0:1], in1=sg[0][:], op0=ALU.mult, op1=ALU.mult),
                         reads=[PB[5], Brs, Bsg[0]], writes=[Bsf[0]])
                    S.op("dve", lambda e: e.tensor_tensor(out=fg[k][:], in0=sf[0][:], in1=fraw[:, i, :], op=ALU.mult),
                         reads=[Bsf[0], Bfraw], writes=[Bfg[k]])
                    for g in range(4):
                        S.op("pe", lambda e, g=g: e.transpose(bankb[6][:, g * 128:(g + 1) * 128], fg[k][:, g * 128:(g + 1) * 128], idb[:]),
                             reads=[Bfg[k], Bc], writes=[PB[6]], signal=(g == 3))
                    S.op("dve", lambda e: e.tensor_copy(out=fT[k][:].rearrange("p g t -> p (g t)"), in_=bankb[6][:, 0:512]), reads=[PB[6]], writes=[BfT[k]])
                    for c in range(2):
                        for g in range(4):
                            S.op("pe", lambda e, g=g, c=c: e.matmul(bank[c], lhsT=fT[k][:, g, :], rhs=Wfp[:, g, c * 512:(c + 1) * 512], start=(g == 0), stop=(g == 3)),
                                 reads=[BfT[k], BWfp], writes=[PB[c]], signal=(g == 3))
                    for c in range(4):
                        pb = 3 + c % 2
                        kk = c % 2
                        proj(i, pb, Wmg, BWmg, c * 512)
                        S.op("dve", lambda e, pb=pb, c=c, kk=kk: e.scalar_tensor_tensor(out=gt[0][:], in0=bank[pb], scalar=rs[:, 0:1], in1=bm[:, c * 512:(c + 1) * 512], op0=ALU.mult, op1=ALU.add),
                             reads=[PB[pb], Brs, Bbm], writes=[Bgt[0]])
                        if c < 2:
                            S.op("act", lambda e, c=c, kk=kk: e.activation(out=ga[0][:, c * 512:(c + 1) * 512], in_=gt[0][:], func=AF.Sigmoid, scale=1.0, bias=0.0),
                                 reads=[Bgt[0]], writes=[Bga[0]])
                        else:
                            S.op("act", lambda e, kk=kk: e.activation(out=gf[0][:], in_=gt[0][:], func=AF.Sigmoid, scale=1.0, bias=0.0),
                                 reads=[Bgt[0]], writes=[Bgf[0]])
                            S.op("dve", lambda e, c=c, kk=kk: e.tensor_tensor(out=mf[0][:, (c - 2) * 512:(c - 1) * 512], in0=bank[c - 2], in1=gf[0][:], op=ALU.mult),
                                 reads=[PB[c - 2], Bgf[0]], writes=[Bmf[0]])
                    S.dma("pool", GAs[i], ga[0][:], reads=[Bga[0]], writes=[BGAs[i]])
                    S.dma("pool", MFs[i], mf[0][:], reads=[Bmf[0]], writes=[BMFs[i]])

                pipeline([b_s0, b_s1, b_s2, b_s3], NQ)
                S.barrier()
        cab.close()
        if "C" in phases:
            S.pe_scale = 1.5
            with contextlib.ExitStack() as cd:
                KT = T(cd, "KT", [128, 2, NT, 128], BF16)
                Vs = T(cd, "Vs", [128, NT, 2, 129], BF16)
                Wap = T(cd, "Wap", [128, 8, 1024], BF16)
                Wout = T(cd, "Wout", [128, 8, 1024], BF16)
                BKT, BV, BWap, BWout = Buf(multi=True), Buf(multi=True), Buf(multi=True), Buf(multi=True)
                with contextlib.ExitStack() as cc:
                    Wkv = T(cc, "Wkv", [128, 8, 512], BF16)
                    BWkv = Buf(multi=True)
                    stg = [T(cc, "stgC%d" % i, [128, 8, 128], F32) for i in range(2)]
                    Bstg = [Buf(), Buf()]
                    load_w(stg, Bstg, Wkv, BWkv, w_in, 1024, 512, True)
                    load_w(stg, Bstg, Wap, BWap, w_ap, 0, 1024, False)
                    load_w(stg, Bstg, Wout, BWout, w_out, 0, 1024, False)
                    xp = XPipe(cc, "C", (6, 7), rb=3)

                    def ringc(name, shape, dt, n=2):
                        return [T(cc, "%s%d" % (name, i), shape, dt) for i in range(n)], [Buf() for _ in range(n)]
                    tabt, Btab = ringc("tabC", [128, 256], F32, 5)
                    kn, Bkn = ringc("kn", [128, 256], F32, 3)
                    ssq, Bssq = ringc("ssqc", [128, 32], F32, 2)
                    t1, Bt1 = ringc("t1c", [128, 256], F32, 1)
                    t2, Bt2 = ringc("t2c", [128, 256], F32, 1)
                    kr, Bkr = ringc("kr", [128, 2, 128], BF16, 1)
                    S.op("pool", lambda e: e.memset(Vs[:, :, :, 128:129], 1.0), writes=[BV])

                    def c_s0(t):
                        xp.load(t, t)
                        S.dma("sp", tabt[t % 5][:], tab[t], writes=[Btab[t % 5]])

                    def c_s1(t):
                        xp.prep(t)

                    def c_s2(t):
                        xb, Bxb = xp.XB(t)
                        rs, Brs = xp.RS(t)
                        kb_ = t % 2
                        for dh in range(8):
                            S.op("pe", lambda e, dh=dh: e.matmul(bank[kb_], lhsT=xb[:, dh, :], rhs=Wkv[:, dh, :], start=(dh == 0), stop=(dh == 7)),
                                 reads=[Bxb, BWkv], writes=[PB[kb_]], signal=(dh == 7))
                        S.op("act", lambda e: e.activation(out=Vs[:, t, :, 0:128], in_=bank[kb_][:, 256:512].rearrange("p (h d) -> p h d", h=2), func=AF.Identity, scale=rs[:, 0:1], bias=0.0),
                             reads=[PB[kb_], Brs], writes=[BV])
                        heads = [(bank[kb_][:, h * 128:(h + 1) * 128], PB[kb_]) for h in range(2)]
                        head_norm(heads, rs, Brs, 128, kn[t % 3], Bkn[t % 3], ssq[t % 2], Bssq[t % 2])

                    def c_s3(t):
                        k = t % 2
                        rope(kn[t % 3], Bkn[t % 3], 2, tabt[t % 5], Btab[t % 5], t1[0], t2[0], Bt1[0], Bt2[0], kr[0], Bkr[0])
                        tb = 2 + t % 2
                        for h in range(2):
                            S.op("pe", lambda e, h=h: e.transpose(bankb[tb][:, h * 128:(h + 1) * 128], kr[0][:, h, :], idb[:]),
                                 reads=[Bkr[0], Bc], writes=[PB[tb]], signal=(h == 1))
                        S.op("act", lambda e: e.copy(out=KT[:, :, t, :], in_=bankb[tb][:, 0:256].rearrange("p (h k) -> p h k", h=2)),
                             reads=[PB[tb]], writes=[BKT])

                    pipeline([c_s0, c_s1, c_s2, c_s3], NT)
                    S.barrier()

                if "D" in phases:
                    S.pe_scale = 1.0
                    with contextlib.ExitStack() as c4:
                        QTt = [T(c4, "QTd%d" % i, [128, 8, 128], BF16) for i in range(2)]
                        sat = T(c4, "sat", [128, 8, 128], BF16)
                        gat = T(c4, "gat", [128, 1024], BF16)
                        mft = T(c4, "mft", [128, 1024], F32)
                        xTt = T(c4, "xTt", [128, 8, 128], F32)
                        pT = [T(c4, "pT%d" % i, [128, 1024], BF16) for i in range(4)]
                        asb = T(c4, "asb", [128, 8, 128], BF16)
                        aT = T(c4, "aT", [128, 8, 128], BF16)
                        tmpf = T(c4, "tmpf", [128, 1024], F32)
                        mg = T(c4, "mg", [128, 1024], BF16)
                        mT = T(c4, "mT", [128, 8, 128], BF16)
                        osb = T(c4, "osb", [128, 8, 128], F32)
                        rinv = T(c4, "rinv", [128, 4], F32)
                        BQd = [Buf(), Buf()]
                        Bsat, Bgat, Bmft, BxT, Basb, BaT, Btmp, Bmg, BmT, Bosb, Brinv = [Buf() for _ in range(11)]
                        BpT = [Buf() for _ in range(4)]
                        obank = [bank[6][:, 0:129], bank[6][:, 256:385], bank[7][:, 0:129], bank[7][:, 256:385]]
                        POB = [PB[6], PB[6], PB[7], PB[7]]
                        sc = float(1.0 / np.sqrt(128.0))
                        S.dma("sp", QTt[0][:].rearrange("p h q -> p (h q)"), QTs[0], reads=[BQTs[0]], writes=[BQd[0]])
                        for dp in range(NQ):
                            sl = dp % 2
                            if dp + 1 < NQ:
                                S.dma("sp", QTt[1 - sl][:].rearrange("p h q -> p (h q)"), QTs[dp + 1], reads=[BQTs[dp + 1]], writes=[BQd[1 - sl]])
                            S.dma("sp", sat[:].rearrange("p h q -> p (h q)"), SAs[dp], reads=[BSAs[dp]], writes=[Bsat])
                            S.dma("sp", gat[:], GAs[dp], reads=[BGAs[dp]], writes=[Bgat])
                            S.dma("sp", mft[:], MFs[dp], reads=[BMFs[dp]], writes=[Bmft])
                            S.dma("sp", xTt[:], xs[4 * dp], writes=[BxT])
                            for kvh in range(2):
                                def qk(j, kvh=kvh, sl=sl):
                                    p2 = j % 2
                                    for i in range(2):
                                        kb = 2 * j + i
                                        S.op("pe", lambda e, i=i, kb=kb: e.matmul(pk[p2][:, i * 512:(i + 1) * 512], lhsT=KT[:, kvh, kb, :],
                                                                                rhs=QTt[sl][:, 4 * kvh:4 * kvh + 4, :].rearrange("p h q -> p (h q)"), start=True, stop=True),
                                             reads=[BKT, BQd[sl]], writes=[PB[2 * p2 + i]], signal=(i == 1))
                                    S.op("act", lambda e: e.activation(out=pT[j % 4][:], in_=pk[p2][:], func=AF.Exp, scale=sc, bias=negB[:, 0:1]),
                                         reads=[PB[2 * p2], PB[2 * p2 + 1], Bc], writes=[BpT[j % 4]])

                                def pv(j, kvh=kvh):
                                    for i in range(2):
                                        kb = 2 * j + i
                                        for hh in range(4):
                                            S.op("pe", lambda e, i=i, kb=kb, hh=hh: e.matmul(obank[hh], lhsT=pT[j % 4][:, i * 512 + hh * 128:i * 512 + hh * 128 + 128],
                                                                                           rhs=Vs[:, kb, kvh, :], start=(kb == 0 and hh % 2 == 0), stop=(kb == NT - 1), skip_group_check=True),
                                                 reads=[BpT[j % 4], BV], writes=[POB[hh]], signal=(i == 1 and hh == 3))
                                qk(0)
                                qk(1)
                                for j in range(NT // 2):
                                    if j + 2 < NT // 2:
                                        qk(j + 2)
                                    pv(j)
                                for hh in range(4):
                                    h = 4 * kvh + hh
                                    S.op("dve", lambda e, hh=hh: e.reciprocal(out=rinv[:, hh:hh + 1], in_=obank[hh][:, 128:129]),
                                         reads=[POB[hh]], writes=[Brinv])
                                    S.op("dve", lambda e, hh=hh, h=h: e.scalar_tensor_tensor(out=asb[:, h, :], in0=obank[hh][:, 0:128], scalar=rinv[:, hh:hh + 1], in1=sat[:, h, :], op0=ALU.mult, op1=ALU.mult),
                                         reads=[POB[hh], Brinv, Bsat], writes=[Basb])
                            S.cur_prio = 1
                            for h in range(8):
                                S.op("pe", lambda e, h=h: e.transpose(bankb[4][:, h * 128:(h + 1) * 128], asb[:, h, :], idb[:]),
                                     reads=[Basb, Bc], writes=[PB[4]], signal=True)
                            S.op("dve", lambda e: e.tensor_copy(out=aT[:].rearrange("p h q -> p (h q)"), in_=bankb[4][:, 0:1024]), reads=[PB[4]], writes=[BaT])
                            for c in range(2):
                                for eh in range(8):
                                    S.op("pe", lambda e, c=c, eh=eh: e.matmul(bank[5], lhsT=aT[:, eh, :], rhs=Wap[:, eh, c * 512:(c + 1) * 512], start=(eh == 0), stop=(eh == 7)),
                                         reads=[BaT, BWap], writes=[PB[5]], signal=True)
                                S.op("dve", lambda e, c=c: e.tensor_tensor(out=tmpf[:, c * 512:(c + 1) * 512], in0=bank[5], in1=gat[:, c * 512:(c + 1) * 512], op=ALU.mult),
                                     reads=[PB[5], Bgat], writes=[Btmp])
                            S.op("dve", lambda e: e.tensor_tensor(out=mg[:], in0=tmpf[:], in1=mft[:], op=ALU.add),
                                 reads=[Btmp, Bmft], writes=[Bmg])
                            for h in range(8):
                                S.op("pe", lambda e, h=h: e.transpose(bankb[4][:, h * 128:(h + 1) * 128], mg[:, h * 128:(h + 1) * 128], idb[:]),
                                     reads=[Bmg, Bc], writes=[PB[4]], signal=True)
                            S.op("dve", lambda e: e.tensor_copy(out=mT[:].rearrange("p h q -> p (h q)"), in_=bankb[4][:, 0:1024]), reads=[PB[4]], writes=[BmT])
                            for half in range(2):
                                for e4 in range(4):
                                    eo = 4 * half + e4
                                    for dh in range(8):
                                        S.op("pe", lambda e, eo=eo, e4=e4, dh=dh: e.matmul(bank[5][:, e4 * 128:(e4 + 1) * 128], lhsT=Wout[:, dh, eo * 128:(eo + 1) * 128], rhs=mT[:, dh, :], start=(dh == 0), stop=(dh == 7)),
                                             reads=[BmT, BWout], writes=[PB[5]], signal=True)
                                S.op("dve", lambda e, half=half: e.tensor_tensor(out=osb[:, 4 * half:4 * half + 4, :].rearrange("p h q -> p (h q)"), in0=bank[5],
                                                                                   in1=xTt[:, 4 * half:4 * half + 4, :].rearrange("p h q -> p (h q)"), op=ALU.add),
                                     reads=[PB[5], BxT], writes=[Bosb])
                            S.dma("pool", outT[dp], osb[:], reads=[Bosb], writes=[out_buf])
                            S.cur_prio = 0

        S.wait_all("sp", [out_buf])
        S.wait_all("pool", [out_buf])
        S.emit()
    return nc


def _alpha(j):
    return np.array([4 * (t // 4) + ((j + t % 4) % 4) for t in range(NT)], dtype=np.int64)


def _consts(j):
    al = _alpha(j)
    d = 4 * np.arange(32, dtype=np.int64) + j
    bt = np.arange(128, dtype=np.int64)
    num = (128 * bt[:, None, None] * d[None, None, :] + al[None, :, None] * d[None, None, :]) % 16384
    ang = 2.0 * np.pi * num.astype(np.float64) / 16384.0
    sc = 1.0 / np.sqrt(128.0)
    ec = np.concatenate([np.cos(ang) * sc, -np.sin(ang) * sc], axis=2).astype(np.float32)
    a2 = 2.0 * np.pi * ((bt[:, None] * bt[None, :]) % 128).astype(np.float64) / 128.0
    Cc, Sc = np.cos(a2) * sc, np.sin(a2) * sc
    wc = np.concatenate([Cc, -Sc, Sc, Cc], axis=1).astype(np.float32)
    a3 = 2.0 * np.pi * ((al[:, None] * bt[None, :]) % 128).astype(np.float64) / 128.0
    f2 = np.concatenate([np.cos(a3) * sc, np.sin(a3) * sc], axis=1).astype(np.float32)
    inv = (np.float32(10000.0) ** (-np.arange(0, 64, 2, dtype=np.float32) / np.float32(64))).astype(np.float32)
    s = al[:, None] + 128 * bt[None, :]
    row = (s // 64).astype(np.float32)
    col = (s % 64).astype(np.float32)
    ar = (row[:, :, None] * inv[None, None, :]).astype(np.float32)
    ac = (col[:, :, None] * inv[None, None, :]).astype(np.float32)
    cr, sr, cc, scn = np.cos(ar), np.sin(ar), np.cos(ac), np.sin(ac)
    tab = np.concatenate([cr, cr, cc, cc, -sr, sr, -scn, scn], axis=2).astype(np.float32)
    return al, ec, wc, f2, np.ascontiguousarray(tab)


def kernel(x, norm_g, w_in, q_norm_g, k_norm_g, w_attn_proj, w_fourier_proj, w_merge, b_merge, w_out, _phases=PHASES, _debug=False):
    x = np.asarray(x, dtype=np.float32)
    f = lambda a: np.ascontiguousarray(np.asarray(a, dtype=np.float32))
    gl = f(np.asarray(norm_g)[0].reshape(8, 128).T)
    gqk = f(np.concatenate([np.broadcast_to(np.asarray(q_norm_g)[0][None, :], (128, 128)),
                            np.broadcast_to(np.asarray(k_norm_g)[0][None, :], (128, 128))], axis=1))
    bm = f(np.broadcast_to(np.asarray(b_merge)[0][None, :], (128, 2048)))
    common = {"idn": np.eye(128, dtype=np.float32), "gl": gl, "gqk": gqk, "bm": bm,
              "w_in": f(np.asarray(w_in)[0]), "w_ap": f(np.asarray(w_attn_proj)[0]), "w_fp": f(np.asarray(w_fourier_proj)[0]),
              "w_mg": f(np.asarray(w_merge)[0]), "w_out": f(np.asarray(w_out)[0])}
    in_maps = []
    for core in range(8):
        b, j = core // 4, core % 4
        al, ec, wc, f2, tab = _consts(j)
        xv = x[b].reshape(128, 128, 8, 128).transpose(1, 3, 2, 0)
        xsv = np.ascontiguousarray(xv[al])
        m = dict(common)
        m.update({"xs": xsv, "tab": tab, "ec": ec, "wc": wc, "f2": f2})
        in_maps.append(m)
    nc = build_nc(_phases, _debug)
    res = run_bass_kernel_spmd(nc, in_maps, core_ids=list(range(8)))
    out = np.empty((B_, S_, D_), dtype=np.float32)
    for core in range(8):
        b, j = core // 4, core % 4
        o = res.results[core]["outT"].transpose(3, 0, 2, 1).reshape(128, NQ, 1024)
        out[b].reshape(128, NQ, 4, 1024)[:, :, j, :] = o
    if _debug:
        return out, res
    return out
```
